# Optimizing a Trainium2 kernel written in Bass

```python
import math
import jax, jax.numpy as jnp
from jax import lax
import numpy as np

D_MODEL = 2048
BATCH = 4
SEQ = 2048
DEPTH = 1
DEC_BATCH = 128
DEC_SEQ = 4
PAST_LEN = 16384
PAGE_SIZE = 128

N_MEM = 256
EPS = 1e-6
D_INNER = D_MODEL
SSD_HEAD_DIM = 64
SSD_HEADS = D_INNER // SSD_HEAD_DIM
D_STATE = 128
N_GROUPS = 4
HEADS_PER_GROUP = SSD_HEADS // N_GROUPS
SSD_CONV = 4
CONV_DIM = D_INNER + 2 * N_GROUPS * D_STATE
SSD_CHUNK = 128
GLA_HEADS = 4
GLA_KEY_DIM = D_MODEL // 2
GLA_VAL_DIM = D_MODEL
GLA_HEAD_K = GLA_KEY_DIM // GLA_HEADS
GLA_HEAD_V = GLA_VAL_DIM // GLA_HEADS
GATE_RANK = 16
GATE_TAU = 16.0
GLA_CHUNK = 64
CROSS_HEADS = 4
CROSS_HEAD_DIM = D_MODEL // CROSS_HEADS
FFN_DIM = 5632
FFN_CONV = 3
IN_SIZES = (D_INNER, CONV_DIM, SSD_HEADS, GLA_KEY_DIM, GLA_KEY_DIM, GLA_VAL_DIM, GLA_VAL_DIM, GATE_RANK, D_MODEL, D_MODEL)
IN_DIM = D_INNER + CONV_DIM + SSD_HEADS + 2 * GLA_KEY_DIM + 2 * GLA_VAL_DIM + GATE_RANK + 2 * D_MODEL

kernel_name = "hybrid_ssd_gla_memx_convffn_step"


def _rmsnorm(x, g):
    x32 = x.astype(jnp.float32)
    y = x32 * lax.rsqrt(jnp.mean(x32 * x32, axis=-1, keepdims=True) + EPS)
    return (y * g.astype(jnp.float32)).astype(x.dtype)


def _split(t, sizes):
    idx = np.cumsum(np.array(sizes))[:-1].tolist()
    return jnp.split(t, idx, axis=-1)


def _causal_dwconv(u, prev, w, b):
    L = u.shape[1]
    full = jnp.concatenate([prev.astype(u.dtype), u], axis=1)
    out = b
    for k in range(w.shape[0]):
        out = out + full[:, k:k + L] * w[k]
    return out, full[:, L:]


def _chunk_len(L, c):
    return c if L % c == 0 else L


def _to_chunks(t, c):
    B, L = t.shape[:2]
    return jnp.moveaxis(t.reshape((B, L // c, c) + t.shape[2:]), 1, 0)


def _from_chunks(t):
    nc, B, c = t.shape[:3]
    return jnp.moveaxis(t, 0, 1).reshape((B, nc * c) + t.shape[3:])


def _ssd_scan(x, dt, A, Bm, Cm, S0):
    Bsz, L = x.shape[:2]
    c = _chunk_len(L, SSD_CHUNK)
    xg = x.reshape(Bsz, L, N_GROUPS, HEADS_PER_GROUP, SSD_HEAD_DIM)
    dtg = dt.reshape(Bsz, L, N_GROUPS, HEADS_PER_GROUP)
    Ag = A.reshape(N_GROUPS, HEADS_PER_GROUP)
    S = S0.reshape(Bsz, N_GROUPS, HEADS_PER_GROUP, SSD_HEAD_DIM, D_STATE)
    mask = jnp.tril(jnp.ones((c, c), dtype=bool))

    def step(S, inp):
        xc, dtc, Bc, Cc = inp
        acum = jnp.cumsum(dtc * Ag, axis=1)
        acum_t = jnp.moveaxis(acum, 1, -1)
        diff = acum_t[..., :, None] - acum_t[..., None, :]
        decay = jnp.exp(jnp.where(mask, diff, -jnp.inf))
        cb = jnp.einsum('bign,bjgn->bgij', Cc, Bc)
        m = cb[:, :, None] * decay * jnp.moveaxis(dtc, 1, -1)[..., None, :]
        y = jnp.einsum('bgrij,bjgrp->bigrp', m, xc)
        y = y + jnp.einsum('bign,bgrpn->bigrp', Cc, S) * jnp.exp(acum)[..., None]
        last = acum[:, -1]
        w = jnp.exp(last[:, None] - acum) * dtc
        S = jnp.exp(last)[..., None, None] * S + jnp.einsum('bjgr,bjgn,bjgrp->bgrpn', w, Bc, xc)
        return S, y

    S, ys = lax.scan(step, S, (_to_chunks(xg, c), _to_chunks(dtg, c), _to_chunks(Bm, c), _to_chunks(Cm, c)))
    y = _from_chunks(ys).reshape(Bsz, L, SSD_HEADS, SSD_HEAD_DIM)
    return y, S.reshape(Bsz, SSD_HEADS, SSD_HEAD_DIM, D_STATE)


def _gla_scan(q, k, v, g, S0):
    L = q.shape[1]
    c = _chunk_len(L, GLA_CHUNK)
    mask = jnp.tril(jnp.ones((c, c), dtype=bool))

    def step(S, inp):
        qc, kc, vc, gc = inp
        b = jnp.cumsum(gc, axis=1)
        qe = qc * jnp.exp(b)
        ke = kc * jnp.exp(-b)
        att = jnp.where(mask, jnp.einsum('bihk,bjhk->bhij', qe, ke), 0.0)
        o = jnp.einsum('bhij,bjhv->bihv', att, vc) + jnp.einsum('bihk,bhkv->bihv', qe, S)
        last = b[:, -1]
        S = jnp.exp(last)[..., None] * S + jnp.einsum('bjhk,bjhv->bhkv', kc * jnp.exp(last[:, None] - b), vc)
        return S, o

    S, os_ = lax.scan(step, S0, (_to_chunks(q, c), _to_chunks(k, c), _to_chunks(v, c), _to_chunks(g, c)))
    return _from_chunks(os_), S


def _mem_kv(mem, norm_mem, w_ck, w_cv):
    Bsz = mem.shape[0]
    mn = _rmsnorm(mem, norm_mem)
    k = (mn @ w_ck).reshape(Bsz, N_MEM, CROSS_HEADS, CROSS_HEAD_DIM)
    v = (mn @ w_cv).reshape(Bsz, N_MEM, CROSS_HEADS, CROSS_HEAD_DIM)
    return k, v


def _cross_attend(hn, mem_k, mem_v, w_cq, w_co):
    Bsz, L, _ = hn.shape
    q = (hn @ w_cq).reshape(Bsz, L, CROSS_HEADS, CROSS_HEAD_DIM).astype(jnp.float32)
    s = jnp.einsum('blhd,bmhd->bhlm', q, mem_k.astype(jnp.float32)) * (CROSS_HEAD_DIM ** -0.5)
    pr = jax.nn.softmax(s, axis=-1)
    o = jnp.einsum('bhlm,bmhd->blhd', pr, mem_v.astype(jnp.float32)).astype(hn.dtype)
    return o.reshape(Bsz, L, D_MODEL) @ w_co


def _layer(x, mem_k, mem_v, ssd_conv, ssd_state, gla_state, ffn_conv, p):
    f32 = jnp.float32
    Bsz, L, _ = x.shape
    dtype = x.dtype
    xn = _rmsnorm(x, p['norm_mix'])
    z, xbc, dt_raw, q, k, v, r, g_lr, gate_a, gate_b = _split(xn @ p['w_in'], IN_SIZES)

    xbc, ssd_conv_new = _causal_dwconv(xbc, ssd_conv, p['ssd_conv_w'], p['ssd_conv_b'])
    xbc = jax.nn.silu(xbc)
    xs, Bm, Cm = _split(xbc, (D_INNER, N_GROUPS * D_STATE, N_GROUPS * D_STATE))
    dt = jax.nn.softplus(dt_raw.astype(f32) + p['ssd_dt_bias'].astype(f32))
    A = -jnp.exp(p['ssd_A_log'].astype(f32))
    xs_h = xs.astype(f32).reshape(Bsz, L, SSD_HEADS, SSD_HEAD_DIM)
    y_ssd, ssd_state_new = _ssd_scan(xs_h, dt, A,
                                     Bm.astype(f32).reshape(Bsz, L, N_GROUPS, D_STATE),
                                     Cm.astype(f32).reshape(Bsz, L, N_GROUPS, D_STATE),
                                     ssd_state.astype(f32))
    y_ssd = y_ssd + p['ssd_D'].astype(f32)[:, None] * xs_h
    u = y_ssd.reshape(Bsz, L, N_GROUPS, D_INNER // N_GROUPS) * jax.nn.silu(z.astype(f32)).reshape(Bsz, L, N_GROUPS, D_INNER // N_GROUPS)
    u = u * lax.rsqrt(jnp.mean(u * u, axis=-1, keepdims=True) + EPS)
    u = (u.reshape(Bsz, L, D_INNER) * p['ssd_norm'].astype(f32)).astype(dtype)
    branch_a = u @ p['w_ssd_out']

    qh = q.astype(f32).reshape(Bsz, L, GLA_HEADS, GLA_HEAD_K) * (GLA_HEAD_K ** -0.5)
    kh = k.astype(f32).reshape(Bsz, L, GLA_HEADS, GLA_HEAD_K)
    vh = v.astype(f32).reshape(Bsz, L, GLA_HEADS, GLA_HEAD_V)
    glog = jax.nn.log_sigmoid((g_lr @ p['w_gla_gate']).astype(f32) + p['b_gla_gate'].astype(f32)) / GATE_TAU
    glog = glog.reshape(Bsz, L, GLA_HEADS, GLA_HEAD_K)
    o, gla_state_new = _gla_scan(qh, kh, vh, glog, gla_state.astype(f32))
    o = o * lax.rsqrt(jnp.mean(o * o, axis=-1, keepdims=True) + EPS) * p['gla_norm'].astype(f32)
    o = (o.reshape(Bsz, L, GLA_VAL_DIM) * jax.nn.silu(r.astype(f32))).astype(dtype)
    branch_b = o @ p['w_gla_out']

    merged = jax.nn.sigmoid(gate_a) * branch_a + jax.nn.sigmoid(gate_b) * branch_b
    h = x + merged @ p['w_mix_out']

    h = h + _cross_attend(_rmsnorm(h, p['norm_cross']), mem_k, mem_v, p['w_cq'], p['w_co'])

    up = _rmsnorm(h, p['norm_ffn']) @ p['w_up']
    up, ffn_conv_new = _causal_dwconv(up, ffn_conv, p['ffn_conv_w'], p['ffn_conv_b'])
    a, gt = jnp.split(up, 2, axis=-1)
    h = h + (jax.nn.silu(gt) * a) @ p['w_down']
    return h, ssd_conv_new, ssd_state_new, gla_state_new, ffn_conv_new


def setup_inputs(seed: int = 0) -> dict:
    key = jax.random.key(seed)
    ks = iter(list(jax.random.split(key, 48)))
    f32 = jnp.float32

    def nrm(shape, scale=1.0):
        return scale * jax.random.normal(next(ks), shape, f32)

    def dense(fi, fo):
        return jax.random.normal(next(ks), (fi, fo), f32) * (fi ** -0.5)

    def gain(n):
        return 1.0 + 0.02 * jax.random.normal(next(ks), (n,), f32)

    d = {}
    d['x_prompt'] = nrm((BATCH, SEQ, D_MODEL))
    d['x_sample'] = nrm((DEC_BATCH, DEC_SEQ, D_MODEL))
    d['cache_mem_k'] = nrm((DEC_BATCH, N_MEM, CROSS_HEADS, CROSS_HEAD_DIM))
    d['cache_mem_v'] = nrm((DEC_BATCH, N_MEM, CROSS_HEADS, CROSS_HEAD_DIM))
    d['state_ssd_conv'] = nrm((DEC_BATCH, SSD_CONV - 1, CONV_DIM))
    d['state_ssd'] = nrm((DEC_BATCH, SSD_HEADS, SSD_HEAD_DIM, D_STATE), 0.5)
    d['state_gla'] = nrm((DEC_BATCH, GLA_HEADS, GLA_HEAD_K, GLA_HEAD_V), 0.5)
    d['state_ffn_conv'] = nrm((DEC_BATCH, FFN_CONV - 1, 2 * FFN_DIM))
    d['mem_prompt'] = nrm((BATCH, N_MEM, D_MODEL))
    d['norm_mix'] = gain(D_MODEL)
    d['w_in'] = dense(D_MODEL, IN_DIM)
    d['ssd_conv_w'] = nrm((SSD_CONV, CONV_DIM), SSD_CONV ** -0.5)
    d['ssd_conv_b'] = nrm((CONV_DIM,), 0.02)
    dt0 = jnp.exp(jax.random.uniform(next(ks), (SSD_HEADS,), f32, math.log(1e-3), math.log(1e-1)))
    d['ssd_dt_bias'] = dt0 + jnp.log(-jnp.expm1(-dt0))
    d['ssd_A_log'] = jnp.log(jax.random.uniform(next(ks), (SSD_HEADS,), f32, 1.0, 16.0))
    d['ssd_D'] = 1.0 + nrm((SSD_HEADS,), 0.1)
    d['ssd_norm'] = gain(D_INNER)
    d['w_ssd_out'] = dense(D_INNER, D_MODEL)
    d['w_gla_gate'] = dense(GATE_RANK, GLA_KEY_DIM)
    d['b_gla_gate'] = nrm((GLA_KEY_DIM,), 0.1)
    d['gla_norm'] = gain(GLA_HEAD_V)
    d['w_gla_out'] = dense(GLA_VAL_DIM, D_MODEL)
    d['w_mix_out'] = dense(D_MODEL, D_MODEL)
    d['norm_cross'] = gain(D_MODEL)
    d['norm_mem'] = gain(D_MODEL)
    d['w_cq'] = dense(D_MODEL, D_MODEL)
    d['w_ck'] = dense(D_MODEL, D_MODEL)
    d['w_cv'] = dense(D_MODEL, D_MODEL)
    d['w_co'] = dense(D_MODEL, D_MODEL)
    d['norm_ffn'] = gain(D_MODEL)
    d['w_up'] = dense(D_MODEL, 2 * FFN_DIM)
    d['ffn_conv_w'] = nrm((FFN_CONV, 2 * FFN_DIM), FFN_CONV ** -0.5)
    d['ffn_conv_b'] = nrm((2 * FFN_DIM,), 0.02)
    d['w_down'] = dense(FFN_DIM, D_MODEL)
    d['norm_final'] = gain(D_MODEL)
    return d


def reference(x_prompt, x_sample, cache_mem_k, cache_mem_v, state_ssd_conv, state_ssd, state_gla, state_ffn_conv,
              mem_prompt, norm_mix, w_in, ssd_conv_w, ssd_conv_b, ssd_dt_bias, ssd_A_log, ssd_D, ssd_norm, w_ssd_out,
              w_gla_gate, b_gla_gate, gla_norm, w_gla_out, w_mix_out, norm_cross, norm_mem, w_cq, w_ck, w_cv, w_co,
              norm_ffn, w_up, ffn_conv_w, ffn_conv_b, w_down, norm_final):
    p = dict(norm_mix=norm_mix, w_in=w_in, ssd_conv_w=ssd_conv_w, ssd_conv_b=ssd_conv_b, ssd_dt_bias=ssd_dt_bias,
             ssd_A_log=ssd_A_log, ssd_D=ssd_D, ssd_norm=ssd_norm, w_ssd_out=w_ssd_out, w_gla_gate=w_gla_gate,
             b_gla_gate=b_gla_gate, gla_norm=gla_norm, w_gla_out=w_gla_out, w_mix_out=w_mix_out,
             norm_cross=norm_cross, w_cq=w_cq, w_co=w_co, norm_ffn=norm_ffn, w_up=w_up,
             ffn_conv_w=ffn_conv_w, ffn_conv_b=ffn_conv_b, w_down=w_down)
    nb = x_prompt.shape[0]
    dtype = x_prompt.dtype
    p_mem_k, p_mem_v = _mem_kv(mem_prompt, norm_mem, w_ck, w_cv)
    hp, p_ssd_conv, p_ssd, p_gla, p_ffn_conv = _layer(
        x_prompt, p_mem_k, p_mem_v,
        jnp.zeros((nb, SSD_CONV - 1, CONV_DIM), dtype),
        jnp.zeros((nb, SSD_HEADS, SSD_HEAD_DIM, D_STATE), jnp.float32),
        jnp.zeros((nb, GLA_HEADS, GLA_HEAD_K, GLA_HEAD_V), jnp.float32),
        jnp.zeros((nb, FFN_CONV - 1, 2 * FFN_DIM), dtype), p)
    y_prompt = _rmsnorm(hp, norm_final)
    hs, s_ssd_conv, s_ssd, s_gla, s_ffn_conv = _layer(
        x_sample, cache_mem_k, cache_mem_v, state_ssd_conv, state_ssd, state_gla, state_ffn_conv, p)
    y_sample = _rmsnorm(hs, norm_final)
    return (y_prompt, y_sample, p_ssd_conv, p_ssd, p_gla, p_ffn_conv, p_mem_k, p_mem_v,
            s_ssd_conv, s_ssd, s_gla, s_ffn_conv)
```

```python
import os
import numpy as np
from contextlib import ExitStack
import concourse.bass as bass
import concourse.mybir as mybir
from concourse.bass_utils import run_bass_kernel_spmd

F32 = mybir.dt.float32
BF16 = mybir.dt.bfloat16
AF = mybir.ActivationFunctionType
ALU = mybir.AluOpType

D = 2048
CONV_DIM = 3072
FFN = 5632
IN_DIM = 15408
EPS = 1e-6
NMEM = 256
C_Z, C_XBC, C_DT, C_Q, C_K, C_V, C_R, C_G, C_GA, C_GB = 0, 2048, 5120, 5152, 6176, 7200, 9248, 11296, 11312, 13360
ENGS = ['pe', 'dve', 'act', 'pool', 'sp']
SKIPGLA = bool(os.environ.get('SKIPGLA'))


class Prog:
    def __init__(self):
        self.ops = []
        self.last_w = {}
        self.readers = {}
        self.alias = {}
        self._exp = {}

    def expand(self, keys):
        out = []
        for k in keys:
            if k not in self._exp:
                kk = k if k in self.alias else (k[0] if isinstance(k, tuple) and k[0] in self.alias else None)
                if kk is None:
                    self._exp[k] = [k]
                else:
                    lo, hi = self.alias[kk]
                    if k not in self.alias:
                        self.alias[k] = (lo, hi)
                        self._exp = {k0: v for k0, v in self._exp.items() if False}
                    self._exp[k] = [k] + [k2 for k2, (l2, h2) in self.alias.items() if k2 != k and l2 < hi and lo < h2]
            out.extend(self._exp[k])
        return out

    def op(self, eng, fn, r=(), w=(), dma=None):
        oid = len(self.ops)
        deps = set()
        r = self.expand(r)
        w = self.expand(w)
        for b in r:
            if b in self.last_w:
                deps.add(self.last_w[b])
            if isinstance(b, tuple) and b[0] == 'ps':
                for x in self.readers.get(b, ()):
                    if self.ops[x]['eng'] != eng:
                        deps.add(x)
        for b in w:
            if b in self.last_w:
                deps.add(self.last_w[b])
            for x in self.readers.get(b, ()):
                deps.add(x)
        for b in r:
            self.readers.setdefault(b, []).append(oid)
        for b in w:
            self.last_w[b] = oid
            self.readers[b] = []
        self.ops.append(dict(eng=eng, fn=fn, deps=deps, dma=dma, marked=False))
        return oid

    def finalize(self, sems, dma_sems):
        ops = self.ops
        for o in ops:
            for d in o['deps']:
                od = ops[d]
                if od['dma'] is None:
                    if od['eng'] == 'pe' and o['eng'] == 'pe' and o['dma'] is None:
                        continue
                    od['marked'] = True
        cnt = {e: 0 for e in ENGS}
        dcnt = {}
        for o in ops:
            if o['dma'] is not None:
                dcnt[o['dma']] = dcnt.get(o['dma'], 0) + 16
                o['sig'] = (dma_sems[o['dma']], dcnt[o['dma']])
            elif o['marked']:
                cnt[o['eng']] += 1
                o['sig'] = (sems[o['eng']], cnt[o['eng']])
        self.per = {e: [o for o in ops if o['eng'] == e] for e in ENGS}
        self.stats = dict(cnt=cnt, dcnt={str(k): v for k, v in dcnt.items()}, nops={e: len(self.per[e]) for e in ENGS})
        self.final_waits = [(dma_sems[k], v) for k, v in dcnt.items()]

    def run(self, engname, e):
        ops = self.ops
        waited = {}
        for o in self.per[engname]:
            for d in sorted(o['deps']):
                od = ops[d]
                if 'sig' not in od:
                    continue
                sem, val = od['sig']
                key = id(sem)
                if waited.get(key, 0) >= val:
                    continue
                waited[key] = val
                e.wait_ge(sem, val)
            ins = o['fn'](e)
            if 'sig' in o:
                ins.then_inc(o['sig'][0], 16 if o['dma'] is not None else 1)
        if engname == 'sp':
            for sem, val in self.final_waits:
                e.wait_ge(sem, val)


class _Stop(Exception):
    pass


def build(NPT, NS, TPP, dbg_names=(), stop=None):
    nc = bass.Bass("TRN2", target_bir_lowering=False)
    din_ = {}
    dout_ = {}

    def din(name, shape):
        din_[name] = nc.dram_tensor(name, list(shape), F32, kind="ExternalInput").ap()
        return din_[name]

    def dout(name, shape):
        dout_[name] = nc.dram_tensor(name, list(shape), F32, kind="ExternalOutput").ap()
        return dout_[name]

    x_p = din("x_p", [NPT * 128, D]); x_s = din("x_s", [128, D])
    mem_p = din("mem_p", [NMEM, D])
    ck = din("cache_k", [NS, NMEM, D]); cv = din("cache_v", [NS, NMEM, D])
    st_sc = din("st_ssd_conv", [NS * 3, CONV_DIM]); st_ssd = din("st_ssd", [NS, D, 128])
    st_gla = din("st_gla", [NS, 4, 256, 512]); st_fc = din("st_ffn_conv", [NS * 2, 2 * FFN])
    consts = din("consts", [128, 1024]); consts2 = din("consts2", [128, 2048]); params = din("params", [128, 672])
    w_in = din("w_in", [D, IN_DIM]); w_ssd_out = din("w_ssd_out", [D, D]); w_gla_out = din("w_gla_out", [D, D])
    w_mix = din("w_mix_out", [D, D]); w_cq = din("w_cq", [D, D]); w_ck = din("w_ck", [D, D])
    w_cv = din("w_cv", [D, D]); w_co = din("w_co", [D, D]); w_up = din("w_up", [D, 2 * FFN])
    w_down = din("w_down", [FFN, D]); w_gate = din("w_gla_gate", [16, 1024])
    vec = {}
    for n, sz_ in [("norm_final", D), ("b_gla_gate", 1024)]:
        vec[n] = din(n, [sz_])

    y_p = dout("y_p", [NPT * 128, D]); y_s = dout("y_s", [128, D])
    o_psc = dout("p_ssd_conv", [3, CONV_DIM]); o_pssd = dout("p_ssd", [D, 128]); o_pgla = dout("p_gla", [4, 256, 512])
    o_pfc = dout("p_ffn_conv", [2, 2 * FFN]); o_pmk = dout("p_mem_k", [NMEM, D]); o_pmv = dout("p_mem_v", [NMEM, D])
    o_ssc = dout("s_ssd_conv", [NS, 3, CONV_DIM]); o_sssd = dout("s_ssd", [NS, D, 128])
    o_sgla = dout("s_gla", [NS, 4, 256, 512]); o_sfc = dout("s_ffn_conv", [NS, 2, 2 * FFN])

    P = Prog()
    es = ExitStack()
    with es:
        es.enter_context(nc.allow_non_contiguous_dma(reason="small parameter loads"))

        def sb(name, shape, dt=F32):
            return es.enter_context(nc.sbuf_tensor(name, list(shape), dt))
        psum = es.enter_context(nc.psum_tensor("psum", [128, 8, 512], F32))
        NT = TPP
        T = NT * 128
        assert T >= 256

        dsem_keys = []

        def dma(eng, out, in_, r=(), w=(), key=None):
            key = ('d', (list(w) + list(r))[0])
            if key not in dsem_keys:
                dsem_keys.append(key)
            return P.op(eng, lambda e: e.dma_start(out=out, in_=in_), r=r, w=w, dma=key)

        def ACT(out, in_, func, r, w, **kw):
            P.op('act', lambda e: e.activation(out=out, in_=in_, func=func, **kw), r=r, w=w)

        def TT(out, in0, in1, op, r, w):
            P.op('dve', lambda e: e.tensor_tensor(out=out, in0=in0, in1=in1, op=op), r=r, w=w)

        def TS(out, in0, s1, s2, op0, op1, r, w):
            if s2 is None:
                P.op('dve', lambda e: e.tensor_scalar(out=out, in0=in0, scalar1=s1, scalar2=None, op0=op0), r=r, w=w)
            else:
                P.op('dve', lambda e: e.tensor_scalar(out=out, in0=in0, scalar1=s1, scalar2=s2, op0=op0, op1=op1), r=r, w=w)

        def STT(out, in0, scalar, in1, op0, op1, r, w):
            P.op('dve', lambda e: e.scalar_tensor_tensor(out=out, in0=in0, scalar=scalar, in1=in1, op0=op0, op1=op1), r=r, w=w)

        def CP(out, in_, r, w):
            P.op('dve', lambda e: e.tensor_copy(out=out, in_=in_), r=r, w=w)

        def MS(ap, val, w):
            P.op('dve', lambda e: e.memset(ap, val), w=w)

        def mm(out, lhsT, rhs, start, stop, r, w):
            P.op('pe', lambda e: e.matmul(out, lhsT=lhsT, rhs=rhs, start=start, stop=stop, skip_group_check=True), r=r, w=w)

        def tr(out, in_, ident, r, w):
            P.op('pe', lambda e: e.transpose(out, in_, ident), r=r, w=w)

        psn = [0]

        def PS():
            b = psn[0]
            psn[0] = (psn[0] + 1) % 4
            return b

        def pk(b):
            return [('ps', b)]

        CST = sb("cst", [128, 1024])
        identf = CST[:, 0:128]; U_P = CST[:, 128:256]; ONE = CST[:, 384:512]
        L_P = CST[:, 256:384]; MN_P = CST[:, 512:640]; M01_P = CST[:, 640:768]
        CS2 = sb("cs2", [128, 2048])
        U_S = CS2[:, 0:128]; L_S = CS2[:, 128:256]; BLK_S = CS2[:, 256:384]; MN_S = CS2[:, 384:512]
        M01_S = CS2[:, 512:640]; BLKC = CS2[:, 640:656]
        BLKS = CS2[:, 1024:2048].rearrange("p (s t) -> p s t", t=64)
        identb = sb("identb", [128, 128], BF16)
        m01b_p = sb("m01bp", [128, 128], BF16); m01b_s = sb("m01bs", [128, 128], BF16)
        PAR = sb("par", [128, 672])
        GF = PAR[:, 0:96].rearrange("p (g c) -> p g c", c=16)
        CW = PAR[:, 96:192].rearrange("p (c k) -> p c k", k=4); CB = PAR[:, 192:216]
        FW = PAR[:, 216:480].rearrange("p (c k) -> p c k", k=3); FB = PAR[:, 480:568]
        DTB = PAR[:, 568:600]; AB = sb("ab", [128, 32]); DB = PAR[:, 632:664]
        WG = sb("wg", [32, 1024], BF16)
        dma('sp', CST[:], consts[:, :], w=['cst'])
        dma('sp', CS2[:], consts2[:, :], w=['cs2'])
        dma('sp', PAR[:], params[:, :], w=['par'])
        dma('pool', WG[0:16, :], w_gate[:, :], w=['wg'])
        dma('pool', WG[16:17, :], vec["b_gla_gate"].rearrange("(o n) -> o n", o=1), w=['wg'])
        ACT(AB[:], PAR[:, 600:632], AF.Exp, r=['par'], w=['ab'])
        TS(AB[:], AB[:], -1.0, None, ALU.mult, None, r=['ab'], w=['ab'])
        CP(identb[:], identf, r=['cst'], w=['identb'])
        CP(m01b_p[:], M01_P, r=['cst'], w=['m01b'])
        CP(m01b_s[:], M01_S, r=['cs2'], w=['m01b'])
        KC = {'P': dict(U=U_P, L=L_P, BLK=ONE, MN=MN_P, M01=m01b_p),
              'S': dict(U=U_S, L=L_S, BLK=BLK_S, MN=MN_S, M01=m01b_s)}
        CK = ['cst', 'cs2']

        NW = 2
        WS = [sb(f"ws{i}", [128, 16, 512], BF16) for i in range(NW)]
        A_fm = sb("A_fm", [128, 16, T], BF16)
        U_fm = sb("u_fm", [128, 16, T], BF16)
        MRG = sb("mrg", [128, 16, T], BF16)
        ST = sb("st", [128, D]); GS = sb("gs", [128, 8, 512])
        KFM = sb("kfm", [128, 16, NMEM], BF16); VTM = sb("vtm", [128, 2, D], BF16)
        HALO = sb("halo", [128, 24, 3]); FHALO = sb("fhalo", [128, 88, 2])
        SS = sb("ss", [128, 8]); TMP32 = sb("tmp32", [128, 8, 32])
        RH = NT * D * 4
        R1 = 8192 * 2 + 4096 + 2048
        R2 = max(NT * (4096 + 6144 + 128), NT * 16 * 1024 + NT * 256, NT * 11 * 1024 + NT * 2048 + 32 + 4096) + 64
        R3 = 21504
        ARENA = sb("arena", [128, (RH + R1 + R2 + R3) // 4])
        class Carver:
            def __init__(self, base, size):
                self.base = base; self.size = size; self.off = 0
            def reset(self):
                self.off = 0
            def get(self, shape, dt=F32, key=None):
                esz = 4 if dt == F32 else 2
                n = 1
                for x in shape[1:]:
                    n *= x
                nb = (n * esz + 31) // 32 * 32
                assert self.off + nb <= self.size, (self.off, nb, self.size, shape)
                o = (self.base + self.off) // 4
                if key is not None:
                    P.alias[key] = (self.base + self.off, self.base + self.off + nb)
                ap = ARENA[0:shape[0], o:o + nb // 4]
                self.off += nb
                if dt != F32:
                    ap = ap.bitcast(dt)
                ap = ap[:, 0:n]
                if len(shape) == 3:
                    ap = ap.rearrange("p (a b) -> p a b", b=shape[2])
                elif len(shape) == 4:
                    ap = ap.rearrange("p (a b c) -> p a b c", b=shape[2], c=shape[3])
                return ap
        cH = Carver(0, RH); c1 = Carver(RH, R1); c2 = Carver(RH + R1, R2); c3 = Carver(RH + R1 + R2, R3)
        H = cH.get([128, NT, D])
        for t_ in range(NT):
            P.alias[('H', t_)] = (t_ * D * 4, (t_ + 1) * D * 4)
        cH.reset()
        SQ = cH.get([128, 16, 128], key='sq'); SQB = cH.get([128, D], BF16, key='sqb'); XWM = cH.get([128, D], BF16, key='xwm')
        T1 = c1.get([128, D], key='t1'); T2 = c1.get([128, D], key='t2'); XNT = c1.get([128, D], BF16, key='xnt')
        CMS = c1.get([128, 16, 64], BF16, key='cms')
        SZ = c2.get([128, NT, D], BF16, key='sz'); XBC = c2.get([128, 24, T], BF16, key='xbc'); DT = c2.get([128, NT, 32], key='dt')
        c2.reset()
        SG = c2.get([128, 16, T], BF16, key='sg'); QF = c2.get([128, 8, T], BF16, key='qf'); KF = c2.get([128, 8, T], BF16, key='kf')
        VT = c2.get([128, NT, D], BF16, key='vt'); SR = c2.get([128, NT, D], BF16, key='sr')
        GLR = c2.get([32, T], BF16, key='glr')
        c2.reset()
        ACTF = c2.get([128, 44, T], BF16, key='actf'); RAW2 = c2.get([128, 2, 2 + T], key='raw2'); ACC2 = c2.get([128, 2, T], key='acc2'); FST = c2.get([32, 1024], key='fst')
        OST = c3.get([128, 512], key='ost'); RAW = c3.get([128, 3 + T], key='raw'); ACC = c3.get([128, T], key='acc')
        RAWS = c3.get([128, 16, 8], key='raws'); RAWS2 = c3.get([128, 16, 8], key='raws2')
        c3.reset()
        XS = c3.get([128, D], BF16, key='xs'); BT = c3.get([128, 512], BF16, key='bt'); UA = c3.get([128, 4, 128], key='ua'); DEC = c3.get([128, 2, 128], key='dec')
        CBT = c3.get([128, 4, 128], key='cbt'); MT = c3.get([128, 2, 4, 128], BF16, key='mt'); STB = c3.get([128, D], BF16, key='stb'); XW = c3.get([128, D], BF16, key='xw')
        ELS = c3.get([128, 16, 16], key='els')
        c3.reset()
        G = c3.get([128, 1024], key='g'); GT1 = T1[:, 0:1024]; GT2 = T2[:, 0:1024]; QE = c3.get([128, 8, 128], BF16, key='qe')
        KE = c3.get([128, 8, 128], BF16, key='ke'); EBT = c3.get([128, 2, 128], key='ebt'); ATT = c3.get([128, 4, 128], BF16, key='att'); KW = c3.get([128, 1024], BF16, key='kw')
        GSB = c3.get([128, 8, 512], BF16, key='gsb'); ELG = c3.get([128, 8, 16], key='elg')
        c3.reset()
        PSM = T1[:, 0:4 * NMEM].rearrange("p (h m) -> p h m", m=NMEM); PB = c3.get([128, 4, NMEM], BF16, key='pb'); PT = c3.get([128, 8, 128], BF16, key='ptt')
        KSB = c3.get([128, 2, D], BF16, key='ksb'); KTS = c3.get([128, 16, NMEM], BF16, key='kts')
        TMPG = c3.get([128, T], key='tmpg')

        wslot = [0]

        def load_w(wd, r0, nkc, c0, ncols):
            s = wslot[0] % NW
            wslot[0] += 1
            src = wd[r0:r0 + nkc * 128, c0:c0 + ncols].rearrange("(kc p) n -> p kc n", p=128)
            dma('pool', WS[s][:, 0:nkc, 0:ncols], src, w=[('W', s)], key=('w', s))
            return s

        def proj_tm(lhs, lkey, tlist, wd, c0, ncols, epi, nkc=16, r0=0, slot=None):
            s = load_w(wd, r0, nkc, c0, ncols) if slot is None else slot
            for t in tlist:
                b = PS()
                for kc in range(nkc):
                    mm(psum[:, b, 0:ncols], lhs[:, kc, t * 128:(t + 1) * 128], WS[s][:, kc, 0:ncols], kc == 0, kc == nkc - 1,
                       r=[lkey, ('W', s)], w=pk(b))
                epi(t, b)
            return s

        def proj_fm(rhs, rkey, Tn, wd, c0, nch, epi, nkc=16, slot=None):
            s = load_w(wd, 0, nkc, c0, nch * 128) if slot is None else slot
            for ci in range(nch):
                b = PS()
                for kc in range(nkc):
                    mm(psum[:, b, 0:Tn], WS[s][:, kc, ci * 128:(ci + 1) * 128], rhs[:, kc, 0:Tn], kc == 0, kc == nkc - 1,
                       r=[rkey, ('W', s)], w=pk(b))
                epi(ci, b)
            return s

        def rstd_from_ss(col, n=1):
            TS(SS[:, col:col + n], SS[:, col:col + n], EPS, None, ALU.add, None, r=['ss'], w=['ss'])
            ACT(SS[:, col:col + n], SS[:, col:col + n], AF.Ln, r=['ss'], w=['ss'])
            ACT(SS[:, col:col + n], SS[:, col:col + n], AF.Exp, r=['ss'], w=['ss'], scale=-0.5)

        def norm_rstd(src, skey, n, col):
            ACT(T2[:, 0:n], src, AF.Square, r=skey, w=['t2', 'ss'], scale=float(n) ** -0.5, accum_out=SS[:, col:col + 1])
            rstd_from_ss(col)

        def to_fm(src_bf, skey, dst, dkey, tcol, gain, nk=16):
            for g4 in range(0, nk, 4):
                b = PS()
                pv = psum[:, b, :].bitcast(BF16)
                for j in range(4):
                    tr(pv[:, j * 128:(j + 1) * 128], src_bf[:, (g4 + j) * 128:(g4 + j + 1) * 128], identb[:], r=[skey, 'identb'], w=pk(b))
                pin = pv[:, 0:512].rearrange("p (a b) -> p a b", b=128)
                if gain is None:
                    ACT(dst[:, g4:g4 + 4, tcol:tcol + 128], pin, AF.Copy, r=pk(b), w=[dkey])
                else:
                    TT(dst[:, g4:g4 + 4, tcol:tcol + 128], pin, gain(g4).unsqueeze(2).to_broadcast([128, 4, 128]), ALU.mult,
                       r=pk(b) + ['par'], w=[dkey])

        def rms_to_fm(src, skey, gi, dst, dkey, tcol):
            norm_rstd(src, [skey], D, 0)
            ACT(XNT[:], src, AF.Copy, r=[skey, 'ss'], w=['xnt'], scale=SS[:, 0:1])
            to_fm(XNT, 'xnt', dst, dkey, tcol, lambda g4: GF[:, gi, g4:g4 + 4])

        dbg_out = {}

        def dbg(name, ap, keys):
            if name not in dbg_names:
                return
            shp = list(ap.shape)
            o = nc.dram_tensor("dbg_" + name, shp, F32, kind="ExternalOutput").ap()
            dbg_out[name] = o
            dma('pool', o, ap, r=keys, key='dbg')

        pass_idx = [0]

        def chk(name):
            if stop == name or stop == f"{name}@{pass_idx[0]}":
                raise _Stop()

        try:
            MNF = U_fm
            for mt_ in range(2):
                dma('sp', H[:, 0, :], mem_p[mt_ * 128:(mt_ + 1) * 128, :], w=[('H', 0)], key='xin')
                rms_to_fm(H[:, 0, :], ('H', 0), 2, MNF, 'u_fm', mt_ * 128)
            for cb_ in range(4):
                def epi_k(ci, b, cb_=cb_):
                    ACT(KFM[:, cb_ * 4 + ci, :], psum[:, b, 0:NMEM], AF.Copy, r=pk(b), w=['kfm'])
                s = proj_fm(MNF, 'u_fm', NMEM, w_ck, cb_ * 512, 4, epi_k)

                def epi_kt(t, b, cb_=cb_):
                    ACT(OST[:], psum[:, b, :], AF.Copy, r=pk(b), w=['ost'])
                    dma('sp', o_pmk[t * 128:(t + 1) * 128, cb_ * 512:(cb_ + 1) * 512], OST[:], r=['ost'], key='o1')
                proj_tm(MNF, 'u_fm', range(2), w_ck, cb_ * 512, 512, epi_kt, slot=s)
            for cb_ in range(4):
                def epi_vt(t, b, cb_=cb_):
                    ACT(OST[:], psum[:, b, :], AF.Copy, r=pk(b), w=['ost'])
                    CP(VTM[:, t, cb_ * 512:(cb_ + 1) * 512], OST[:], r=['ost'], w=['vtm'])
                    dma('sp', o_pmv[t * 128:(t + 1) * 128, cb_ * 512:(cb_ + 1) * 512], OST[:], r=['ost'], key='o1')
                proj_tm(MNF, 'u_fm', range(2), w_cv, cb_ * 512, 512, epi_vt)

            MS(ST[:], 0.0, w=['st'])
            MS(GS[:], 0.0, w=['gs'])
            MS(HALO[:], 0.0, w=['halo']); MS(FHALO[:], 0.0, w=['fhalo'])
            chk('memkv')

            tiles_all = [('P', i) for i in range(NPT)] + ([('S', 0)] if NS > 0 else [])
            passes = [tiles_all[i:i + NT] for i in range(0, len(tiles_all), NT)]

            for pi_, tiles in enumerate(passes):
                pass_idx[0] = pi_
                nt = len(tiles)
                Tn = nt * 128
                has_s = tiles[-1][0] == 'S'
                np_t = nt - (1 if has_s else 0)
                Tp = np_t * 128
                flagged = [ti for ti, (k, i) in enumerate(tiles) if k == 'S' or i == NPT - 1]
                for ti, (k, i) in enumerate(tiles):
                    src = x_p[i * 128:(i + 1) * 128, :] if k == 'P' else x_s[:, :]
                    dma('sp', H[:, ti, :], src, w=[('H', ti)], key='xin')
                    rms_to_fm(H[:, ti, :], ('H', ti), 0, A_fm, 'a_fm', ti * 128)
                dbg('a_fm', A_fm[:, :, 0:Tn], ['a_fm'])
                chk('a_fm')

                for cb_ in range(4):
                    def epi_z(t, b, cb_=cb_):
                        ACT(SZ[:, t, cb_ * 512:(cb_ + 1) * 512], psum[:, b, :], AF.Silu, r=pk(b), w=[('sz', t)])
                    proj_tm(A_fm, 'a_fm', range(nt), w_in, C_Z + cb_ * 512, 512, epi_z)
                if has_s:
                    dma('sp', T1[0:NS * 3, :], st_sc[:, 0:D], w=['t1'], key='xin')
                    dma('sp', T2[0:NS * 3, 0:1024], st_sc[:, D:CONV_DIM], w=['t2'], key='xin')

                def conv_chunk(b, c, RAWb, rk, ACCb, ak, HAL, W, ntap, dst_fn):
                    hl = ntap - 1
                    if np_t > 0:
                        ACT(RAWb[:, hl:hl + Tp], psum[:, b, 0:Tp], AF.Copy, r=pk(b), w=[rk])
                        ACT(RAWb[:, 0:hl], HAL[:, c, :], AF.Copy, r=['halo'], w=[rk])
                        TS(ACCb[:, 0:Tp], RAWb[:, hl:hl + Tp], W[:, c, hl:hl + 1], None, ALU.mult, None, r=[rk, 'par', 'par'], w=[ak])
                        for k_ in range(hl):
                            STT(ACCb[:, 0:Tp], RAWb[:, k_:k_ + Tp], W[:, c, k_:k_ + 1], ACCb[:, 0:Tp], ALU.mult, ALU.add, r=[rk, 'par', 'par', ak], w=[ak])
                        ACT(HAL[:, c, :], RAWb[:, Tp:Tp + hl], AF.Copy, r=[rk], w=['halo'])

                def conv_chunk_s(b, c, RS, rsk, stT, stkeys, ccol, ACCb, ak, W, ntap):
                    hl = ntap - 1
                    b2 = PS()
                    tr(psum[:, b2, 0:NS * hl], stT[0:NS * hl, ccol:ccol + 128], identf[0:NS * hl, 0:NS * hl], r=stkeys + ['cst'], w=pk(b2))
                    ACT(RS[:, 0:NS, 0:hl], psum[:, b2, 0:NS * hl].rearrange("p (s r) -> p s r", r=hl), AF.Copy, r=pk(b2), w=[rsk])
                    ACT(RS[:, :, hl:hl + 4], psum[:, b, Tp:Tp + 64].rearrange("p (s r) -> p s r", r=4), AF.Copy, r=pk(b), w=[rsk])
                    accv = ACCb[:, Tp:Tp + 64].rearrange("p (s r) -> p s r", r=4)
                    TS(accv, RS[:, :, hl:hl + 4], W[:, c, hl:hl + 1], None, ALU.mult, None, r=[rsk, 'par', 'par'], w=[ak])
                    for k_ in range(hl):
                        STT(accv, RS[:, :, k_:k_ + 4], W[:, c, k_:k_ + 1], accv, ALU.mult, ALU.add, r=[rsk, 'par', 'par', ak], w=[ak])

                if has_s:
                    MS(RAWS[:], 0.0, w=['raws']); MS(RAWS2[:], 0.0, w=['raws2'])
                for cb_ in range(6):
                    def epi_x(ci, b, cb_=cb_):
                        c = cb_ * 4 + ci
                        conv_chunk(b, c, RAW, 'raw', ACC, 'acc', HALO, CW, 4, None)
                        if has_s:
                            stT = T1 if c < 16 else T2
                            conv_chunk_s(b, c, RAWS, 'raws', stT, ['t1', 't2'], (c % 16) * 128, ACC, 'acc', CW, 4)
                            MS(XBC[:, c, Tp + 64:Tp + 128], 0.0, w=['xbc'])
                        nact = Tp + (64 if has_s else 0)
                        ACT(XBC[:, c, 0:nact], ACC[:, 0:nact], AF.Silu, r=['acc', 'par'], w=['xbc'], bias=CB[:, c:c + 1])
                    s = proj_fm(A_fm, 'a_fm', Tn, w_in, C_XBC + cb_ * 512, 4, epi_x)
                    for ti in flagged:
                        b = PS()
                        for kc in range(16):
                            mm(psum[:, b, :], A_fm[:, kc, ti * 128:(ti + 1) * 128], WS[s][:, kc, :], kc == 0, kc == 15, r=['a_fm', ('W', s)], w=pk(b))
                        ACT(OST[:], psum[:, b, :], AF.Copy, r=pk(b), w=['ost'])
                        if tiles[ti][0] == 'P':
                            dma('sp', o_psc[:, cb_ * 512:(cb_ + 1) * 512], OST[125:128, :], r=['ost'], key='o1')
                        else:
                            for r_ in range(1, 4):
                                dma('sp', o_ssc[:, r_ - 1, cb_ * 512:(cb_ + 1) * 512], OST[r_:4 * NS:4, :], r=['ost'], key='o1')
                dbg('xbc', XBC[:, :, 0:Tn], ['xbc'])
                chk('xbc')

                def epi_dt(t, b):
                    x_ = TMP32[:, 0, :]; a_ = TMP32[:, 1, :]
                    TT(x_, psum[:, b, 0:32], DTB[:], ALU.add, r=pk(b) + ['par'], w=['tmpA'])
                    STT(a_, x_, -1.0, x_, ALU.mult, ALU.max, r=['tmpA'], w=['tmpB'])
                    ACT(a_, a_, AF.Exp, r=['tmpB'], w=['tmpB'], scale=-1.0)
                    ACT(a_, a_, AF.Ln, r=['tmpB'], w=['tmpB'], bias=1.0)
                    STT(DT[:, t, :], x_, 0.0, a_, ALU.max, ALU.add, r=['tmpA', 'tmpB'], w=['dt'])
                proj_tm(A_fm, 'a_fm', range(nt), w_in, C_DT, 32, epi_dt)
                dbg('dt', DT[:, 0:nt, :], ['dt'])
                chk('dt')

                ACT(STB[:], ST[:], AF.Copy, r=['st'], w=['stb'])
                for ti, (kind, i) in enumerate(tiles):
                    kc_ = KC[kind]
                    tc = ti * 128
                    for g4 in range(0, 16, 4):
                        b = PS(); pv = psum[:, b, :].bitcast(BF16)
                        for j in range(4):
                            tr(pv[:, j * 128:(j + 1) * 128], XBC[:, g4 + j, tc:tc + 128], identb[:], r=['xbc', 'identb'], w=pk(b))
                        ACT(XS[:, g4 * 128:(g4 + 4) * 128], pv[:, 0:512], AF.Copy, r=pk(b), w=['xs'])
                    b = PS(); pv = psum[:, b, :].bitcast(BF16)
                    for j in range(4):
                        tr(pv[:, j * 128:(j + 1) * 128], XBC[:, 16 + j, tc:tc + 128], identb[:], r=['xbc', 'identb'], w=pk(b))
                    ACT(BT[:], pv[:, 0:512], AF.Copy, r=pk(b), w=['bt'])
                    a_ = TMP32[:, 2, :]; acum = TMP32[:, 3, :]; nacum = TMP32[:, 4, :]; wgt = TMP32[:, 5, :]; elast = TMP32[:, 6, :]; eacum = TMP32[:, 7, :]
                    TT(a_, DT[:, ti, :], AB[:], ALU.mult, r=['dt', 'ab'], w=['tmp_a'])
                    b = PS()
                    mm(psum[:, b, 0:32], kc_['U'], a_, True, True, r=['tmp_a'] + CK, w=pk(b))
                    mm(psum[:, b, 32:64], kc_['BLK'], a_, True, True, r=['tmp_a'] + CK, w=pk(b))
                    CP(acum, psum[:, b, 0:32], r=pk(b), w=['tmp_ac'])
                    TS(nacum, psum[:, b, 0:32], -1.0, None, ALU.mult, None, r=pk(b), w=['tmp_nac'])
                    ACT(eacum, psum[:, b, 0:32], AF.Exp, r=pk(b), w=['tmp_eac'])
                    ACT(elast, psum[:, b, 32:64], AF.Exp, r=pk(b), w=['tmp_el'])
                    TT(wgt, psum[:, b, 32:64], acum, ALU.subtract, r=pk(b) + ['tmp_ac'], w=['tmp_w'])
                    ACT(wgt, wgt, AF.Exp, r=['tmp_w'], w=['tmp_w'])
                    TT(wgt, wgt, DT[:, ti, :], ALU.mult, r=['tmp_w', 'dt'], w=['tmp_w'])
                    b = PS()
                    for g_ in range(4):
                        mm(psum[:, b, g_ * 128:(g_ + 1) * 128], XBC[:, 16 + g_, tc:tc + 128], XBC[:, 20 + g_, tc:tc + 128], True, True, r=['xbc'], w=pk(b))
                    ACT(CBT[:], psum[:, b, :].rearrange("p (a b) -> p a b", b=128), AF.Copy, r=pk(b), w=['cbt'])
                    TT(XW[:].rearrange("p (h d) -> p h d", d=64), XS[:].rearrange("p (h d) -> p h d", d=64),
                       wgt.unsqueeze(2).to_broadcast([128, 32, 64]), ALU.mult, r=['xs', 'tmp_w'], w=['xw'])
                    if kind == 'P':
                        for g_ in range(4):
                            b = PS()
                            mm(psum[:, b, :], XBC[:, 20 + g_, tc:tc + 128], STB[:, g_ * 512:(g_ + 1) * 512], True, True, r=['xbc', 'stb'], w=pk(b))
                            TT(T1[:, g_ * 512:(g_ + 1) * 512].rearrange("p (h d) -> p h d", d=64), psum[:, b, :].rearrange("p (h d) -> p h d", d=64),
                               eacum[:, g_ * 8:(g_ + 1) * 8].unsqueeze(2).to_broadcast([128, 8, 64]), ALU.mult, r=pk(b) + ['tmp_eac'], w=['t1'])
                    else:
                        AEX = T2
                        CP(AEX[:].rearrange("p (h d) -> p h d", d=64), a_.unsqueeze(2).to_broadcast([128, 32, 64]), r=['tmp_a'], w=['t2'])
                        b = PS()
                        for t_ in range(16):
                            mm(psum[:, b, t_ * 16:(t_ + 1) * 16], AEX[:, t_ * 128:(t_ + 1) * 128], BLKC, True, True, r=['t2', 'cs2'], w=pk(b))
                        ACT(ELS[:], psum[:, b, 0:256].rearrange("p (t s) -> p t s", s=16), AF.Exp, r=pk(b), w=['els'])
                        for s_ in range(NS):
                            dma('sp', SQ[:], st_ssd[s_].rearrange("(t p) n -> p t n", p=128), w=['sq'])
                            for q in range(4):
                                b = q % 2
                                for j in range(4):
                                    tr(psum[:, b, j * 128:(j + 1) * 128], SQ[:, q * 4 + j, :], identf, r=['sq', 'cst'], w=pk(b))
                                ACT(SQB[:, q * 512:(q + 1) * 512], psum[:, b, :], AF.Copy, r=pk(b), w=['sqb'])
                            for g_ in range(4):
                                TT(CMS[:, g_, :], XBC[:, 20 + g_, tc:tc + 64], BLKS[:, s_, :], ALU.mult, r=['xbc', 'cs2'], w=['cms'])
                            for g_ in range(4):
                                mm(psum[0:64, 4 + g_, :], CMS[:, g_, :], SQB[:, g_ * 512:(g_ + 1) * 512], s_ == 0, s_ == NS - 1, r=['cms', 'sqb'], w=pk(4 + g_))
                            TS(XWM[:], XW[:], BLKC[:, s_:s_ + 1], None, ALU.mult, None, r=['xw', 'cs2'], w=['xwm'])
                            for q in range(4):
                                b = 2 + q % 2
                                for j in range(4):
                                    t_ = q * 4 + j
                                    mm(psum[:, b, j * 128:(j + 1) * 128], XWM[:, t_ * 128:(t_ + 1) * 128], BT[:, (t_ // 4) * 128:(t_ // 4 + 1) * 128], True, True,
                                       r=['xwm', 'bt'], w=pk(b))
                                for j in range(4):
                                    t_ = q * 4 + j
                                    STT(SQ[:, t_, :], SQ[:, t_, :], ELS[:, t_, s_:s_ + 1], psum[:, b, j * 128:(j + 1) * 128], ALU.mult, ALU.add,
                                        r=['sq', 'els'] + pk(b), w=['sq'])
                            dma('sp', o_sssd[s_].rearrange("(t p) n -> p t n", p=128), SQ[:], r=['sq'])
                        for g_ in range(4):
                            TT(T1[0:64, g_ * 512:(g_ + 1) * 512].rearrange("p (h d) -> p h d", d=64), psum[0:64, 4 + g_, :].rearrange("p (h d) -> p h d", d=64),
                               eacum[0:64, g_ * 8:(g_ + 1) * 8].unsqueeze(2).to_broadcast([64, 8, 64]), ALU.mult, r=pk(4 + g_) + ['tmp_eac'], w=['t1'])
                        MS(T1[64:128, :], 0.0, w=['t1'])
                    for h4 in range(8):
                        TT(UA[:], kc_['U'].unsqueeze(1).to_broadcast([128, 4, 128]), a_[:, h4 * 4:(h4 + 1) * 4].unsqueeze(2).to_broadcast([128, 4, 128]), ALU.mult,
                           r=['tmp_a'] + CK, w=['ua'])
                        b = PS()
                        mm(psum[:, b, :], ONE, UA[:].rearrange("p a b -> p (a b)"), True, False, r=['ua'] + CK, w=pk(b))
                        for hh in range(4):
                            mm(psum[:, b, hh * 128:(hh + 1) * 128], identf, kc_['MN'], False, hh == 3, r=CK, w=pk(b))
                        mb = h4 % 2
                        for hh in range(4):
                            h = h4 * 4 + hh
                            ACT(DEC[:, hh % 2, :], psum[:, b, hh * 128:(hh + 1) * 128], AF.Exp, r=pk(b) + ['tmp_nac'], w=[('dec', hh % 2)],
                                bias=nacum[:, h:h + 1])
                            STT(MT[:, mb, hh, :], DEC[:, hh % 2, :], DT[:, ti, h:h + 1], CBT[:, h // 8, :], ALU.mult, ALU.mult,
                                r=[('dec', hh % 2), 'dt', 'cbt'], w=[('mt', mb)])
                        for hh in range(4):
                            h = h4 * 4 + hh
                            bb = 4 + h // 8
                            mm(psum[:, bb, (h % 8) * 64:(h % 8 + 1) * 64], MT[:, mb, hh, :], XS[:, h * 64:(h + 1) * 64], True, True, r=[('mt', mb), 'xs'], w=pk(bb))
                    for g_ in range(4):
                        TT(T1[:, g_ * 512:(g_ + 1) * 512], T1[:, g_ * 512:(g_ + 1) * 512], psum[:, 4 + g_, :], ALU.add, r=['t1'] + pk(4 + g_), w=['t1'])
                    TT(T2[:].rearrange("p (h d) -> p h d", d=64), XS[:].rearrange("p (h d) -> p h d", d=64),
                       DB[:].unsqueeze(2).to_broadcast([128, 32, 64]), ALU.mult, r=['xs', 'par'], w=['t2'])
                    TT(T1[:], T1[:], T2[:], ALU.add, r=['t1', 't2'], w=['t1'])
                    TT(T1[:], T1[:], SZ[:, ti, :], ALU.mult, r=['t1', ('sz', ti)], w=['t1'])
                    if ti == 0:
                        dbg('yssd', T1[:], ['t1'])
                    for g_ in range(4):
                        ACT(T2[:, g_ * 512:(g_ + 1) * 512], T1[:, g_ * 512:(g_ + 1) * 512], AF.Square, r=['t1'], w=['t2', 'ss'], scale=512.0 ** -0.5,
                            accum_out=SS[:, 1 + g_:2 + g_])
                    rstd_from_ss(1, 4)
                    for g_ in range(4):
                        ACT(XNT[:, g_ * 512:(g_ + 1) * 512], T1[:, g_ * 512:(g_ + 1) * 512], AF.Copy, r=['t1', 'ss'], w=['xnt'], scale=SS[:, 1 + g_:2 + g_])
                    to_fm(XNT, 'xnt', U_fm, 'u_fm', tc, lambda g4: GF[:, 4, g4:g4 + 4])
                    if kind == 'P':
                        for g_ in range(4):
                            b = PS()
                            mm(psum[:, b, :], BT[:, g_ * 128:(g_ + 1) * 128], XW[:, g_ * 512:(g_ + 1) * 512], True, True, r=['bt', 'xw'], w=pk(b))
                            sv = ST[:, g_ * 512:(g_ + 1) * 512]
                            TT(sv.rearrange("p (h d) -> p h d", d=64), sv.rearrange("p (h d) -> p h d", d=64),
                               elast[:, g_ * 8:(g_ + 1) * 8].unsqueeze(2).to_broadcast([128, 8, 64]), ALU.mult, r=['st', 'tmp_el'], w=['st'])
                            TT(sv, sv, psum[:, b, :], ALU.add, r=['st'] + pk(b), w=['st'])
                        ACT(STB[:], ST[:], AF.Copy, r=['st'], w=['stb'])
                dbg('u_fm', U_fm[:, :, 0:Tn], ['u_fm'])
                chk('ssd')

                for cb_ in range(4):
                    def epi_ga(ci, b, cb_=cb_):
                        ACT(SG[:, cb_ * 4 + ci, 0:Tn], psum[:, b, 0:Tn], AF.Sigmoid, r=pk(b), w=['sg'])
                    proj_fm(A_fm, 'a_fm', Tn, w_in, C_GA + cb_ * 512, 4, epi_ga)
                for cb_ in range(4):
                    def epi_a(ci, b, cb_=cb_):
                        TT(MRG[:, cb_ * 4 + ci, 0:Tn], psum[:, b, 0:Tn], SG[:, cb_ * 4 + ci, 0:Tn], ALU.mult, r=pk(b) + ['sg'], w=['mrg'])
                    proj_fm(U_fm, 'u_fm', Tn, w_ssd_out, cb_ * 512, 4, epi_a)

                if not SKIPGLA:
                    for cb_ in range(2):
                        def epi_q(ci, b, cb_=cb_):
                            ACT(QF[:, cb_ * 4 + ci, 0:Tn], psum[:, b, 0:Tn], AF.Copy, r=pk(b), w=['qf'], scale=1.0 / 16.0)
                        proj_fm(A_fm, 'a_fm', Tn, w_in, C_Q + cb_ * 512, 4, epi_q)
                    for cb_ in range(2):
                        def epi_kf(ci, b, cb_=cb_):
                            ACT(KF[:, cb_ * 4 + ci, 0:Tn], psum[:, b, 0:Tn], AF.Copy, r=pk(b), w=['kf'])
                        proj_fm(A_fm, 'a_fm', Tn, w_in, C_K + cb_ * 512, 4, epi_kf)
                    for cb_ in range(4):
                        def epi_v(t, b, cb_=cb_):
                            ACT(VT[:, t, cb_ * 512:(cb_ + 1) * 512], psum[:, b, :], AF.Copy, r=pk(b), w=['vt'])
                        proj_tm(A_fm, 'a_fm', range(nt), w_in, C_V + cb_ * 512, 512, epi_v)
                    for cb_ in range(4):
                        def epi_r(t, b, cb_=cb_):
                            ACT(SR[:, t, cb_ * 512:(cb_ + 1) * 512], psum[:, b, :], AF.Silu, r=pk(b), w=['sr'])
                        proj_tm(A_fm, 'a_fm', range(nt), w_in, C_R + cb_ * 512, 512, epi_r)
                    s = load_w(w_in, 0, 16, C_G, 16)
                    b = PS()
                    for kc in range(16):
                        mm(psum[0:16, b, 0:Tn], WS[s][:, kc, 0:16], A_fm[:, kc, 0:Tn], kc == 0, kc == 15, r=['a_fm', ('W', s)], w=pk(b))
                    MS(GLR[:, 0:Tn], 1.0, w=['glr'])
                    ACT(GLR[0:16, 0:Tn], psum[0:16, b, 0:Tn], AF.Copy, r=pk(b), w=['glr'])

                    ACT(GSB[:].rearrange("p a b -> p (a b)"), GS[:].rearrange("p a b -> p (a b)"), AF.Copy, r=['gs'], w=['gsb'])
                    for ti, (kind, i) in enumerate(tiles):
                        kc_ = KC[kind]
                        tc = ti * 128
                        for hb in range(2):
                            b = PS()
                            mm(psum[:, b, :], GLR[0:17, tc:tc + 128], WG[0:17, hb * 512:(hb + 1) * 512], True, True, r=['glr', 'wg'], w=pk(b))
                            gs_ = slice(hb * 512, (hb + 1) * 512)
                            CP(GT2[:, gs_], psum[:, b, :], r=pk(b), w=['t2'])
                            STT(GT1[:, gs_], GT2[:, gs_], -1.0, GT2[:, gs_], ALU.mult, ALU.max, r=['t2'], w=['t1'])
                            ACT(GT1[:, gs_], GT1[:, gs_], AF.Exp, r=['t1'], w=['t1'], scale=-1.0)
                            ACT(GT1[:, gs_], GT1[:, gs_], AF.Ln, r=['t1'], w=['t1'], bias=1.0)
                            TS(GT2[:, gs_], GT2[:, gs_], 0.0, None, ALU.min, None, r=['t2'], w=['t2'])
                            TT(G[:, gs_], GT2[:, gs_], GT1[:, gs_], ALU.subtract, r=['t1', 't2'], w=['g'])
                        TS(G[:], G[:], 1.0 / 16.0, None, ALU.mult, None, r=['g'], w=['g'])
                        if ti == 0:
                            dbg('glog', G[:], ['g'])
                        for c in range(8):
                            b = PS()
                            mm(psum[:, b, 0:128], G[:, c * 128:(c + 1) * 128], kc_['U'], True, True, r=['g'] + CK, w=pk(b))
                            ACT(EBT[:, 0, :], psum[:, b, 0:128], AF.Exp, r=pk(b), w=[('ebt', 0)])
                            ACT(EBT[:, 1, :], psum[:, b, 0:128], AF.Exp, r=pk(b), w=[('ebt', 1)], scale=-1.0)
                            TT(QE[:, c, :], QF[:, c, tc:tc + 128], EBT[:, 0, :], ALU.mult, r=['qf', ('ebt', 0)], w=['qe'])
                            TT(KE[:, c, :], KF[:, c, tc:tc + 128], EBT[:, 1, :], ALU.mult, r=['kf', ('ebt', 1)], w=['ke'])
                        b = PS()
                        for h in range(4):
                            for kc in range(2):
                                mm(psum[:, b, h * 128:(h + 1) * 128], KE[:, h * 2 + kc, :], QE[:, h * 2 + kc, :], kc == 0, kc == 1, r=['ke', 'qe'], w=pk(b))
                        TT(ATT[:], psum[:, b, :].rearrange("p (h i) -> p h i", i=128), kc_['M01'][:].unsqueeze(1).to_broadcast([128, 4, 128]), ALU.mult,
                           r=pk(b) + ['m01b'], w=['att'])
                        for hb in range(2):
                            b = PS()
                            mm(psum[:, b, :], kc_['L'], G[:, hb * 512:(hb + 1) * 512], True, True, r=['g'] + CK, w=pk(b))
                            ACT(GT1[:, hb * 512:(hb + 1) * 512], psum[:, b, :], AF.Exp, r=pk(b), w=['t1'])
                        b = PS(); pv = psum[:, b, :].bitcast(BF16)
                        for c in range(8):
                            tr(pv[:, c * 128:(c + 1) * 128], KF[:, c, tc:tc + 128], identb[:], r=['kf', 'identb'], w=pk(b))
                        ACT(KW[:], pv[:, 0:1024], AF.Copy, r=pk(b), w=['kw'])
                        TT(KW[:], KW[:], GT1[:], ALU.mult, r=['kw', 't1'], w=['kw'])
                        b = PS()
                        ncol = 1 if kind == 'P' else 16
                        for c in range(8):
                            mm(psum[:, b, c * 16:c * 16 + ncol], G[:, c * 128:(c + 1) * 128], (ONE[:, 0:1] if kind == 'P' else BLKC), True, True,
                               r=['g'] + CK, w=pk(b))
                        ACT(ELG[:, :, 0:ncol], psum[:, b, 0:128].rearrange("p (c s) -> p c s", s=16)[:, :, 0:ncol], AF.Exp, r=pk(b), w=['elg'])
                        if kind == 'P':
                            for h in range(4):
                                mm(psum[:, 4 + h, :], ATT[:, h, :], VT[:, ti, h * 512:(h + 1) * 512], True, False, r=['att', 'vt'], w=pk(4 + h))
                                for kc in range(2):
                                    mm(psum[:, 4 + h, :], QE[:, h * 2 + kc, :], GSB[:, h * 2 + kc, :], False, kc == 1, r=['qe', 'gsb'], w=pk(4 + h))
                            for c in range(8):
                                b = PS()
                                mm(psum[:, b, :], KW[:, c * 128:(c + 1) * 128], VT[:, ti, (c // 2) * 512:(c // 2 + 1) * 512], True, True, r=['kw', 'vt'], w=pk(b))
                                STT(GS[:, c, :], GS[:, c, :], ELG[:, c, 0:1], psum[:, b, :], ALU.mult, ALU.add, r=['gs', 'elg'] + pk(b), w=['gs'])
                        else:
                            for h in range(4):
                                mm(psum[:, 4 + h, :], ATT[:, h, :], VT[:, ti, h * 512:(h + 1) * 512], True, False, r=['att', 'vt'], w=pk(4 + h))
                            for s_ in range(NS):
                                sgv = SQ[:].rearrange("p a b -> p (a b)")
                                for hh in range(2):
                                    dma('sp', sgv.rearrange("p (c v) -> p c v", v=512),
                                        st_gla[s_, hh * 2:hh * 2 + 2].rearrange("h (kc p) v -> p (h kc) v", p=128), w=['sq'], key='sq')
                                    ACT(SQB[:], sgv, AF.Copy, r=['sq'], w=['sqb'])
                                    TT(CMS[:, 0:4, :], QE[:, hh * 4:hh * 4 + 4, 0:64], BLKS[:, s_:s_ + 1, :].to_broadcast([128, 4, 64]), ALU.mult,
                                       r=['qe', 'cs2'], w=['cms'])
                                    for c4 in range(4):
                                        h = hh * 2 + c4 // 2
                                        last = (s_ == NS - 1) and (c4 % 2 == 1)
                                        mm(psum[0:64, 4 + h, :], CMS[:, c4, :], SQB[:, c4 * 512:(c4 + 1) * 512], False, last, r=['cms', 'sqb'], w=pk(4 + h))
                                    TS(XWM[:, 0:512], KW[:, hh * 512:(hh + 1) * 512], BLKC[:, s_:s_ + 1], None, ALU.mult, None, r=['kw', 'cs2'], w=['xwm'])
                                    for c4 in range(4):
                                        c = hh * 4 + c4
                                        b = PS()
                                        mm(psum[:, b, :], XWM[:, c4 * 128:(c4 + 1) * 128], VT[:, ti, (c // 2) * 512:(c // 2 + 1) * 512], True, True, r=['xwm', 'vt'], w=pk(b))
                                        STT(SQ[:, c4 * 4:(c4 + 1) * 4, :].rearrange("p a b -> p (a b)"), SQ[:, c4 * 4:(c4 + 1) * 4, :].rearrange("p a b -> p (a b)"),
                                            ELG[:, c, s_:s_ + 1], psum[:, b, :], ALU.mult, ALU.add, r=['sq', 'elg'] + pk(b), w=['sq'])
                                    dma('sp', o_sgla[s_, hh * 2:hh * 2 + 2].rearrange("h (kc p) v -> p (h kc) v", p=128),
                                        sgv.rearrange("p (c v) -> p c v", v=512), r=['sq'], key='sq')
                        for h in range(4):
                            ACT(T2[:, h * 512:(h + 1) * 512], psum[:, 4 + h, :], AF.Square, r=pk(4 + h), w=['t2', 'ss'], scale=512.0 ** -0.5, accum_out=SS[:, 1 + h:2 + h])
                        rstd_from_ss(1, 4)
                        for h in range(4):
                            ACT(T1[:, h * 512:(h + 1) * 512], psum[:, 4 + h, :], AF.Copy, r=pk(4 + h) + ['ss'], w=['t1'], scale=SS[:, 1 + h:2 + h])
                        if ti == 0:
                            dbg('ogla', T1[:], ['t1'])
                        TT(XNT[:], T1[:], SR[:, ti, :], ALU.mult, r=['t1', 'sr'], w=['xnt'])
                        to_fm(XNT, 'xnt', U_fm, 'u_fm', tc, lambda g4: GF[:, 5, 0:4])
                        if kind == 'P':
                            ACT(GSB[:].rearrange("p a b -> p (a b)"), GS[:].rearrange("p a b -> p (a b)"), AF.Copy, r=['gs'], w=['gsb'])
                    dbg('o_fm', U_fm[:, :, 0:Tn], ['u_fm'])
                    chk('gla')

                for cb_ in range(int(os.environ.get("GBN", 4))):
                    def epi_gb(ci, b, cb_=cb_):
                        ACT(SG[:, cb_ * 4 + ci, 0:Tn], psum[:, b, 0:Tn], AF.Sigmoid, r=pk(b), w=['sg'])
                    proj_fm(A_fm, 'a_fm', Tn, w_in, C_GB + cb_ * 512, 4, epi_gb)
                chk('gb')
                for cb_ in range(4):
                    def epi_b(ci, b, cb_=cb_):
                        c = cb_ * 4 + ci
                        TT(TMPG[:, 0:Tn], psum[:, b, 0:Tn], SG[:, c, 0:Tn], ALU.mult, r=pk(b) + ['sg'], w=['tmpg'])
                        TT(MRG[:, c, 0:Tn], MRG[:, c, 0:Tn], TMPG[:, 0:Tn], ALU.add, r=['mrg', 'tmpg'], w=['mrg'])
                    proj_fm(U_fm, 'u_fm', Tn, w_gla_out, cb_ * 512, 4, epi_b)
                dbg('mrg', MRG[:, :, 0:Tn], ['mrg'])
                chk('mrg')
                for ti, (k, i) in enumerate(tiles):
                    src = x_p[i * 128:(i + 1) * 128, :] if k == 'P' else x_s[:, :]
                    dma('sp', H[:, ti, :], src, w=[('H', ti)])
                for cb_ in range(4):
                    def epi_m(t, b, cb_=cb_):
                        hv = H[:, t, cb_ * 512:(cb_ + 1) * 512]
                        TT(hv, hv, psum[:, b, :], ALU.add, r=[('H', t)] + pk(b), w=[('H', t)])
                    proj_tm(MRG, 'mrg', range(nt), w_mix, cb_ * 512, 512, epi_m)
                dbg('h1', H[:, 0:nt, :], [('H', t) for t in range(nt)])
                chk('h1')

                for ti in range(nt):
                    rms_to_fm(H[:, ti, :], ('H', ti), 1, A_fm, 'a_fm', ti * 128)
                QC = SG
                for cb_ in range(4):
                    def epi_cq(ci, b, cb_=cb_):
                        ACT(QC[:, cb_ * 4 + ci, 0:Tn], psum[:, b, 0:Tn], AF.Copy, r=pk(b), w=['sg'], scale=512.0 ** -0.5)
                    proj_fm(A_fm, 'a_fm', Tn, w_cq, cb_ * 512, 4, epi_cq)
                OC = U_fm

                def softmax_rows(np_):
                    for h in range(4):
                        P.op('dve', lambda e, h=h: e.reduce_max(out=SS[0:np_, 1 + h:2 + h], in_=psum[0:np_, 4 + h, 0:NMEM], axis=mybir.AxisListType.X),
                             r=pk(4 + h), w=['ss'])
                    TS(SS[0:np_, 1:5], SS[0:np_, 1:5], -1.0, None, ALU.mult, None, r=['ss'], w=['ss'])
                    for h in range(4):
                        ACT(PSM[0:np_, h, :], psum[0:np_, 4 + h, 0:NMEM], AF.Exp, r=pk(4 + h) + ['ss'], w=['t1', 'tmpA'], bias=SS[0:np_, 1 + h:2 + h],
                            accum_out=TMP32[0:np_, 0, h:h + 1])
                    P.op('dve', lambda e: e.reciprocal(out=TMP32[0:np_, 1, 0:4], in_=TMP32[0:np_, 0, 0:4]), r=['tmpA'], w=['tmpB'])
                    TT(PB[0:np_], PSM[0:np_], TMP32[0:np_, 1, 0:4].unsqueeze(2).to_broadcast([np_, 4, NMEM]), ALU.mult, r=['t1', 'tmpB'], w=['pb'])

                def probs_T(np_):
                    b = PS(); pv = psum[:, b, :].bitcast(BF16)
                    for h in range(4):
                        for m_ in range(2):
                            j = h * 2 + m_
                            tr(pv[:, j * 128:j * 128 + np_], PB[0:np_, h, m_ * 128:(m_ + 1) * 128], identb[0:np_, 0:np_], r=['pb', 'identb'], w=pk(b))
                    ACT(PT[:, :, 0:np_], pv[:, 0:1024].rearrange("p (j t) -> p j t", t=128)[:, :, 0:np_], AF.Copy, r=pk(b), w=['ptt'])

                for ti, (kind, i) in enumerate(tiles):
                    tc = ti * 128
                    if kind == 'P':
                        for h in range(4):
                            for dc in range(4):
                                mm(psum[:, 4 + h, 0:NMEM], QC[:, h * 4 + dc, tc:tc + 128], KFM[:, h * 4 + dc, :], dc == 0, dc == 3, r=['sg', 'kfm'], w=pk(4 + h))
                        softmax_rows(128)
                        probs_T(128)
                        for q in range(4):
                            b = PS()
                            for j in range(4):
                                dc = q * 4 + j
                                for m_ in range(2):
                                    mm(psum[:, b, j * 128:(j + 1) * 128], VTM[:, m_, dc * 128:(dc + 1) * 128], PT[:, (dc // 4) * 2 + m_, :], m_ == 0, m_ == 1,
                                       r=['vtm', 'ptt'], w=pk(b))
                            ACT(OC[:, q * 4:(q + 1) * 4, tc:tc + 128], psum[:, b, :].rearrange("p (a t) -> p a t", t=128), AF.Copy, r=pk(b), w=['u_fm'])
                    else:
                        for s_ in range(NS):
                            dma('pool', KSB[:], ck[s_].rearrange("(m p) d -> p m d", p=128), w=['ksb'], key='ksb')
                            for m_ in range(2):
                                for q in range(2):
                                    b = PS(); pv = psum[:, b, :].bitcast(BF16)
                                    for j in range(8):
                                        dc = q * 8 + j
                                        tr(pv[:, j * 128:(j + 1) * 128], KSB[:, m_, dc * 128:(dc + 1) * 128], identb[:], r=['ksb', 'identb'], w=pk(b))
                                    ACT(KTS[:, q * 8:(q + 1) * 8, m_ * 128:(m_ + 1) * 128], pv[:, 0:1024].rearrange("p (j t) -> p j t", t=128), AF.Copy,
                                        r=pk(b), w=['kts'])
                            TT(CMS[:], QC[:, :, tc:tc + 64], BLKS[:, s_:s_ + 1, :].to_broadcast([128, 16, 64]), ALU.mult, r=['sg', 'cs2'], w=['cms'])
                            for h in range(4):
                                for dc in range(4):
                                    mm(psum[0:64, 4 + h, 0:NMEM], CMS[:, h * 4 + dc, :], KTS[:, h * 4 + dc, :], s_ == 0 and dc == 0, s_ == NS - 1 and dc == 3,
                                       r=['cms', 'kts'], w=pk(4 + h))
                        softmax_rows(64)
                        probs_T(64)
                        for s_ in range(NS):
                            dma('pool', KSB[:], cv[s_].rearrange("(m p) d -> p m d", p=128), w=['ksb'], key='ksb')
                            for dc in range(16):
                                bb = 4 + dc // 8
                                for m_ in range(2):
                                    first = (s_ == 0 and dc % 8 == 0 and m_ == 0)
                                    mm(psum[:, bb, (dc % 8) * 64 + s_ * 4:(dc % 8) * 64 + s_ * 4 + 4], KSB[:, m_, dc * 128:(dc + 1) * 128],
                                       PT[:, (dc // 4) * 2 + m_, s_ * 4:s_ * 4 + 4], first, False, r=['ksb', 'ptt'], w=pk(bb))
                        for q in range(2):
                            ACT(OC[:, q * 8:(q + 1) * 8, tc:tc + 4 * NS], psum[:, 4 + q, :].rearrange("p (a t) -> p a t", t=64)[:, :, 0:4 * NS], AF.Copy,
                                r=pk(4 + q), w=['u_fm'])
                        if 4 * NS < 128:
                            MS(OC[:, :, tc + 4 * NS:tc + 128], 0.0, w=['u_fm'])
                dbg('oc', OC[:, :, 0:Tn], ['u_fm'])
                chk('oc')
                for cb_ in range(4):
                    def epi_co(t, b, cb_=cb_):
                        hv = H[:, t, cb_ * 512:(cb_ + 1) * 512]
                        TT(hv, hv, psum[:, b, :], ALU.add, r=[('H', t)] + pk(b), w=[('H', t)])
                    proj_tm(OC, 'u_fm', range(nt), w_co, cb_ * 512, 512, epi_co)
                dbg('h2', H[:, 0:nt, :], [('H', t) for t in range(nt)])
                chk('h2')

                for ti in range(nt):
                    rms_to_fm(H[:, ti, :], ('H', ti), 3, A_fm, 'a_fm', ti * 128)
                for cb_ in range(11):
                    sa = load_w(w_up, 0, 16, cb_ * 512, 512)
                    sg_ = load_w(w_up, 0, 16, FFN + cb_ * 512, 512)
                    if has_s:
                        dma('sp', FST[0:NS * 2, 0:512], st_fc[:, cb_ * 512:(cb_ + 1) * 512], w=['fst'], key='xin')
                        dma('sp', FST[0:NS * 2, 512:1024], st_fc[:, FFN + cb_ * 512:FFN + (cb_ + 1) * 512], w=['fst'], key='xin')
                    for ci in range(4):
                        c = cb_ * 4 + ci
                        ba = PS(); bg_ = PS()
                        for kc in range(16):
                            mm(psum[:, ba, 0:Tn], WS[sa][:, kc, ci * 128:(ci + 1) * 128], A_fm[:, kc, 0:Tn], kc == 0, kc == 15, r=['a_fm', ('W', sa)], w=pk(ba))
                        for kc in range(16):
                            mm(psum[:, bg_, 0:Tn], WS[sg_][:, kc, ci * 128:(ci + 1) * 128], A_fm[:, kc, 0:Tn], kc == 0, kc == 15, r=['a_fm', ('W', sg_)], w=pk(bg_))
                        for half, (b, cc) in enumerate([(ba, c), (bg_, 44 + c)]):
                            conv_chunk(b, cc, RAW2[:, half, :], ('raw2', half), ACC2[:, half, :], ('acc2', half), FHALO, FW, 3, None)
                            if has_s:
                                conv_chunk_s(b, cc, (RAWS if half == 0 else RAWS2), ('raws' if half == 0 else 'raws2'), FST, ['fst'], half * 512 + ci * 128,
                                             ACC2[:, half, :], ('acc2', half), FW, 3)
                        nact = Tp + (64 if has_s else 0)
                        ACT(ACC2[:, 1, 0:nact], ACC2[:, 1, 0:nact], AF.Silu, r=[('acc2', 1), 'par'], w=[('acc2', 1)], bias=FB[:, 44 + c:45 + c])
                        STT(ACTF[:, c, 0:nact], ACC2[:, 0, 0:nact], FB[:, c:c + 1], ACC2[:, 1, 0:nact], ALU.add, ALU.mult,
                            r=[('acc2', 0), ('acc2', 1), 'par'], w=['actf'])
                        if has_s:
                            MS(ACTF[:, c, Tp + 64:Tp + 128], 0.0, w=['actf'])
                    for ti in flagged:
                        for half, s in enumerate([sa, sg_]):
                            b = PS()
                            for kc in range(16):
                                mm(psum[:, b, :], A_fm[:, kc, ti * 128:(ti + 1) * 128], WS[s][:, kc, :], kc == 0, kc == 15, r=['a_fm', ('W', s)], w=pk(b))
                            ACT(OST[:], psum[:, b, :], AF.Copy, r=pk(b), w=['ost'])
                            col0 = half * FFN + cb_ * 512
                            if tiles[ti][0] == 'P':
                                dma('sp', o_pfc[:, col0:col0 + 512], OST[126:128, :], r=['ost'], key='o1')
                            else:
                                for r_ in range(2, 4):
                                    dma('sp', o_sfc[:, r_ - 2, col0:col0 + 512], OST[r_:4 * NS:4, :], r=['ost'], key='o1')
                dbg('actf', ACTF[:, :, 0:Tn], ['actf'])
                chk('actf')
                assert nt <= 4
                for cb_ in range(4):
                    for kg, (k0, nk) in enumerate([(0, 16), (16, 16), (32, 12)]):
                        s = load_w(w_down, k0 * 128, nk, cb_ * 512, 512)
                        for t in range(nt):
                            for kc in range(nk):
                                mm(psum[:, 4 + t, :], ACTF[:, k0 + kc, t * 128:(t + 1) * 128], WS[s][:, kc, :], kg == 0 and kc == 0, kg == 2 and kc == nk - 1,
                                   r=['actf', ('W', s)], w=pk(4 + t))
                    for t in range(nt):
                        hv = H[:, t, cb_ * 512:(cb_ + 1) * 512]
                        TT(hv, hv, psum[:, 4 + t, :], ALU.add, r=[('H', t)] + pk(4 + t), w=[('H', t)])
                for ti, (kind, i) in enumerate(tiles):
                    norm_rstd(H[:, ti, :], [('H', ti)], D, 0)
                    dma('sp', T2[:], vec["norm_final"].partition_broadcast(128), w=['t2'])
                    STT(T1[:], H[:, ti, :], SS[:, 0:1], T2[:], ALU.mult, ALU.mult, r=[('H', ti), 'ss', 't2'], w=['t1'])
                    dst = y_p[i * 128:(i + 1) * 128, :] if kind == 'P' else y_s[:, :]
                    dma('sp', dst, T1[:], r=['t1'], key='o2')

            for q in range(4):
                b = PS()
                for j in range(4):
                    tr(psum[:, b, j * 128:(j + 1) * 128], ST[:, (q * 4 + j) * 128:(q * 4 + j + 1) * 128], identf, r=['st', 'cst'], w=pk(b))
                ACT(SQ[:, q * 4:(q + 1) * 4, :], psum[:, b, :].rearrange("p (a b) -> p a b", b=128), AF.Copy, r=pk(b), w=['sq'])
            dma('sp', o_pssd.rearrange("(t p) n -> p t n", p=128), SQ[:], r=['sq'], key='sq')
            dma('sp', o_pgla.rearrange("h (kc p) v -> p (h kc) v", p=128), GS[:], r=['gs'], key='o2')


        except _Stop:
            pass

        sems = {}
        for e_ in ENGS:
            sems[e_] = es.enter_context(nc.semaphore("sem_" + e_))
        dma_sems = {}
        for i_, k in enumerate(dsem_keys):
            dma_sems[k] = es.enter_context(nc.semaphore(f"dsem{i_}"))
        P.finalize(sems, dma_sems)
        if dbg_names or stop:
            print('STATS', P.stats)
        block = es.enter_context(nc.Block())

        @block.tensor
        def _(e):
            P.run('pe', e)

        @block.vector
        def _(e):
            P.run('dve', e)

        @block.scalar
        def _(e):
            P.run('act', e)

        @block.gpsimd
        def _(e):
            P.run('pool', e)

        @block.sync
        def _(e):
            P.run('sp', e)
    return nc, din_, dout_, dbg_out


def make_consts():
    t = np.arange(128)
    c = np.zeros((128, 1024), np.float32)
    c[:, 0:128] = np.eye(128)
    c[:, 128:256] = (t[:, None] <= t[None, :])
    c[:, 256:384] = (t[:, None] > t[None, :])
    c[:, 384:512] = 1.0
    valid = (t[None, :] >= t[:, None])
    c[:, 512:640] = np.where(valid, 0.0, -30000.0)
    c[:, 640:768] = valid
    c2 = np.zeros((128, 2048), np.float32)
    same = (t[:, None] // 4) == (t[None, :] // 4)
    c2[:, 0:128] = (t[:, None] <= t[None, :]) & same
    c2[:, 128:256] = (t[:, None] > t[None, :]) & same
    c2[:, 256:384] = same
    c2[:, 384:512] = np.where(valid & same, 0.0, -30000.0)
    c2[:, 512:640] = valid & same
    c2[:, 640:656] = (t[:, None] // 4) == np.arange(16)[None, :]
    blks = ((np.arange(64)[None, :] // 4) == np.arange(16)[:, None]).astype(np.float32)
    c2[:, 1024:2048] = blks.reshape(1, 1024)
    return c, c2


WEIGHT_NAMES = ["w_in", "w_ssd_out", "w_gla_out", "w_mix_out", "w_cq", "w_ck", "w_cv", "w_co", "w_up", "w_down", "w_gla_gate",
                "norm_final", "b_gla_gate"]


def make_params(inp):
    f = lambda a: np.asarray(a, dtype=np.float32)
    p = np.zeros((128, 672), np.float32)
    for gi, n in enumerate(["norm_mix", "norm_cross", "norm_mem", "norm_ffn", "ssd_norm"]):
        p[:, gi * 16:(gi + 1) * 16] = f(inp[n]).reshape(16, 128).T
    p[:, 80:84] = f(inp["gla_norm"]).reshape(4, 128).T
    p[:, 96:192] = f(inp["ssd_conv_w"]).reshape(4, 24, 128).transpose(2, 1, 0).reshape(128, 96)
    p[:, 192:216] = f(inp["ssd_conv_b"]).reshape(24, 128).T
    p[:, 216:480] = f(inp["ffn_conv_w"]).reshape(3, 88, 128).transpose(2, 1, 0).reshape(128, 264)
    p[:, 480:568] = f(inp["ffn_conv_b"]).reshape(88, 128).T
    p[:, 568:600] = f(inp["ssd_dt_bias"])[None, :]
    p[:, 600:632] = f(inp["ssd_A_log"])[None, :]
    p[:, 632:664] = f(inp["ssd_D"])[None, :]
    return p


TPP_FULL = 2


def kernel(**inp):
    f = lambda a: np.ascontiguousarray(np.asarray(a, dtype=np.float32))
    NB, L = inp["x_prompt"].shape[:2]
    NPT = L // 128
    NSB = inp["x_sample"].shape[0]
    ncores = 8
    NS = NSB // ncores
    nc, _, _, _ = build(NPT, NS, TPP_FULL)
    c1, c2 = make_consts()
    shared = {n: f(inp[n]) for n in WEIGHT_NAMES}
    shared["consts"] = c1; shared["consts2"] = c2; shared["params"] = make_params(inp)
    in_maps = []
    for c in range(ncores):
        b = c // 2
        m = dict(shared)
        m["x_p"] = f(inp["x_prompt"][b])
        xs = np.zeros((128, D), np.float32)
        xs[:NS * 4] = f(inp["x_sample"][c * NS:(c + 1) * NS]).reshape(NS * 4, D)
        m["x_s"] = xs
        m["mem_p"] = f(inp["mem_prompt"][b])
        sl = slice(c * NS, (c + 1) * NS)
        m["cache_k"] = f(inp["cache_mem_k"][sl]).reshape(NS, NMEM, D)
        m["cache_v"] = f(inp["cache_mem_v"][sl]).reshape(NS, NMEM, D)
        m["st_ssd_conv"] = f(inp["state_ssd_conv"][sl]).reshape(NS * 3, CONV_DIM)
        m["st_ssd"] = f(inp["state_ssd"][sl]).reshape(NS, D, 128)
        m["st_gla"] = f(inp["state_gla"][sl])
        m["st_ffn_conv"] = f(inp["state_ffn_conv"][sl]).reshape(NS * 2, 2 * FFN)
        in_maps.append(m)
    res = run_bass_kernel_spmd(nc, in_maps, core_ids=list(range(ncores)))
    R = res.results
    pc = [R[2 * b] for b in range(NB)]
    y_prompt = np.stack([r["y_p"] for r in pc]).reshape(NB, L, D)
    y_sample = np.concatenate([r["y_s"][:NS * 4].reshape(NS, 4, D) for r in R])
    p_ssd_conv = np.stack([r["p_ssd_conv"] for r in pc])
    p_ssd = np.stack([r["p_ssd"].reshape(32, 64, 128) for r in pc])
    p_gla = np.stack([r["p_gla"] for r in pc])
    p_ffn_conv = np.stack([r["p_ffn_conv"] for r in pc])
    p_mem_k = np.stack([r["p_mem_k"].reshape(NMEM, 4, 512) for r in pc])
    p_mem_v = np.stack([r["p_mem_v"].reshape(NMEM, 4, 512) for r in pc])
    s_ssd_conv = np.concatenate([r["s_ssd_conv"] for r in R])
    s_ssd = np.concatenate([r["s_ssd"].reshape(NS, 32, 64, 128) for r in R])
    s_gla = np.concatenate([r["s_gla"] for r in R])
    s_ffn_conv = np.concatenate([r["s_ffn_conv"] for r in R])
    outs = (y_prompt, y_sample, p_ssd_conv, p_ssd, p_gla, p_ffn_conv, p_mem_k, p_mem_v, s_ssd_conv, s_ssd, s_gla, s_ffn_conv)
    return tuple(np.ascontiguousarray(o, dtype=np.float32) for o in outs)
```

```python
import os
import numpy as np
from contextlib import ExitStack
import concourse.bass as bass
import concourse.mybir as mybir
from concourse.bass_utils import run_bass_kernel_spmd

F32 = mybir.dt.float32
BF16 = mybir.dt.bfloat16
AF = mybir.ActivationFunctionType
ALU = mybir.AluOpType

D = 2048
CONV_DIM = 3072
FFN = 5632
IN_DIM = 15408
EPS = 1e-6
NMEM = 256
C_Z, C_XBC, C_DT, C_Q, C_K, C_V, C_R, C_G, C_GA, C_GB = 0, 2048, 5120, 5152, 6176, 7200, 9248, 11296, 11312, 13360
ENGS = ['pe', 'dve', 'act', 'pool', 'sp']
SKIPGLA = bool(os.environ.get('SKIPGLA'))


class Prog:
    def __init__(self):
        self.ops = []
        self.last_w = {}
        self.readers = {}
        self.alias = {}
        self._exp = {}

    def expand(self, keys):
        out = []
        for k in keys:
            if k not in self._exp:
                kk = k if k in self.alias else (k[0] if isinstance(k, tuple) and k[0] in self.alias else None)
                if kk is None:
                    self._exp[k] = [k]
                else:
                    lo, hi = self.alias[kk]
                    if k not in self.alias:
                        self.alias[k] = (lo, hi)
                        self._exp = {k0: v for k0, v in self._exp.items() if False}
                    self._exp[k] = [k] + [k2 for k2, (l2, h2) in self.alias.items() if k2 != k and l2 < hi and lo < h2]
            out.extend(self._exp[k])
        return out

    def op(self, eng, fn, r=(), w=(), dma=None):
        oid = len(self.ops)
        deps = set()
        r = self.expand(r)
        w = self.expand(w)
        for b in r:
            if b in self.last_w:
                deps.add(self.last_w[b])
            if isinstance(b, tuple) and b[0] == 'ps':
                for x in self.readers.get(b, ()):
                    if self.ops[x]['eng'] != eng:
                        deps.add(x)
        for b in w:
            if b in self.last_w:
                deps.add(self.last_w[b])
            for x in self.readers.get(b, ()):
                deps.add(x)
        for b in r:
            self.readers.setdefault(b, []).append(oid)
        for b in w:
            self.last_w[b] = oid
            self.readers[b] = []
        self.ops.append(dict(eng=eng, fn=fn, deps=deps, dma=dma, marked=False))
        return oid

    def finalize(self, sems, dma_sems):
        ops = self.ops
        for o in ops:
            best = {}
            cd = []
            for d in o['deps']:
                od = ops[d]
                if od['dma'] is not None:
                    cd.append(d)
                    continue
                if od['eng'] == 'pe' and o['eng'] == 'pe' and o['dma'] is None:
                    continue
                if od['eng'] not in best or d > best[od['eng']]:
                    best[od['eng']] = d
            for d in best.values():
                ops[d]['marked'] = True
                cd.append(d)
            o['cdeps'] = sorted(cd)
        cnt = {e: 0 for e in ENGS}
        dcnt = {}
        for o in ops:
            if o['dma'] is not None:
                dcnt[o['dma']] = dcnt.get(o['dma'], 0) + 16
                o['sig'] = (dma_sems[o['dma']], dcnt[o['dma']])
            elif o['marked']:
                cnt[o['eng']] += 1
                o['sig'] = (sems[o['eng']], cnt[o['eng']])
        self.per = {e: [o for o in ops if o['eng'] == e] for e in ENGS}
        self.stats = dict(cnt=cnt, dcnt={str(k): v for k, v in dcnt.items()}, nops={e: len(self.per[e]) for e in ENGS})
        self.final_waits = [(dma_sems[k], v) for k, v in dcnt.items()]

    def run(self, engname, e):
        ops = self.ops
        waited = {}
        for o in self.per[engname]:
            for d in o['cdeps']:
                od = ops[d]
                if 'sig' not in od:
                    continue
                sem, val = od['sig']
                key = id(sem)
                if waited.get(key, 0) >= val:
                    continue
                waited[key] = val
                e.wait_ge(sem, val)
            ins = o['fn'](e)
            if 'sig' in o:
                ins.then_inc(o['sig'][0], 16 if o['dma'] is not None else 1)
        if engname == 'sp':
            for sem, val in self.final_waits:
                e.wait_ge(sem, val)


class _Stop(Exception):
    pass


def build(NPT, NS, TPP, dbg_names=(), stop=None, NCTX=0):
    nc = bass.Bass("TRN2", target_bir_lowering=False)
    din_ = {}
    dout_ = {}

    def din(name, shape):
        din_[name] = nc.dram_tensor(name, list(shape), F32, kind="ExternalInput").ap()
        return din_[name]

    def dout(name, shape):
        dout_[name] = nc.dram_tensor(name, list(shape), F32, kind="ExternalOutput").ap()
        return dout_[name]

    x_p = din("x_p", [NPT * 128, D]); x_s = din("x_s", [128, D]); flag_d = din("flag", [128, 1])
    mem_p = din("mem_p", [NMEM, D])
    ck = din("cache_k", [NS, NMEM, D]); cv = din("cache_v", [NS, NMEM, D])
    st_sc = din("st_ssd_conv", [NS * 3, CONV_DIM]); st_ssd = din("st_ssd", [NS, D, 128])
    st_gla = din("st_gla", [NS, 4, 256, 512]); st_fc = din("st_ffn_conv", [NS * 2, 2 * FFN])
    consts = din("consts", [128, 1024]); consts2 = din("consts2", [128, 2048]); params = din("params", [128, 672])
    w_in = din("w_in", [D, IN_DIM]); w_ssd_out = din("w_ssd_out", [D, D]); w_gla_out = din("w_gla_out", [D, D])
    w_mix = din("w_mix_out", [D, D]); w_cq = din("w_cq", [D, D]); w_ck = din("w_ck", [D, D])
    w_cv = din("w_cv", [D, D]); w_co = din("w_co", [D, D]); w_up = din("w_up", [D, 2 * FFN])
    w_down = din("w_down", [FFN, D]); w_gate = din("w_gla_gate", [16, 1024])
    vec = {}
    for n, sz_ in [("norm_final", D), ("b_gla_gate", 1024)]:
        vec[n] = din(n, [sz_])

    y_p = dout("y_p", [(NPT - NCTX) * 128, D]); y_s = dout("y_s", [128, D])
    o_psc = dout("p_ssd_conv", [3, CONV_DIM]); o_pssd = dout("p_ssd", [D, 128]); o_pgla = dout("p_gla", [4, 256, 512])
    o_pfc = dout("p_ffn_conv", [2, 2 * FFN]); o_pmk = dout("p_mem_k", [NMEM, D]); o_pmv = dout("p_mem_v", [NMEM, D])
    o_ssc = dout("s_ssd_conv", [NS, 3, CONV_DIM]); o_sssd = dout("s_ssd", [NS, D, 128])
    o_sgla = dout("s_gla", [NS, 4, 256, 512]); o_sfc = dout("s_ffn_conv", [NS, 2, 2 * FFN])

    P = Prog()
    es = ExitStack()
    with es:
        es.enter_context(nc.allow_non_contiguous_dma(reason="small parameter loads"))

        def sb(name, shape, dt=F32):
            return es.enter_context(nc.sbuf_tensor(name, list(shape), dt))
        psum = es.enter_context(nc.psum_tensor("psum", [128, 8, 512], F32))
        NT = TPP
        T = NT * 128
        assert T >= 256

        dsem_keys = []

        def dma(eng, out, in_, r=(), w=(), key=None):
            key = ('d', (list(w) + list(r))[0])
            if key not in dsem_keys:
                dsem_keys.append(key)
            return P.op(eng, lambda e: e.dma_start(out=out, in_=in_), r=r, w=w, dma=key)

        def ACT(out, in_, func, r, w, **kw):
            P.op('act', lambda e: e.activation(out=out, in_=in_, func=func, **kw), r=r, w=w)

        def TT(out, in0, in1, op, r, w):
            P.op('dve', lambda e: e.tensor_tensor(out=out, in0=in0, in1=in1, op=op), r=r, w=w)

        def TS(out, in0, s1, s2, op0, op1, r, w):
            if s2 is None:
                P.op('dve', lambda e: e.tensor_scalar(out=out, in0=in0, scalar1=s1, scalar2=None, op0=op0), r=r, w=w)
            else:
                P.op('dve', lambda e: e.tensor_scalar(out=out, in0=in0, scalar1=s1, scalar2=s2, op0=op0, op1=op1), r=r, w=w)

        def STT(out, in0, scalar, in1, op0, op1, r, w):
            P.op('dve', lambda e: e.scalar_tensor_tensor(out=out, in0=in0, scalar=scalar, in1=in1, op0=op0, op1=op1), r=r, w=w)

        def CP(out, in_, r, w):
            P.op('dve', lambda e: e.tensor_copy(out=out, in_=in_), r=r, w=w)

        def MS(ap, val, w):
            P.op('dve', lambda e: e.memset(ap, val), w=w)

        def mm(out, lhsT, rhs, start, stop, r, w):
            P.op('pe', lambda e: e.matmul(out, lhsT=lhsT, rhs=rhs, start=start, stop=stop, skip_group_check=True), r=r, w=w)

        def tr(out, in_, ident, r, w):
            P.op('pe', lambda e: e.transpose(out, in_, ident), r=r, w=w)

        psn = [0]

        def PS():
            b = psn[0]
            psn[0] = (psn[0] + 1) % 4
            return b

        def pk(b):
            return [('ps', b)]

        CST = sb("cst", [128, 1024])
        identf = CST[:, 0:128]; U_P = CST[:, 128:256]; ONE = CST[:, 384:512]
        L_P = CST[:, 256:384]; MN_P = CST[:, 512:640]; M01_P = CST[:, 640:768]
        CS2 = sb("cs2", [128, 2048])
        U_S = CS2[:, 0:128]; L_S = CS2[:, 128:256]; BLK_S = CS2[:, 256:384]; MN_S = CS2[:, 384:512]
        M01_S = CS2[:, 512:640]; BLKC = CS2[:, 640:656]
        BLKS = CS2[:, 1024:2048].rearrange("p (s t) -> p s t", t=64)
        identb = sb("identb", [128, 128], BF16)
        m01b_p = sb("m01bp", [128, 128], BF16); m01b_s = sb("m01bs", [128, 128], BF16)
        PAR = sb("par", [128, 672])
        GF = PAR[:, 0:96].rearrange("p (g c) -> p g c", c=16)
        CW = PAR[:, 96:192].rearrange("p (c k) -> p c k", k=4); CB = PAR[:, 192:216]
        FW = PAR[:, 216:480].rearrange("p (c k) -> p c k", k=3); FB = PAR[:, 480:568]
        DTB = PAR[:, 568:600]; AB = sb("ab", [128, 32]); DB = PAR[:, 632:664]
        WG = sb("wg", [32, 1024], BF16)
        dma('sp', CST[:], consts[:, :], w=['cst'])
        dma('sp', CS2[:], consts2[:, :], w=['cs2'])
        dma('sp', PAR[:], params[:, :], w=['par'])
        FLAG = sb("flag_sb", [128, 1])
        dma('sp', FLAG[:], flag_d[:, :], w=['flag'])
        dma('pool', WG[0:16, :], w_gate[:, :], w=['wg'])
        dma('pool', WG[16:17, :], vec["b_gla_gate"].rearrange("(o n) -> o n", o=1), w=['wg'])
        ACT(AB[:], PAR[:, 600:632], AF.Exp, r=['par'], w=['ab'])
        TS(AB[:], AB[:], -1.0, None, ALU.mult, None, r=['ab'], w=['ab'])
        CP(identb[:], identf, r=['cst'], w=['identb'])
        CP(m01b_p[:], M01_P, r=['cst'], w=['m01b'])
        CP(m01b_s[:], M01_S, r=['cs2'], w=['m01b'])
        KC = {'P': dict(U=U_P, L=L_P, BLK=ONE, MN=MN_P, M01=m01b_p),
              'S': dict(U=U_S, L=L_S, BLK=BLK_S, MN=MN_S, M01=m01b_s)}
        CK = ['cst', 'cs2']

        NW = 2
        WS = [sb(f"ws{i}", [128, 16, 512], BF16) for i in range(NW)]
        A_fm = sb("A_fm", [128, 16, T], BF16)
        U_fm = sb("u_fm", [128, 16, T], BF16)
        MRG = sb("mrg", [128, 16, T], BF16)
        ST = sb("st", [128, D]); GS = sb("gs", [128, 8, 512])
        KFM = sb("kfm", [128, 16, NMEM], BF16); VTM = sb("vtm", [128, 2, D], BF16)
        HALO = sb("halo", [128, 24, 3]); FHALO = sb("fhalo", [128, 88, 2])
        SS = sb("ss", [128, 8]); TMP32 = sb("tmp32", [128, 8, 32])
        RH = NT * D * 4
        R1 = 8192 * 2 + 4096 + 2048
        R2 = max(NT * (4096 + 6144 + 128), NT * 16 * 1024 + NT * 256, NT * 11 * 1024 + NT * 2048 + 32 + 4096) + 64
        R3 = 21504
        ARENA = sb("arena", [128, (RH + R1 + R2 + R3) // 4])
        class Carver:
            def __init__(self, base, size):
                self.base = base; self.size = size; self.off = 0
            def reset(self):
                self.off = 0
            def get(self, shape, dt=F32, key=None):
                esz = 4 if dt == F32 else 2
                n = 1
                for x in shape[1:]:
                    n *= x
                nb = (n * esz + 31) // 32 * 32
                assert self.off + nb <= self.size, (self.off, nb, self.size, shape)
                o = (self.base + self.off) // 4
                if key is not None:
                    P.alias[key] = (self.base + self.off, self.base + self.off + nb)
                ap = ARENA[0:shape[0], o:o + nb // 4]
                self.off += nb
                if dt != F32:
                    ap = ap.bitcast(dt)
                ap = ap[:, 0:n]
                if len(shape) == 3:
                    ap = ap.rearrange("p (a b) -> p a b", b=shape[2])
                elif len(shape) == 4:
                    ap = ap.rearrange("p (a b c) -> p a b c", b=shape[2], c=shape[3])
                return ap
        cH = Carver(0, RH); c1 = Carver(RH, R1); c2 = Carver(RH + R1, R2); c3 = Carver(RH + R1 + R2, R3)
        H = cH.get([128, NT, D])
        for t_ in range(NT):
            P.alias[('H', t_)] = (t_ * D * 4, (t_ + 1) * D * 4)
        cH.reset()
        SQ = cH.get([128, 16, 128], key='sq'); SQB = cH.get([128, D], BF16, key='sqb'); XWM = cH.get([128, D], BF16, key='xwm')
        T1 = c1.get([128, D], key='t1'); T2 = c1.get([128, D], key='t2'); XNT = c1.get([128, D], BF16, key='xnt')
        CMS = c1.get([128, 16, 64], BF16, key='cms')
        SZ = c2.get([128, NT, D], BF16, key='sz'); XBC = c2.get([128, 24, T], BF16, key='xbc'); DT = c2.get([128, NT, 32], key='dt')
        c2.reset()
        SG = c2.get([128, 16, T], BF16, key='sg'); QF = c2.get([128, 8, T], BF16, key='qf'); KF = c2.get([128, 8, T], BF16, key='kf')
        VT = c2.get([128, NT, D], BF16, key='vt'); SR = c2.get([128, NT, D], BF16, key='sr')
        GLR = c2.get([32, T], BF16, key='glr')
        c2.reset()
        ACTF = c2.get([128, 44, T], BF16, key='actf'); RAW2 = c2.get([128, 2, 2 + T], key='raw2'); ACC2 = c2.get([128, 2, T], key='acc2'); FST = c2.get([32, 1024], key='fst')
        OST = c3.get([128, 512], key='ost'); RAW = c3.get([128, 3 + T], key='raw'); ACC = c3.get([128, T], key='acc')
        RAWS = c3.get([128, 16, 8], key='raws'); RAWS2 = c3.get([128, 16, 8], key='raws2')
        c3.reset()
        XS = c3.get([128, D], BF16, key='xs'); BT = c3.get([128, 512], BF16, key='bt'); UA = c3.get([128, 4, 128], key='ua'); DEC = c3.get([128, 2, 128], key='dec')
        CBT = c3.get([128, 4, 128], key='cbt'); MT = c3.get([128, 2, 4, 128], BF16, key='mt'); STB = c3.get([128, D], BF16, key='stb'); XW = c3.get([128, D], BF16, key='xw')
        ELS = c3.get([128, 16, 16], key='els')
        c3.reset()
        G = c3.get([128, 1024], key='g'); GT1 = T1[:, 0:1024]; GT2 = T2[:, 0:1024]; QE = c3.get([128, 8, 128], BF16, key='qe')
        KE = c3.get([128, 8, 128], BF16, key='ke'); EBT = c3.get([128, 2, 128], key='ebt'); ATT = c3.get([128, 4, 128], BF16, key='att'); KW = c3.get([128, 1024], BF16, key='kw')
        GSB = c3.get([128, 8, 512], BF16, key='gsb'); ELG = c3.get([128, 8, 16], key='elg')
        c3.reset()
        PSM = T1[:, 0:4 * NMEM].rearrange("p (h m) -> p h m", m=NMEM); PB = c3.get([128, 4, NMEM], BF16, key='pb'); PT = c3.get([128, 8, 128], BF16, key='ptt')
        KSB = c3.get([128, 2, D], BF16, key='ksb'); KTS = c3.get([128, 16, NMEM], BF16, key='kts')
        TMPG = c3.get([128, T], key='tmpg')

        wslot = [0]

        def load_w(wd, r0, nkc, c0, ncols):
            s = wslot[0] % NW
            wslot[0] += 1
            src = wd[r0:r0 + nkc * 128, c0:c0 + ncols].rearrange("(kc p) n -> p kc n", p=128)
            dma('pool', WS[s][:, 0:nkc, 0:ncols], src, w=[('W', s)], key=('w', s))
            return s

        def proj_tm(lhs, lkey, tlist, wd, c0, ncols, epi, nkc=16, r0=0, slot=None):
            s = load_w(wd, r0, nkc, c0, ncols) if slot is None else slot
            for t in tlist:
                b = PS()
                for kc in range(nkc):
                    mm(psum[:, b, 0:ncols], lhs[:, kc, t * 128:(t + 1) * 128], WS[s][:, kc, 0:ncols], kc == 0, kc == nkc - 1,
                       r=[lkey, ('W', s)], w=pk(b))
                epi(t, b)
            return s

        def proj_fm(rhs, rkey, Tn, wd, c0, nch, epi, nkc=16, slot=None):
            s = load_w(wd, 0, nkc, c0, nch * 128) if slot is None else slot
            for ci in range(nch):
                b = PS()
                for kc in range(nkc):
                    mm(psum[:, b, 0:Tn], WS[s][:, kc, ci * 128:(ci + 1) * 128], rhs[:, kc, 0:Tn], kc == 0, kc == nkc - 1,
                       r=[rkey, ('W', s)], w=pk(b))
                epi(ci, b)
            return s

        def rstd_from_ss(col, n=1):
            TS(SS[:, col:col + n], SS[:, col:col + n], EPS, None, ALU.add, None, r=['ss'], w=['ss'])
            ACT(SS[:, col:col + n], SS[:, col:col + n], AF.Ln, r=['ss'], w=['ss'])
            ACT(SS[:, col:col + n], SS[:, col:col + n], AF.Exp, r=['ss'], w=['ss'], scale=-0.5)

        def norm_rstd(src, skey, n, col):
            ACT(T2[:, 0:n], src, AF.Square, r=skey, w=['t2', 'ss'], scale=float(n) ** -0.5, accum_out=SS[:, col:col + 1])
            rstd_from_ss(col)

        def to_fm(src_bf, skey, dst, dkey, tcol, gain, nk=16):
            for g4 in range(0, nk, 4):
                b = PS()
                pv = psum[:, b, :].bitcast(BF16)
                for j in range(4):
                    tr(pv[:, j * 128:(j + 1) * 128], src_bf[:, (g4 + j) * 128:(g4 + j + 1) * 128], identb[:], r=[skey, 'identb'], w=pk(b))
                pin = pv[:, 0:512].rearrange("p (a b) -> p a b", b=128)
                if gain is None:
                    ACT(dst[:, g4:g4 + 4, tcol:tcol + 128], pin, AF.Copy, r=pk(b), w=[dkey])
                else:
                    TT(dst[:, g4:g4 + 4, tcol:tcol + 128], pin, gain(g4).unsqueeze(2).to_broadcast([128, 4, 128]), ALU.mult,
                       r=pk(b) + ['par'], w=[dkey])

        def rms_to_fm(src, skey, gi, dst, dkey, tcol):
            norm_rstd(src, [skey], D, 0)
            ACT(XNT[:], src, AF.Copy, r=[skey, 'ss'], w=['xnt'], scale=SS[:, 0:1])
            to_fm(XNT, 'xnt', dst, dkey, tcol, lambda g4: GF[:, gi, g4:g4 + 4])

        dbg_out = {}

        def dbg(name, ap, keys):
            if name not in dbg_names:
                return
            shp = list(ap.shape)
            o = nc.dram_tensor("dbg_" + name, shp, F32, kind="ExternalOutput").ap()
            dbg_out[name] = o
            dma('pool', o, ap, r=keys, key='dbg')

        pass_idx = [0]

        def chk(name):
            if stop == name or stop == f"{name}@{pass_idx[0]}":
                raise _Stop()

        try:
            MNF = U_fm
            for mt_ in range(2):
                dma('sp', H[:, 0, :], mem_p[mt_ * 128:(mt_ + 1) * 128, :], w=[('H', 0)], key='xin')
                rms_to_fm(H[:, 0, :], ('H', 0), 2, MNF, 'u_fm', mt_ * 128)
            for cb_ in range(4):
                def epi_k(ci, b, cb_=cb_):
                    ACT(KFM[:, cb_ * 4 + ci, :], psum[:, b, 0:NMEM], AF.Copy, r=pk(b), w=['kfm'])
                s = proj_fm(MNF, 'u_fm', NMEM, w_ck, cb_ * 512, 4, epi_k)

                def epi_kt(t, b, cb_=cb_):
                    ACT(OST[:], psum[:, b, :], AF.Copy, r=pk(b), w=['ost'])
                    dma('sp', o_pmk[t * 128:(t + 1) * 128, cb_ * 512:(cb_ + 1) * 512], OST[:], r=['ost'], key='o1')
                proj_tm(MNF, 'u_fm', range(2), w_ck, cb_ * 512, 512, epi_kt, slot=s)
            for cb_ in range(4):
                def epi_vt(t, b, cb_=cb_):
                    ACT(OST[:], psum[:, b, :], AF.Copy, r=pk(b), w=['ost'])
                    CP(VTM[:, t, cb_ * 512:(cb_ + 1) * 512], OST[:], r=['ost'], w=['vtm'])
                    dma('sp', o_pmv[t * 128:(t + 1) * 128, cb_ * 512:(cb_ + 1) * 512], OST[:], r=['ost'], key='o1')
                proj_tm(MNF, 'u_fm', range(2), w_cv, cb_ * 512, 512, epi_vt)

            MS(ST[:], 0.0, w=['st'])
            MS(GS[:], 0.0, w=['gs'])
            MS(HALO[:], 0.0, w=['halo']); MS(FHALO[:], 0.0, w=['fhalo'])
            chk('memkv')

            tiles_all = ([('C', i) for i in range(NCTX - 1)] + [('H', NCTX - 1)] if NCTX > 0 else []) + \
                [('P', i) for i in range(NCTX, NPT)] + ([('S', 0)] if NS > 0 else [])
            passes = [tiles_all[i:i + NT] for i in range(0, len(tiles_all), NT)]

            for pi_, tiles in enumerate(passes):
                pass_idx[0] = pi_
                nt = len(tiles)
                Tn = nt * 128
                has_s = tiles[-1][0] == 'S'
                np_t = nt - (1 if has_s else 0)
                Tp = np_t * 128
                flagged = [ti for ti, (k, i) in enumerate(tiles) if k == 'S' or i == NPT - 1]
                full = [ti for ti, (k, i) in enumerate(tiles) if k != 'C']
                has_h = any(k == 'H' for k, _ in tiles)
                for ti, (k, i) in enumerate(tiles):
                    src = x_p[i * 128:(i + 1) * 128, :] if k != 'S' else x_s[:, :]
                    dma('sp', H[:, ti, :], src, w=[('H', ti)], key='xin')
                    rms_to_fm(H[:, ti, :], ('H', ti), 0, A_fm, 'a_fm', ti * 128)
                dbg('a_fm', A_fm[:, :, 0:Tn], ['a_fm'])
                chk('a_fm')

                for cb_ in range(4):
                    def epi_z(t, b, cb_=cb_):
                        ACT(SZ[:, t, cb_ * 512:(cb_ + 1) * 512], psum[:, b, :], AF.Silu, r=pk(b), w=[('sz', t)])
                    if full:
                        proj_tm(A_fm, 'a_fm', full, w_in, C_Z + cb_ * 512, 512, epi_z)
                if has_s:
                    dma('sp', T1[0:NS * 3, :], st_sc[:, 0:D], w=['t1'], key='xin')
                    dma('sp', T2[0:NS * 3, 0:1024], st_sc[:, D:CONV_DIM], w=['t2'], key='xin')

                def conv_chunk(b, c, RAWb, rk, ACCb, ak, HAL, W, ntap, dst_fn):
                    hl = ntap - 1
                    if np_t > 0:
                        ACT(RAWb[:, hl:hl + Tp], psum[:, b, 0:Tp], AF.Copy, r=pk(b), w=[rk])
                        ACT(RAWb[:, 0:hl], HAL[:, c, :], AF.Copy, r=['halo'], w=[rk])
                        TS(ACCb[:, 0:Tp], RAWb[:, hl:hl + Tp], W[:, c, hl:hl + 1], None, ALU.mult, None, r=[rk, 'par', 'par'], w=[ak])
                        for k_ in range(hl):
                            STT(ACCb[:, 0:Tp], RAWb[:, k_:k_ + Tp], W[:, c, k_:k_ + 1], ACCb[:, 0:Tp], ALU.mult, ALU.add, r=[rk, 'par', 'par', ak], w=[ak])
                        ACT(HAL[:, c, :], RAWb[:, Tp:Tp + hl], AF.Copy, r=[rk], w=['halo'])

                def conv_chunk_s(b, c, RS, rsk, stT, stkeys, ccol, ACCb, ak, W, ntap):
                    hl = ntap - 1
                    b2 = PS()
                    tr(psum[:, b2, 0:NS * hl], stT[0:NS * hl, ccol:ccol + 128], identf[0:NS * hl, 0:NS * hl], r=stkeys + ['cst'], w=pk(b2))
                    ACT(RS[:, 0:NS, 0:hl], psum[:, b2, 0:NS * hl].rearrange("p (s r) -> p s r", r=hl), AF.Copy, r=pk(b2), w=[rsk])
                    ACT(RS[:, :, hl:hl + 4], psum[:, b, Tp:Tp + 64].rearrange("p (s r) -> p s r", r=4), AF.Copy, r=pk(b), w=[rsk])
                    accv = ACCb[:, Tp:Tp + 64].rearrange("p (s r) -> p s r", r=4)
                    TS(accv, RS[:, :, hl:hl + 4], W[:, c, hl:hl + 1], None, ALU.mult, None, r=[rsk, 'par', 'par'], w=[ak])
                    for k_ in range(hl):
                        STT(accv, RS[:, :, k_:k_ + 4], W[:, c, k_:k_ + 1], accv, ALU.mult, ALU.add, r=[rsk, 'par', 'par', ak], w=[ak])

                if has_s:
                    MS(RAWS[:], 0.0, w=['raws']); MS(RAWS2[:], 0.0, w=['raws2'])
                for cb_ in range(6):
                    def epi_x(ci, b, cb_=cb_):
                        c = cb_ * 4 + ci
                        conv_chunk(b, c, RAW, 'raw', ACC, 'acc', HALO, CW, 4, None)
                        if has_s:
                            stT = T1 if c < 16 else T2
                            conv_chunk_s(b, c, RAWS, 'raws', stT, ['t1', 't2'], (c % 16) * 128, ACC, 'acc', CW, 4)
                            MS(XBC[:, c, Tp + 64:Tp + 128], 0.0, w=['xbc'])
                        nact = Tp + (64 if has_s else 0)
                        ACT(XBC[:, c, 0:nact], ACC[:, 0:nact], AF.Silu, r=['acc', 'par'], w=['xbc'], bias=CB[:, c:c + 1])
                    s = proj_fm(A_fm, 'a_fm', Tn, w_in, C_XBC + cb_ * 512, 4, epi_x)
                    for ti in flagged:
                        b = PS()
                        for kc in range(16):
                            mm(psum[:, b, :], A_fm[:, kc, ti * 128:(ti + 1) * 128], WS[s][:, kc, :], kc == 0, kc == 15, r=['a_fm', ('W', s)], w=pk(b))
                        ACT(OST[:], psum[:, b, :], AF.Copy, r=pk(b), w=['ost'])
                        if tiles[ti][0] == 'P':
                            dma('sp', o_psc[:, cb_ * 512:(cb_ + 1) * 512], OST[125:128, :], r=['ost'], key='o1')
                        else:
                            for r_ in range(1, 4):
                                dma('sp', o_ssc[:, r_ - 1, cb_ * 512:(cb_ + 1) * 512], OST[r_:4 * NS:4, :], r=['ost'], key='o1')
                dbg('xbc', XBC[:, :, 0:Tn], ['xbc'])
                chk('xbc')

                def epi_dt(t, b):
                    x_ = TMP32[:, 0, :]; a_ = TMP32[:, 1, :]
                    TT(x_, psum[:, b, 0:32], DTB[:], ALU.add, r=pk(b) + ['par'], w=['tmpA'])
                    STT(a_, x_, -1.0, x_, ALU.mult, ALU.max, r=['tmpA'], w=['tmpB'])
                    ACT(a_, a_, AF.Exp, r=['tmpB'], w=['tmpB'], scale=-1.0)
                    ACT(a_, a_, AF.Ln, r=['tmpB'], w=['tmpB'], bias=1.0)
                    STT(DT[:, t, :], x_, 0.0, a_, ALU.max, ALU.add, r=['tmpA', 'tmpB'], w=['dt'])
                proj_tm(A_fm, 'a_fm', range(nt), w_in, C_DT, 32, epi_dt)
                dbg('dt', DT[:, 0:nt, :], ['dt'])
                chk('dt')

                ACT(STB[:], ST[:], AF.Copy, r=['st'], w=['stb'])
                for ti, (kind, i) in enumerate(tiles):
                    kc_ = KC['S' if kind == 'S' else 'P']
                    tc = ti * 128
                    for g4 in range(0, 16, 4):
                        b = PS(); pv = psum[:, b, :].bitcast(BF16)
                        for j in range(4):
                            tr(pv[:, j * 128:(j + 1) * 128], XBC[:, g4 + j, tc:tc + 128], identb[:], r=['xbc', 'identb'], w=pk(b))
                        ACT(XS[:, g4 * 128:(g4 + 4) * 128], pv[:, 0:512], AF.Copy, r=pk(b), w=['xs'])
                    b = PS(); pv = psum[:, b, :].bitcast(BF16)
                    for j in range(4):
                        tr(pv[:, j * 128:(j + 1) * 128], XBC[:, 16 + j, tc:tc + 128], identb[:], r=['xbc', 'identb'], w=pk(b))
                    ACT(BT[:], pv[:, 0:512], AF.Copy, r=pk(b), w=['bt'])
                    a_ = TMP32[:, 2, :]; acum = TMP32[:, 3, :]; nacum = TMP32[:, 4, :]; wgt = TMP32[:, 5, :]; elast = TMP32[:, 6, :]; eacum = TMP32[:, 7, :]
                    TT(a_, DT[:, ti, :], AB[:], ALU.mult, r=['dt', 'ab'], w=['tmp_a'])
                    b = PS()
                    mm(psum[:, b, 0:32], kc_['U'], a_, True, True, r=['tmp_a'] + CK, w=pk(b))
                    mm(psum[:, b, 32:64], kc_['BLK'], a_, True, True, r=['tmp_a'] + CK, w=pk(b))
                    CP(acum, psum[:, b, 0:32], r=pk(b), w=['tmp_ac'])
                    TS(nacum, psum[:, b, 0:32], -1.0, None, ALU.mult, None, r=pk(b), w=['tmp_nac'])
                    ACT(eacum, psum[:, b, 0:32], AF.Exp, r=pk(b), w=['tmp_eac'])
                    ACT(elast, psum[:, b, 32:64], AF.Exp, r=pk(b), w=['tmp_el'])
                    TT(wgt, psum[:, b, 32:64], acum, ALU.subtract, r=pk(b) + ['tmp_ac'], w=['tmp_w'])
                    ACT(wgt, wgt, AF.Exp, r=['tmp_w'], w=['tmp_w'])
                    TT(wgt, wgt, DT[:, ti, :], ALU.mult, r=['tmp_w', 'dt'], w=['tmp_w'])
                    if kind in ('C', 'H'):
                        TS(wgt, wgt, FLAG[:, 0:1], None, ALU.mult, None, r=['tmp_w', 'flag'], w=['tmp_w'])
                    TT(XW[:].rearrange("p (h d) -> p h d", d=64), XS[:].rearrange("p (h d) -> p h d", d=64),
                       wgt.unsqueeze(2).to_broadcast([128, 32, 64]), ALU.mult, r=['xs', 'tmp_w'], w=['xw'])

                    def ssd_state_update():
                        for g_ in range(4):
                            b = PS()
                            mm(psum[:, b, :], BT[:, g_ * 128:(g_ + 1) * 128], XW[:, g_ * 512:(g_ + 1) * 512], True, True, r=['bt', 'xw'], w=pk(b))
                            sv = ST[:, g_ * 512:(g_ + 1) * 512]
                            TT(sv.rearrange("p (h d) -> p h d", d=64), sv.rearrange("p (h d) -> p h d", d=64),
                               elast[:, g_ * 8:(g_ + 1) * 8].unsqueeze(2).to_broadcast([128, 8, 64]), ALU.mult, r=['st', 'tmp_el'], w=['st'])
                            TT(sv, sv, psum[:, b, :], ALU.add, r=['st'] + pk(b), w=['st'])
                        ACT(STB[:], ST[:], AF.Copy, r=['st'], w=['stb'])
                    if kind == 'C':
                        ssd_state_update()
                        continue
                    b = PS()
                    for g_ in range(4):
                        mm(psum[:, b, g_ * 128:(g_ + 1) * 128], XBC[:, 16 + g_, tc:tc + 128], XBC[:, 20 + g_, tc:tc + 128], True, True, r=['xbc'], w=pk(b))
                    ACT(CBT[:], psum[:, b, :].rearrange("p (a b) -> p a b", b=128), AF.Copy, r=pk(b), w=['cbt'])
                    if kind != 'S':
                        for g_ in range(4):
                            b = PS()
                            mm(psum[:, b, :], XBC[:, 20 + g_, tc:tc + 128], STB[:, g_ * 512:(g_ + 1) * 512], True, True, r=['xbc', 'stb'], w=pk(b))
                            TT(T1[:, g_ * 512:(g_ + 1) * 512].rearrange("p (h d) -> p h d", d=64), psum[:, b, :].rearrange("p (h d) -> p h d", d=64),
                               eacum[:, g_ * 8:(g_ + 1) * 8].unsqueeze(2).to_broadcast([128, 8, 64]), ALU.mult, r=pk(b) + ['tmp_eac'], w=['t1'])
                    else:
                        AEX = T2
                        CP(AEX[:].rearrange("p (h d) -> p h d", d=64), a_.unsqueeze(2).to_broadcast([128, 32, 64]), r=['tmp_a'], w=['t2'])
                        b = PS()
                        for t_ in range(16):
                            mm(psum[:, b, t_ * 16:(t_ + 1) * 16], AEX[:, t_ * 128:(t_ + 1) * 128], BLKC, True, True, r=['t2', 'cs2'], w=pk(b))
                        ACT(ELS[:], psum[:, b, 0:256].rearrange("p (t s) -> p t s", s=16), AF.Exp, r=pk(b), w=['els'])
                        for s_ in range(NS):
                            dma('sp', SQ[:], st_ssd[s_].rearrange("(t p) n -> p t n", p=128), w=['sq'])
                            for q in range(4):
                                b = q % 2
                                for j in range(4):
                                    tr(psum[:, b, j * 128:(j + 1) * 128], SQ[:, q * 4 + j, :], identf, r=['sq', 'cst'], w=pk(b))
                                ACT(SQB[:, q * 512:(q + 1) * 512], psum[:, b, :], AF.Copy, r=pk(b), w=['sqb'])
                            for g_ in range(4):
                                TT(CMS[:, g_, :], XBC[:, 20 + g_, tc:tc + 64], BLKS[:, s_, :], ALU.mult, r=['xbc', 'cs2'], w=['cms'])
                            for g_ in range(4):
                                mm(psum[0:64, 4 + g_, :], CMS[:, g_, :], SQB[:, g_ * 512:(g_ + 1) * 512], s_ == 0, s_ == NS - 1, r=['cms', 'sqb'], w=pk(4 + g_))
                            TS(XWM[:], XW[:], BLKC[:, s_:s_ + 1], None, ALU.mult, None, r=['xw', 'cs2'], w=['xwm'])
                            for q in range(4):
                                b = 2 + q % 2
                                for j in range(4):
                                    t_ = q * 4 + j
                                    mm(psum[:, b, j * 128:(j + 1) * 128], XWM[:, t_ * 128:(t_ + 1) * 128], BT[:, (t_ // 4) * 128:(t_ // 4 + 1) * 128], True, True,
                                       r=['xwm', 'bt'], w=pk(b))
                                for j in range(4):
                                    t_ = q * 4 + j
                                    STT(SQ[:, t_, :], SQ[:, t_, :], ELS[:, t_, s_:s_ + 1], psum[:, b, j * 128:(j + 1) * 128], ALU.mult, ALU.add,
                                        r=['sq', 'els'] + pk(b), w=['sq'])
                            dma('sp', o_sssd[s_].rearrange("(t p) n -> p t n", p=128), SQ[:], r=['sq'])
                        for g_ in range(4):
                            TT(T1[0:64, g_ * 512:(g_ + 1) * 512].rearrange("p (h d) -> p h d", d=64), psum[0:64, 4 + g_, :].rearrange("p (h d) -> p h d", d=64),
                               eacum[0:64, g_ * 8:(g_ + 1) * 8].unsqueeze(2).to_broadcast([64, 8, 64]), ALU.mult, r=pk(4 + g_) + ['tmp_eac'], w=['t1'])
                        MS(T1[64:128, :], 0.0, w=['t1'])
                    for h4 in range(8):
                        TT(UA[:], kc_['U'].unsqueeze(1).to_broadcast([128, 4, 128]), a_[:, h4 * 4:(h4 + 1) * 4].unsqueeze(2).to_broadcast([128, 4, 128]), ALU.mult,
                           r=['tmp_a'] + CK, w=['ua'])
                        b = PS()
                        mm(psum[:, b, :], ONE, UA[:].rearrange("p a b -> p (a b)"), True, False, r=['ua'] + CK, w=pk(b))
                        for hh in range(4):
                            mm(psum[:, b, hh * 128:(hh + 1) * 128], identf, kc_['MN'], False, hh == 3, r=CK, w=pk(b))
                        mb = h4 % 2
                        for hh in range(4):
                            h = h4 * 4 + hh
                            ACT(DEC[:, hh % 2, :], psum[:, b, hh * 128:(hh + 1) * 128], AF.Exp, r=pk(b) + ['tmp_nac'], w=[('dec', hh % 2)],
                                bias=nacum[:, h:h + 1])
                            STT(MT[:, mb, hh, :], DEC[:, hh % 2, :], DT[:, ti, h:h + 1], CBT[:, h // 8, :], ALU.mult, ALU.mult,
                                r=[('dec', hh % 2), 'dt', 'cbt'], w=[('mt', mb)])
                        for hh in range(4):
                            h = h4 * 4 + hh
                            bb = 4 + h // 8
                            mm(psum[:, bb, (h % 8) * 64:(h % 8 + 1) * 64], MT[:, mb, hh, :], XS[:, h * 64:(h + 1) * 64], True, True, r=[('mt', mb), 'xs'], w=pk(bb))
                    for g_ in range(4):
                        TT(T1[:, g_ * 512:(g_ + 1) * 512], T1[:, g_ * 512:(g_ + 1) * 512], psum[:, 4 + g_, :], ALU.add, r=['t1'] + pk(4 + g_), w=['t1'])
                    TT(T2[:].rearrange("p (h d) -> p h d", d=64), XS[:].rearrange("p (h d) -> p h d", d=64),
                       DB[:].unsqueeze(2).to_broadcast([128, 32, 64]), ALU.mult, r=['xs', 'par'], w=['t2'])
                    TT(T1[:], T1[:], T2[:], ALU.add, r=['t1', 't2'], w=['t1'])
                    TT(T1[:], T1[:], SZ[:, ti, :], ALU.mult, r=['t1', ('sz', ti)], w=['t1'])
                    if ti == 0:
                        dbg('yssd', T1[:], ['t1'])
                    for g_ in range(4):
                        ACT(T2[:, g_ * 512:(g_ + 1) * 512], T1[:, g_ * 512:(g_ + 1) * 512], AF.Square, r=['t1'], w=['t2', 'ss'], scale=512.0 ** -0.5,
                            accum_out=SS[:, 1 + g_:2 + g_])
                    rstd_from_ss(1, 4)
                    for g_ in range(4):
                        ACT(XNT[:, g_ * 512:(g_ + 1) * 512], T1[:, g_ * 512:(g_ + 1) * 512], AF.Copy, r=['t1', 'ss'], w=['xnt'], scale=SS[:, 1 + g_:2 + g_])
                    to_fm(XNT, 'xnt', U_fm, 'u_fm', tc, lambda g4: GF[:, 4, g4:g4 + 4])
                    if kind != 'S':
                        ssd_state_update()
                dbg('u_fm', U_fm[:, :, 0:Tn], ['u_fm'])
                chk('ssd')

                for cb_ in (range(4) if full else ()):
                    def epi_ga(ci, b, cb_=cb_):
                        ACT(SG[:, cb_ * 4 + ci, 0:Tn], psum[:, b, 0:Tn], AF.Sigmoid, r=pk(b), w=['sg'])
                    proj_fm(A_fm, 'a_fm', Tn, w_in, C_GA + cb_ * 512, 4, epi_ga)
                for cb_ in (range(4) if full else ()):
                    def epi_a(ci, b, cb_=cb_):
                        TT(MRG[:, cb_ * 4 + ci, 0:Tn], psum[:, b, 0:Tn], SG[:, cb_ * 4 + ci, 0:Tn], ALU.mult, r=pk(b) + ['sg'], w=['mrg'])
                    proj_fm(U_fm, 'u_fm', Tn, w_ssd_out, cb_ * 512, 4, epi_a)

                if not SKIPGLA:
                    for cb_ in (range(2) if full else ()):
                        def epi_q(ci, b, cb_=cb_):
                            ACT(QF[:, cb_ * 4 + ci, 0:Tn], psum[:, b, 0:Tn], AF.Copy, r=pk(b), w=['qf'], scale=1.0 / 16.0)
                        proj_fm(A_fm, 'a_fm', Tn, w_in, C_Q + cb_ * 512, 4, epi_q)
                    for cb_ in range(2):
                        def epi_kf(ci, b, cb_=cb_):
                            ACT(KF[:, cb_ * 4 + ci, 0:Tn], psum[:, b, 0:Tn], AF.Copy, r=pk(b), w=['kf'])
                        proj_fm(A_fm, 'a_fm', Tn, w_in, C_K + cb_ * 512, 4, epi_kf)
                    for cb_ in range(4):
                        def epi_v(t, b, cb_=cb_):
                            ACT(VT[:, t, cb_ * 512:(cb_ + 1) * 512], psum[:, b, :], AF.Copy, r=pk(b), w=['vt'])
                        proj_tm(A_fm, 'a_fm', range(nt), w_in, C_V + cb_ * 512, 512, epi_v)
                    for cb_ in range(4):
                        def epi_r(t, b, cb_=cb_):
                            ACT(SR[:, t, cb_ * 512:(cb_ + 1) * 512], psum[:, b, :], AF.Silu, r=pk(b), w=['sr'])
                        if full:
                            proj_tm(A_fm, 'a_fm', full, w_in, C_R + cb_ * 512, 512, epi_r)
                    s = load_w(w_in, 0, 16, C_G, 16)
                    b = PS()
                    for kc in range(16):
                        mm(psum[0:16, b, 0:Tn], WS[s][:, kc, 0:16], A_fm[:, kc, 0:Tn], kc == 0, kc == 15, r=['a_fm', ('W', s)], w=pk(b))
                    MS(GLR[:, 0:Tn], 1.0, w=['glr'])
                    ACT(GLR[0:16, 0:Tn], psum[0:16, b, 0:Tn], AF.Copy, r=pk(b), w=['glr'])

                    ACT(GSB[:].rearrange("p a b -> p (a b)"), GS[:].rearrange("p a b -> p (a b)"), AF.Copy, r=['gs'], w=['gsb'])
                    for ti, (kind, i) in enumerate(tiles):
                        kc_ = KC['S' if kind == 'S' else 'P']
                        tc = ti * 128
                        for hb in range(2):
                            b = PS()
                            mm(psum[:, b, :], GLR[0:17, tc:tc + 128], WG[0:17, hb * 512:(hb + 1) * 512], True, True, r=['glr', 'wg'], w=pk(b))
                            gs_ = slice(hb * 512, (hb + 1) * 512)
                            CP(GT2[:, gs_], psum[:, b, :], r=pk(b), w=['t2'])
                            STT(GT1[:, gs_], GT2[:, gs_], -1.0, GT2[:, gs_], ALU.mult, ALU.max, r=['t2'], w=['t1'])
                            ACT(GT1[:, gs_], GT1[:, gs_], AF.Exp, r=['t1'], w=['t1'], scale=-1.0)
                            ACT(GT1[:, gs_], GT1[:, gs_], AF.Ln, r=['t1'], w=['t1'], bias=1.0)
                            TS(GT2[:, gs_], GT2[:, gs_], 0.0, None, ALU.min, None, r=['t2'], w=['t2'])
                            TT(G[:, gs_], GT2[:, gs_], GT1[:, gs_], ALU.subtract, r=['t1', 't2'], w=['g'])
                        TS(G[:], G[:], 1.0 / 16.0, None, ALU.mult, None, r=['g'], w=['g'])
                        if ti == 0:
                            dbg('glog', G[:], ['g'])
                        for c in (range(8) if kind != 'C' else ()):
                            b = PS()
                            mm(psum[:, b, 0:128], G[:, c * 128:(c + 1) * 128], kc_['U'], True, True, r=['g'] + CK, w=pk(b))
                            ACT(EBT[:, 0, :], psum[:, b, 0:128], AF.Exp, r=pk(b), w=[('ebt', 0)])
                            ACT(EBT[:, 1, :], psum[:, b, 0:128], AF.Exp, r=pk(b), w=[('ebt', 1)], scale=-1.0)
                            TT(QE[:, c, :], QF[:, c, tc:tc + 128], EBT[:, 0, :], ALU.mult, r=['qf', ('ebt', 0)], w=['qe'])
                            TT(KE[:, c, :], KF[:, c, tc:tc + 128], EBT[:, 1, :], ALU.mult, r=['kf', ('ebt', 1)], w=['ke'])
                        if kind != 'C':
                            b = PS()
                            for h in range(4):
                                for kc in range(2):
                                    mm(psum[:, b, h * 128:(h + 1) * 128], KE[:, h * 2 + kc, :], QE[:, h * 2 + kc, :], kc == 0, kc == 1, r=['ke', 'qe'], w=pk(b))
                            TT(ATT[:], psum[:, b, :].rearrange("p (h i) -> p h i", i=128), kc_['M01'][:].unsqueeze(1).to_broadcast([128, 4, 128]), ALU.mult,
                               r=pk(b) + ['m01b'], w=['att'])
                        for hb in range(2):
                            b = PS()
                            mm(psum[:, b, :], kc_['L'], G[:, hb * 512:(hb + 1) * 512], True, True, r=['g'] + CK, w=pk(b))
                            ACT(GT1[:, hb * 512:(hb + 1) * 512], psum[:, b, :], AF.Exp, r=pk(b), w=['t1'])
                        b = PS(); pv = psum[:, b, :].bitcast(BF16)
                        for c in range(8):
                            tr(pv[:, c * 128:(c + 1) * 128], KF[:, c, tc:tc + 128], identb[:], r=['kf', 'identb'], w=pk(b))
                        ACT(KW[:], pv[:, 0:1024], AF.Copy, r=pk(b), w=['kw'])
                        TT(KW[:], KW[:], GT1[:], ALU.mult, r=['kw', 't1'], w=['kw'])
                        b = PS()
                        ncol = 1 if kind != 'S' else 16
                        for c in range(8):
                            mm(psum[:, b, c * 16:c * 16 + ncol], G[:, c * 128:(c + 1) * 128], (ONE[:, 0:1] if kind != 'S' else BLKC), True, True,
                               r=['g'] + CK, w=pk(b))
                        ACT(ELG[:, :, 0:ncol], psum[:, b, 0:128].rearrange("p (c s) -> p c s", s=16)[:, :, 0:ncol], AF.Exp, r=pk(b), w=['elg'])
                        if kind != 'S':
                            for h in (range(4) if kind != 'C' else ()):
                                mm(psum[:, 4 + h, :], ATT[:, h, :], VT[:, ti, h * 512:(h + 1) * 512], True, False, r=['att', 'vt'], w=pk(4 + h))
                                for kc in range(2):
                                    mm(psum[:, 4 + h, :], QE[:, h * 2 + kc, :], GSB[:, h * 2 + kc, :], False, kc == 1, r=['qe', 'gsb'], w=pk(4 + h))
                            for c in range(8):
                                b = PS()
                                mm(psum[:, b, :], KW[:, c * 128:(c + 1) * 128], VT[:, ti, (c // 2) * 512:(c // 2 + 1) * 512], True, True, r=['kw', 'vt'], w=pk(b))
                                STT(GS[:, c, :], GS[:, c, :], ELG[:, c, 0:1], psum[:, b, :], ALU.mult, ALU.add, r=['gs', 'elg'] + pk(b), w=['gs'])
                        else:
                            for h in range(4):
                                mm(psum[:, 4 + h, :], ATT[:, h, :], VT[:, ti, h * 512:(h + 1) * 512], True, False, r=['att', 'vt'], w=pk(4 + h))
                            for s_ in range(NS):
                                sgv = SQ[:].rearrange("p a b -> p (a b)")
                                for hh in range(2):
                                    dma('sp', sgv.rearrange("p (c v) -> p c v", v=512),
                                        st_gla[s_, hh * 2:hh * 2 + 2].rearrange("h (kc p) v -> p (h kc) v", p=128), w=['sq'], key='sq')
                                    ACT(SQB[:], sgv, AF.Copy, r=['sq'], w=['sqb'])
                                    TT(CMS[:, 0:4, :], QE[:, hh * 4:hh * 4 + 4, 0:64], BLKS[:, s_:s_ + 1, :].to_broadcast([128, 4, 64]), ALU.mult,
                                       r=['qe', 'cs2'], w=['cms'])
                                    for c4 in range(4):
                                        h = hh * 2 + c4 // 2
                                        last = (s_ == NS - 1) and (c4 % 2 == 1)
                                        mm(psum[0:64, 4 + h, :], CMS[:, c4, :], SQB[:, c4 * 512:(c4 + 1) * 512], False, last, r=['cms', 'sqb'], w=pk(4 + h))
                                    TS(XWM[:, 0:512], KW[:, hh * 512:(hh + 1) * 512], BLKC[:, s_:s_ + 1], None, ALU.mult, None, r=['kw', 'cs2'], w=['xwm'])
                                    for c4 in range(4):
                                        c = hh * 4 + c4
                                        b = PS()
                                        mm(psum[:, b, :], XWM[:, c4 * 128:(c4 + 1) * 128], VT[:, ti, (c // 2) * 512:(c // 2 + 1) * 512], True, True, r=['xwm', 'vt'], w=pk(b))
                                        STT(SQ[:, c4 * 4:(c4 + 1) * 4, :].rearrange("p a b -> p (a b)"), SQ[:, c4 * 4:(c4 + 1) * 4, :].rearrange("p a b -> p (a b)"),
                                            ELG[:, c, s_:s_ + 1], psum[:, b, :], ALU.mult, ALU.add, r=['sq', 'elg'] + pk(b), w=['sq'])
                                    dma('sp', o_sgla[s_, hh * 2:hh * 2 + 2].rearrange("h (kc p) v -> p (h kc) v", p=128),
                                        sgv.rearrange("p (c v) -> p c v", v=512), r=['sq'], key='sq')
                        if kind == 'C':
                            ACT(GSB[:].rearrange("p a b -> p (a b)"), GS[:].rearrange("p a b -> p (a b)"), AF.Copy, r=['gs'], w=['gsb'])
                            continue
                        for h in range(4):
                            ACT(T2[:, h * 512:(h + 1) * 512], psum[:, 4 + h, :], AF.Square, r=pk(4 + h), w=['t2', 'ss'], scale=512.0 ** -0.5, accum_out=SS[:, 1 + h:2 + h])
                        rstd_from_ss(1, 4)
                        for h in range(4):
                            ACT(T1[:, h * 512:(h + 1) * 512], psum[:, 4 + h, :], AF.Copy, r=pk(4 + h) + ['ss'], w=['t1'], scale=SS[:, 1 + h:2 + h])
                        if ti == 0:
                            dbg('ogla', T1[:], ['t1'])
                        TT(XNT[:], T1[:], SR[:, ti, :], ALU.mult, r=['t1', 'sr'], w=['xnt'])
                        to_fm(XNT, 'xnt', U_fm, 'u_fm', tc, lambda g4: GF[:, 5, 0:4])
                        if kind != 'S':
                            ACT(GSB[:].rearrange("p a b -> p (a b)"), GS[:].rearrange("p a b -> p (a b)"), AF.Copy, r=['gs'], w=['gsb'])
                    dbg('o_fm', U_fm[:, :, 0:Tn], ['u_fm'])
                    chk('gla')

                if not full:
                    continue
                for cb_ in range(int(os.environ.get("GBN", 4))):
                    def epi_gb(ci, b, cb_=cb_):
                        ACT(SG[:, cb_ * 4 + ci, 0:Tn], psum[:, b, 0:Tn], AF.Sigmoid, r=pk(b), w=['sg'])
                    proj_fm(A_fm, 'a_fm', Tn, w_in, C_GB + cb_ * 512, 4, epi_gb)
                chk('gb')
                for cb_ in range(4):
                    def epi_b(ci, b, cb_=cb_):
                        c = cb_ * 4 + ci
                        TT(TMPG[:, 0:Tn], psum[:, b, 0:Tn], SG[:, c, 0:Tn], ALU.mult, r=pk(b) + ['sg'], w=['tmpg'])
                        TT(MRG[:, c, 0:Tn], MRG[:, c, 0:Tn], TMPG[:, 0:Tn], ALU.add, r=['mrg', 'tmpg'], w=['mrg'])
                    proj_fm(U_fm, 'u_fm', Tn, w_gla_out, cb_ * 512, 4, epi_b)
                dbg('mrg', MRG[:, :, 0:Tn], ['mrg'])
                chk('mrg')
                for ti, (k, i) in enumerate(tiles):
                    src = x_p[i * 128:(i + 1) * 128, :] if k != 'S' else x_s[:, :]
                    dma('sp', H[:, ti, :], src, w=[('H', ti)])
                for cb_ in range(4):
                    def epi_m(t, b, cb_=cb_):
                        hv = H[:, t, cb_ * 512:(cb_ + 1) * 512]
                        TT(hv, hv, psum[:, b, :], ALU.add, r=[('H', t)] + pk(b), w=[('H', t)])
                    proj_tm(MRG, 'mrg', full, w_mix, cb_ * 512, 512, epi_m)
                dbg('h1', H[:, 0:nt, :], [('H', t) for t in range(nt)])
                chk('h1')

                for ti in range(nt):
                    rms_to_fm(H[:, ti, :], ('H', ti), 1, A_fm, 'a_fm', ti * 128)
                QC = SG
                for cb_ in range(4):
                    def epi_cq(ci, b, cb_=cb_):
                        ACT(QC[:, cb_ * 4 + ci, 0:Tn], psum[:, b, 0:Tn], AF.Copy, r=pk(b), w=['sg'], scale=512.0 ** -0.5)
                    proj_fm(A_fm, 'a_fm', Tn, w_cq, cb_ * 512, 4, epi_cq)
                OC = U_fm

                def softmax_rows(np_):
                    for h in range(4):
                        P.op('dve', lambda e, h=h: e.reduce_max(out=SS[0:np_, 1 + h:2 + h], in_=psum[0:np_, 4 + h, 0:NMEM], axis=mybir.AxisListType.X),
                             r=pk(4 + h), w=['ss'])
                    TS(SS[0:np_, 1:5], SS[0:np_, 1:5], -1.0, None, ALU.mult, None, r=['ss'], w=['ss'])
                    for h in range(4):
                        ACT(PSM[0:np_, h, :], psum[0:np_, 4 + h, 0:NMEM], AF.Exp, r=pk(4 + h) + ['ss'], w=['t1', 'tmpA'], bias=SS[0:np_, 1 + h:2 + h],
                            accum_out=TMP32[0:np_, 0, h:h + 1])
                    P.op('dve', lambda e: e.reciprocal(out=TMP32[0:np_, 1, 0:4], in_=TMP32[0:np_, 0, 0:4]), r=['tmpA'], w=['tmpB'])
                    TT(PB[0:np_], PSM[0:np_], TMP32[0:np_, 1, 0:4].unsqueeze(2).to_broadcast([np_, 4, NMEM]), ALU.mult, r=['t1', 'tmpB'], w=['pb'])

                def probs_T(np_):
                    b = PS(); pv = psum[:, b, :].bitcast(BF16)
                    for h in range(4):
                        for m_ in range(2):
                            j = h * 2 + m_
                            tr(pv[:, j * 128:j * 128 + np_], PB[0:np_, h, m_ * 128:(m_ + 1) * 128], identb[0:np_, 0:np_], r=['pb', 'identb'], w=pk(b))
                    ACT(PT[:, :, 0:np_], pv[:, 0:1024].rearrange("p (j t) -> p j t", t=128)[:, :, 0:np_], AF.Copy, r=pk(b), w=['ptt'])

                for ti, (kind, i) in enumerate(tiles):
                    tc = ti * 128
                    if kind == 'C':
                        continue
                    if kind != 'S':
                        for h in range(4):
                            for dc in range(4):
                                mm(psum[:, 4 + h, 0:NMEM], QC[:, h * 4 + dc, tc:tc + 128], KFM[:, h * 4 + dc, :], dc == 0, dc == 3, r=['sg', 'kfm'], w=pk(4 + h))
                        softmax_rows(128)
                        probs_T(128)
                        for q in range(4):
                            b = PS()
                            for j in range(4):
                                dc = q * 4 + j
                                for m_ in range(2):
                                    mm(psum[:, b, j * 128:(j + 1) * 128], VTM[:, m_, dc * 128:(dc + 1) * 128], PT[:, (dc // 4) * 2 + m_, :], m_ == 0, m_ == 1,
                                       r=['vtm', 'ptt'], w=pk(b))
                            ACT(OC[:, q * 4:(q + 1) * 4, tc:tc + 128], psum[:, b, :].rearrange("p (a t) -> p a t", t=128), AF.Copy, r=pk(b), w=['u_fm'])
                    else:
                        for s_ in range(NS):
                            dma('pool', KSB[:], ck[s_].rearrange("(m p) d -> p m d", p=128), w=['ksb'], key='ksb')
                            for m_ in range(2):
                                for q in range(2):
                                    b = PS(); pv = psum[:, b, :].bitcast(BF16)
                                    for j in range(8):
                                        dc = q * 8 + j
                                        tr(pv[:, j * 128:(j + 1) * 128], KSB[:, m_, dc * 128:(dc + 1) * 128], identb[:], r=['ksb', 'identb'], w=pk(b))
                                    ACT(KTS[:, q * 8:(q + 1) * 8, m_ * 128:(m_ + 1) * 128], pv[:, 0:1024].rearrange("p (j t) -> p j t", t=128), AF.Copy,
                                        r=pk(b), w=['kts'])
                            TT(CMS[:], QC[:, :, tc:tc + 64], BLKS[:, s_:s_ + 1, :].to_broadcast([128, 16, 64]), ALU.mult, r=['sg', 'cs2'], w=['cms'])
                            for h in range(4):
                                for dc in range(4):
                                    mm(psum[0:64, 4 + h, 0:NMEM], CMS[:, h * 4 + dc, :], KTS[:, h * 4 + dc, :], s_ == 0 and dc == 0, s_ == NS - 1 and dc == 3,
                                       r=['cms', 'kts'], w=pk(4 + h))
                        softmax_rows(64)
                        probs_T(64)
                        for s_ in range(NS):
                            dma('pool', KSB[:], cv[s_].rearrange("(m p) d -> p m d", p=128), w=['ksb'], key='ksb')
                            for dc in range(16):
                                bb = 4 + dc // 8
                                for m_ in range(2):
                                    first = (s_ == 0 and dc % 8 == 0 and m_ == 0)
                                    mm(psum[:, bb, (dc % 8) * 64 + s_ * 4:(dc % 8) * 64 + s_ * 4 + 4], KSB[:, m_, dc * 128:(dc + 1) * 128],
                                       PT[:, (dc // 4) * 2 + m_, s_ * 4:s_ * 4 + 4], first, False, r=['ksb', 'ptt'], w=pk(bb))
                        for q in range(2):
                            ACT(OC[:, q * 8:(q + 1) * 8, tc:tc + 4 * NS], psum[:, 4 + q, :].rearrange("p (a t) -> p a t", t=64)[:, :, 0:4 * NS], AF.Copy,
                                r=pk(4 + q), w=['u_fm'])
                        if 4 * NS < 128:
                            MS(OC[:, :, tc + 4 * NS:tc + 128], 0.0, w=['u_fm'])
                dbg('oc', OC[:, :, 0:Tn], ['u_fm'])
                chk('oc')
                for cb_ in range(4):
                    def epi_co(t, b, cb_=cb_):
                        hv = H[:, t, cb_ * 512:(cb_ + 1) * 512]
                        TT(hv, hv, psum[:, b, :], ALU.add, r=[('H', t)] + pk(b), w=[('H', t)])
                    proj_tm(OC, 'u_fm', full, w_co, cb_ * 512, 512, epi_co)
                dbg('h2', H[:, 0:nt, :], [('H', t) for t in range(nt)])
                chk('h2')

                for ti in range(nt):
                    rms_to_fm(H[:, ti, :], ('H', ti), 3, A_fm, 'a_fm', ti * 128)
                for cb_ in range(11):
                    sa = load_w(w_up, 0, 16, cb_ * 512, 512)
                    sg_ = load_w(w_up, 0, 16, FFN + cb_ * 512, 512)
                    if has_s:
                        dma('sp', FST[0:NS * 2, 0:512], st_fc[:, cb_ * 512:(cb_ + 1) * 512], w=['fst'], key='xin')
                        dma('sp', FST[0:NS * 2, 512:1024], st_fc[:, FFN + cb_ * 512:FFN + (cb_ + 1) * 512], w=['fst'], key='xin')
                    for ci in range(4):
                        c = cb_ * 4 + ci
                        ba = PS(); bg_ = PS()
                        for kc in range(16):
                            mm(psum[:, ba, 0:Tn], WS[sa][:, kc, ci * 128:(ci + 1) * 128], A_fm[:, kc, 0:Tn], kc == 0, kc == 15, r=['a_fm', ('W', sa)], w=pk(ba))
                        for kc in range(16):
                            mm(psum[:, bg_, 0:Tn], WS[sg_][:, kc, ci * 128:(ci + 1) * 128], A_fm[:, kc, 0:Tn], kc == 0, kc == 15, r=['a_fm', ('W', sg_)], w=pk(bg_))
                        for half, (b, cc) in enumerate([(ba, c), (bg_, 44 + c)]):
                            conv_chunk(b, cc, RAW2[:, half, :], ('raw2', half), ACC2[:, half, :], ('acc2', half), FHALO, FW, 3, None)
                            if has_s:
                                conv_chunk_s(b, cc, (RAWS if half == 0 else RAWS2), ('raws' if half == 0 else 'raws2'), FST, ['fst'], half * 512 + ci * 128,
                                             ACC2[:, half, :], ('acc2', half), FW, 3)
                        nact = Tp + (64 if has_s else 0)
                        ACT(ACC2[:, 1, 0:nact], ACC2[:, 1, 0:nact], AF.Silu, r=[('acc2', 1), 'par'], w=[('acc2', 1)], bias=FB[:, 44 + c:45 + c])
                        STT(ACTF[:, c, 0:nact], ACC2[:, 0, 0:nact], FB[:, c:c + 1], ACC2[:, 1, 0:nact], ALU.add, ALU.mult,
                            r=[('acc2', 0), ('acc2', 1), 'par'], w=['actf'])
                        if has_s:
                            MS(ACTF[:, c, Tp + 64:Tp + 128], 0.0, w=['actf'])
                    for ti in flagged:
                        for half, s in enumerate([sa, sg_]):
                            b = PS()
                            for kc in range(16):
                                mm(psum[:, b, :], A_fm[:, kc, ti * 128:(ti + 1) * 128], WS[s][:, kc, :], kc == 0, kc == 15, r=['a_fm', ('W', s)], w=pk(b))
                            ACT(OST[:], psum[:, b, :], AF.Copy, r=pk(b), w=['ost'])
                            col0 = half * FFN + cb_ * 512
                            if tiles[ti][0] == 'P':
                                dma('sp', o_pfc[:, col0:col0 + 512], OST[126:128, :], r=['ost'], key='o1')
                            else:
                                for r_ in range(2, 4):
                                    dma('sp', o_sfc[:, r_ - 2, col0:col0 + 512], OST[r_:4 * NS:4, :], r=['ost'], key='o1')
                dbg('actf', ACTF[:, :, 0:Tn], ['actf'])
                chk('actf')
                assert nt <= 4
                for cb_ in range(4):
                    for kg, (k0, nk) in enumerate([(0, 16), (16, 16), (32, 12)]):
                        s = load_w(w_down, k0 * 128, nk, cb_ * 512, 512)
                        for t in full:
                            for kc in range(nk):
                                mm(psum[:, 4 + t, :], ACTF[:, k0 + kc, t * 128:(t + 1) * 128], WS[s][:, kc, :], kg == 0 and kc == 0, kg == 2 and kc == nk - 1,
                                   r=['actf', ('W', s)], w=pk(4 + t))
                    for t in full:
                        hv = H[:, t, cb_ * 512:(cb_ + 1) * 512]
                        TT(hv, hv, psum[:, 4 + t, :], ALU.add, r=[('H', t)] + pk(4 + t), w=[('H', t)])
                if has_h:
                    TS(FHALO[:].rearrange("p a b -> p (a b)"), FHALO[:].rearrange("p a b -> p (a b)"), FLAG[:, 0:1], None, ALU.mult, None,
                       r=['halo', 'flag'], w=['halo'])
                for ti, (kind, i) in enumerate(tiles):
                    if kind in ('C', 'H'):
                        continue
                    norm_rstd(H[:, ti, :], [('H', ti)], D, 0)
                    dma('sp', T2[:], vec["norm_final"].partition_broadcast(128), w=['t2'])
                    STT(T1[:], H[:, ti, :], SS[:, 0:1], T2[:], ALU.mult, ALU.mult, r=[('H', ti), 'ss', 't2'], w=['t1'])
                    dst = y_p[(i - NCTX) * 128:(i - NCTX + 1) * 128, :] if kind == 'P' else y_s[:, :]
                    dma('sp', dst, T1[:], r=['t1'], key='o2')

            for q in range(4):
                b = PS()
                for j in range(4):
                    tr(psum[:, b, j * 128:(j + 1) * 128], ST[:, (q * 4 + j) * 128:(q * 4 + j + 1) * 128], identf, r=['st', 'cst'], w=pk(b))
                ACT(SQ[:, q * 4:(q + 1) * 4, :], psum[:, b, :].rearrange("p (a b) -> p a b", b=128), AF.Copy, r=pk(b), w=['sq'])
            dma('sp', o_pssd.rearrange("(t p) n -> p t n", p=128), SQ[:], r=['sq'], key='sq')
            dma('sp', o_pgla.rearrange("h (kc p) v -> p (h kc) v", p=128), GS[:], r=['gs'], key='o2')


        except _Stop:
            pass

        sems = {}
        for e_ in ENGS:
            sems[e_] = es.enter_context(nc.semaphore("sem_" + e_))
        dma_sems = {}
        for i_, k in enumerate(dsem_keys):
            dma_sems[k] = es.enter_context(nc.semaphore(f"dsem{i_}"))
        P.finalize(sems, dma_sems)
        if dbg_names or stop:
            print('STATS', P.stats)
        block = es.enter_context(nc.Block())

        @block.tensor
        def _(e):
            P.run('pe', e)

        @block.vector
        def _(e):
            P.run('dve', e)

        @block.scalar
        def _(e):
            P.run('act', e)

        @block.gpsimd
        def _(e):
            P.run('pool', e)

        @block.sync
        def _(e):
            P.run('sp', e)
    return nc, din_, dout_, dbg_out


def make_consts():
    t = np.arange(128)
    c = np.zeros((128, 1024), np.float32)
    c[:, 0:128] = np.eye(128)
    c[:, 128:256] = (t[:, None] <= t[None, :])
    c[:, 256:384] = (t[:, None] > t[None, :])
    c[:, 384:512] = 1.0
    valid = (t[None, :] >= t[:, None])
    c[:, 512:640] = np.where(valid, 0.0, -30000.0)
    c[:, 640:768] = valid
    c2 = np.zeros((128, 2048), np.float32)
    same = (t[:, None] // 4) == (t[None, :] // 4)
    c2[:, 0:128] = (t[:, None] <= t[None, :]) & same
    c2[:, 128:256] = (t[:, None] > t[None, :]) & same
    c2[:, 256:384] = same
    c2[:, 384:512] = np.where(valid & same, 0.0, -30000.0)
    c2[:, 512:640] = valid & same
    c2[:, 640:656] = (t[:, None] // 4) == np.arange(16)[None, :]
    blks = ((np.arange(64)[None, :] // 4) == np.arange(16)[:, None]).astype(np.float32)
    c2[:, 1024:2048] = blks.reshape(1, 1024)
    return c, c2


WEIGHT_NAMES = ["w_in", "w_ssd_out", "w_gla_out", "w_mix_out", "w_cq", "w_ck", "w_cv", "w_co", "w_up", "w_down", "w_gla_gate",
                "norm_final", "b_gla_gate"]


def make_params(inp):
    f = lambda a: np.asarray(a, dtype=np.float32)
    p = np.zeros((128, 672), np.float32)
    for gi, n in enumerate(["norm_mix", "norm_cross", "norm_mem", "norm_ffn", "ssd_norm"]):
        p[:, gi * 16:(gi + 1) * 16] = f(inp[n]).reshape(16, 128).T
    p[:, 80:84] = f(inp["gla_norm"]).reshape(4, 128).T
    p[:, 96:192] = f(inp["ssd_conv_w"]).reshape(4, 24, 128).transpose(2, 1, 0).reshape(128, 96)
    p[:, 192:216] = f(inp["ssd_conv_b"]).reshape(24, 128).T
    p[:, 216:480] = f(inp["ffn_conv_w"]).reshape(3, 88, 128).transpose(2, 1, 0).reshape(128, 264)
    p[:, 480:568] = f(inp["ffn_conv_b"]).reshape(88, 128).T
    p[:, 568:600] = f(inp["ssd_dt_bias"])[None, :]
    p[:, 600:632] = f(inp["ssd_A_log"])[None, :]
    p[:, 632:664] = f(inp["ssd_D"])[None, :]
    return p


TPP_FULL = 2


def kernel(**inp):
    f = lambda a: np.ascontiguousarray(np.asarray(a, dtype=np.float32))
    NB, L = inp["x_prompt"].shape[:2]
    NPT = L // 128
    NCTX = NPT // 2
    NSB = inp["x_sample"].shape[0]
    ncores = 8
    NS = NSB // ncores
    nc, _, _, _ = build(NPT, NS, TPP_FULL, NCTX=NCTX)
    c1, c2 = make_consts()
    shared = {n: f(inp[n]) for n in WEIGHT_NAMES}
    shared["consts"] = c1; shared["consts2"] = c2; shared["params"] = make_params(inp)
    in_maps = []
    Lh = L // 2
    for c in range(ncores):
        b, half = c // 2, c % 2
        m = dict(shared)
        xp = np.zeros((L, D), np.float32)
        if half == 1:
            xp[:] = f(inp["x_prompt"][b])
        else:
            xp[Lh:] = f(inp["x_prompt"][b, :Lh])
        m["x_p"] = xp
        m["flag"] = np.full((128, 1), float(half), np.float32)
        xs = np.zeros((128, D), np.float32)
        xs[:NS * 4] = f(inp["x_sample"][c * NS:(c + 1) * NS]).reshape(NS * 4, D)
        m["x_s"] = xs
        m["mem_p"] = f(inp["mem_prompt"][b])
        sl = slice(c * NS, (c + 1) * NS)
        m["cache_k"] = f(inp["cache_mem_k"][sl]).reshape(NS, NMEM, D)
        m["cache_v"] = f(inp["cache_mem_v"][sl]).reshape(NS, NMEM, D)
        m["st_ssd_conv"] = f(inp["state_ssd_conv"][sl]).reshape(NS * 3, CONV_DIM)
        m["st_ssd"] = f(inp["state_ssd"][sl]).reshape(NS, D, 128)
        m["st_gla"] = f(inp["state_gla"][sl])
        m["st_ffn_conv"] = f(inp["state_ffn_conv"][sl]).reshape(NS * 2, 2 * FFN)
        in_maps.append(m)
    res = run_bass_kernel_spmd(nc, in_maps, core_ids=list(range(ncores)))
    R = res.results
    pc = [R[2 * b + 1] for b in range(NB)]
    y_prompt = np.stack([np.concatenate([R[2 * b]["y_p"], R[2 * b + 1]["y_p"]], axis=0) for b in range(NB)]).reshape(NB, L, D)
    y_sample = np.concatenate([r["y_s"][:NS * 4].reshape(NS, 4, D) for r in R])
    p_ssd_conv = np.stack([r["p_ssd_conv"] for r in pc])
    p_ssd = np.stack([r["p_ssd"].reshape(32, 64, 128) for r in pc])
    p_gla = np.stack([r["p_gla"] for r in pc])
    p_ffn_conv = np.stack([r["p_ffn_conv"] for r in pc])
    p_mem_k = np.stack([r["p_mem_k"].reshape(NMEM, 4, 512) for r in pc])
    p_mem_v = np.stack([r["p_mem_v"].reshape(NMEM, 4, 512) for r in pc])
    s_ssd_conv = np.concatenate([r["s_ssd_conv"] for r in R])
    s_ssd = np.concatenate([r["s_ssd"].reshape(NS, 32, 64, 128) for r in R])
    s_gla = np.concatenate([r["s_gla"] for r in R])
    s_ffn_conv = np.concatenate([r["s_ffn_conv"] for r in R])
    outs = (y_prompt, y_sample, p_ssd_conv, p_ssd, p_gla, p_ffn_conv, p_mem_k, p_mem_v, s_ssd_conv, s_ssd, s_gla, s_ffn_conv)
    return tuple(np.ascontiguousarray(o, dtype=np.float32) for o in outs)
```

```python
import os
import numpy as np
from contextlib import ExitStack
import concourse.bass as bass
import concourse.mybir as mybir
from concourse.bass_utils import run_bass_kernel_spmd

F32 = mybir.dt.float32
BF16 = mybir.dt.bfloat16
AF = mybir.ActivationFunctionType
ALU = mybir.AluOpType

D = 2048
CONV_DIM = 3072
FFN = 5632
IN_DIM = 15408
EPS = 1e-6
NMEM = 256
C_Z, C_XBC, C_DT, C_Q, C_K, C_V, C_R, C_G, C_GA, C_GB = 0, 2048, 5120, 5152, 6176, 7200, 9248, 11296, 11312, 13360
ENGS = ['pe', 'dve', 'act', 'pool', 'sp']
SKIPGLA = bool(os.environ.get('SKIPGLA'))


class Prog:
    def __init__(self):
        self.ops = []
        self.last_w = {}
        self.readers = {}
        self.alias = {}
        self._exp = {}

    def expand(self, keys):
        out = []
        for k in keys:
            if k not in self._exp:
                kk = k if k in self.alias else (k[0] if isinstance(k, tuple) and k[0] in self.alias else None)
                if kk is None:
                    self._exp[k] = [k]
                else:
                    lo, hi = self.alias[kk]
                    if k not in self.alias:
                        self.alias[k] = (lo, hi)
                        self._exp = {k0: v for k0, v in self._exp.items() if False}
                    self._exp[k] = [k] + [k2 for k2, (l2, h2) in self.alias.items() if k2 != k and l2 < hi and lo < h2]
            out.extend(self._exp[k])
        return out

    def op(self, eng, fn, r=(), w=(), dma=None):
        oid = len(self.ops)
        deps = set()
        r = self.expand(r)
        w = self.expand(w)
        for b in r:
            if b in self.last_w:
                deps.add(self.last_w[b])
            if isinstance(b, tuple) and b[0] == 'ps':
                for x in self.readers.get(b, ()):
                    if self.ops[x]['eng'] != eng:
                        deps.add(x)
        for b in w:
            if b in self.last_w:
                deps.add(self.last_w[b])
            for x in self.readers.get(b, ()):
                deps.add(x)
        for b in r:
            self.readers.setdefault(b, []).append(oid)
        for b in w:
            self.last_w[b] = oid
            self.readers[b] = []
        self.ops.append(dict(eng=eng, fn=fn, deps=deps, dma=dma, marked=False))
        return oid

    def finalize(self, sems, dma_sems):
        ops = self.ops
        for o in ops:
            best = {}
            cd = []
            for d in o['deps']:
                od = ops[d]
                if od['dma'] is not None:
                    cd.append(d)
                    continue
                if od['eng'] == 'pe' and o['eng'] == 'pe' and o['dma'] is None:
                    continue
                if od['eng'] not in best or d > best[od['eng']]:
                    best[od['eng']] = d
            for d in best.values():
                ops[d]['marked'] = True
                cd.append(d)
            o['cdeps'] = sorted(cd)
        cnt = {e: 0 for e in ENGS}
        dcnt = {}
        for o in ops:
            if o['dma'] is not None:
                dcnt[o['dma']] = dcnt.get(o['dma'], 0) + 16
                o['sig'] = (dma_sems[o['dma']], dcnt[o['dma']])
            elif o['marked']:
                cnt[o['eng']] += 1
                o['sig'] = (sems[o['eng']], cnt[o['eng']])
        self.per = {e: [o for o in ops if o['eng'] == e] for e in ENGS}
        self.stats = dict(cnt=cnt, dcnt={str(k): v for k, v in dcnt.items()}, nops={e: len(self.per[e]) for e in ENGS})
        self.final_waits = [(dma_sems[k], v) for k, v in dcnt.items()]

    def run(self, engname, e):
        ops = self.ops
        waited = {}
        for o in self.per[engname]:
            for d in o['cdeps']:
                od = ops[d]
                if 'sig' not in od:
                    continue
                sem, val = od['sig']
                key = id(sem)
                if waited.get(key, 0) >= val:
                    continue
                waited[key] = val
                e.wait_ge(sem, val)
            ins = o['fn'](e)
            if 'sig' in o:
                ins.then_inc(o['sig'][0], 16 if o['dma'] is not None else 1)
        if engname == 'sp':
            for sem, val in self.final_waits:
                e.wait_ge(sem, val)


class _Stop(Exception):
    pass


def build(NPT, NS, TPP, dbg_names=(), stop=None, NCTX=0):
    nc = bass.Bass("TRN2", target_bir_lowering=False)
    din_ = {}
    dout_ = {}

    def din(name, shape):
        din_[name] = nc.dram_tensor(name, list(shape), F32, kind="ExternalInput").ap()
        return din_[name]

    def dout(name, shape):
        dout_[name] = nc.dram_tensor(name, list(shape), F32, kind="ExternalOutput").ap()
        return dout_[name]

    x_p = din("x_p", [NPT * 128, D]); x_s = din("x_s", [128, D]); flag_d = din("flag", [128, 1])
    mem_p = din("mem_p", [NMEM, D])
    ck = din("cache_k", [NS, NMEM, D]); cv = din("cache_v", [NS, NMEM, D])
    st_sc = din("st_ssd_conv", [NS * 3, CONV_DIM]); st_ssd = din("st_ssd", [NS, D, 128])
    st_gla = din("st_gla", [NS, 4, 256, 512]); st_fc = din("st_ffn_conv", [NS * 2, 2 * FFN])
    consts = din("consts", [128, 1024]); consts2 = din("consts2", [128, 2048]); params = din("params", [128, 672])
    w_in = din("w_in", [D, IN_DIM]); w_ssd_out = din("w_ssd_out", [D, D]); w_gla_out = din("w_gla_out", [D, D])
    w_mix = din("w_mix_out", [D, D]); w_cq = din("w_cq", [D, D]); w_ck = din("w_ck", [D, D])
    w_cv = din("w_cv", [D, D]); w_co = din("w_co", [D, D]); w_up = din("w_up", [D, 2 * FFN])
    w_down = din("w_down", [FFN, D]); w_gate = din("w_gla_gate", [16, 1024])
    vec = {}
    for n, sz_ in [("norm_final", D), ("b_gla_gate", 1024)]:
        vec[n] = din(n, [sz_])

    y_p = dout("y_p", [(NPT - NCTX) * 128, D]); y_s = dout("y_s", [128, D])
    o_psc = dout("p_ssd_conv", [3, CONV_DIM]); o_pssd = dout("p_ssd", [D, 128]); o_pgla = dout("p_gla", [4, 256, 512])
    o_pfc = dout("p_ffn_conv", [2, 2 * FFN]); o_pmk = dout("p_mem_k", [NMEM, D]); o_pmv = dout("p_mem_v", [NMEM, D])
    o_ssc = dout("s_ssd_conv", [NS, 3, CONV_DIM]); o_sssd = dout("s_ssd", [NS, D, 128])
    o_sgla = dout("s_gla", [NS, 4, 256, 512]); o_sfc = dout("s_ffn_conv", [NS, 2, 2 * FFN])

    P = Prog()
    es = ExitStack()
    with es:
        es.enter_context(nc.allow_non_contiguous_dma(reason="small parameter loads"))

        def sb(name, shape, dt=F32):
            return es.enter_context(nc.sbuf_tensor(name, list(shape), dt))
        psum = es.enter_context(nc.psum_tensor("psum", [128, 8, 512], F32))
        NT = TPP
        T = NT * 128
        assert T >= 256

        dsem_keys = []

        def dma(eng, out, in_, r=(), w=(), key=None, semkey=None):
            key = ('d', semkey if semkey is not None else (list(w) + list(r))[0])
            if key not in dsem_keys:
                dsem_keys.append(key)
            return P.op(eng, lambda e: e.dma_start(out=out, in_=in_), r=r, w=w, dma=key)

        def ACT(out, in_, func, r, w, **kw):
            P.op('act', lambda e: e.activation(out=out, in_=in_, func=func, **kw), r=r, w=w)

        def TT(out, in0, in1, op, r, w):
            P.op('dve', lambda e: e.tensor_tensor(out=out, in0=in0, in1=in1, op=op), r=r, w=w)

        def TS(out, in0, s1, s2, op0, op1, r, w):
            if s2 is None:
                P.op('dve', lambda e: e.tensor_scalar(out=out, in0=in0, scalar1=s1, scalar2=None, op0=op0), r=r, w=w)
            else:
                P.op('dve', lambda e: e.tensor_scalar(out=out, in0=in0, scalar1=s1, scalar2=s2, op0=op0, op1=op1), r=r, w=w)

        def STT(out, in0, scalar, in1, op0, op1, r, w):
            P.op('dve', lambda e: e.scalar_tensor_tensor(out=out, in0=in0, scalar=scalar, in1=in1, op0=op0, op1=op1), r=r, w=w)

        def CP(out, in_, r, w):
            P.op('dve', lambda e: e.tensor_copy(out=out, in_=in_), r=r, w=w)

        def MS(ap, val, w):
            P.op('dve', lambda e: e.memset(ap, val), w=w)

        def mm(out, lhsT, rhs, start, stop, r, w):
            P.op('pe', lambda e: e.matmul(out, lhsT=lhsT, rhs=rhs, start=start, stop=stop, skip_group_check=True), r=r, w=w)

        def tr(out, in_, ident, r, w):
            P.op('pe', lambda e: e.transpose(out, in_, ident), r=r, w=w)

        psn = [0]

        def PS():
            b = psn[0]
            psn[0] = (psn[0] + 1) % 4
            return b

        def pk(b):
            return [('ps', b)]

        CST = sb("cst", [128, 1024])
        identf = CST[:, 0:128]; U_P = CST[:, 128:256]; ONE = CST[:, 384:512]
        L_P = CST[:, 256:384]; MN_P = CST[:, 512:640]; M01_P = CST[:, 640:768]
        CS2 = sb("cs2", [128, 2048])
        U_S = CS2[:, 0:128]; L_S = CS2[:, 128:256]; BLK_S = CS2[:, 256:384]; MN_S = CS2[:, 384:512]
        M01_S = CS2[:, 512:640]; BLKC = CS2[:, 640:656]
        BLKS = CS2[:, 1024:2048].rearrange("p (s t) -> p s t", t=64)
        identb = sb("identb", [128, 128], BF16)
        m01b_p = sb("m01bp", [128, 128], BF16); m01b_s = sb("m01bs", [128, 128], BF16)
        PAR = sb("par", [128, 672])
        GF = PAR[:, 0:96].rearrange("p (g c) -> p g c", c=16)
        CW = PAR[:, 96:192].rearrange("p (c k) -> p c k", k=4); CB = PAR[:, 192:216]
        FW = PAR[:, 216:480].rearrange("p (c k) -> p c k", k=3); FB = PAR[:, 480:568]
        DTB = PAR[:, 568:600]; AB = sb("ab", [128, 32]); DB = PAR[:, 632:664]
        WG = sb("wg", [32, 1024], BF16)
        dma('sp', CST[:], consts[:, :], w=['cst'])
        dma('sp', CS2[:], consts2[:, :], w=['cs2'])
        dma('sp', PAR[:], params[:, :], w=['par'])
        FLAG = sb("flag_sb", [128, 1])
        dma('sp', FLAG[:], flag_d[:, :], w=['flag'])
        dma('pool', WG[0:16, :], w_gate[:, :], w=['wg'])
        dma('pool', WG[16:17, :], vec["b_gla_gate"].rearrange("(o n) -> o n", o=1), w=['wg'])
        ACT(AB[:], PAR[:, 600:632], AF.Exp, r=['par'], w=['ab'])
        TS(AB[:], AB[:], -1.0, None, ALU.mult, None, r=['ab'], w=['ab'])
        CP(identb[:], identf, r=['cst'], w=['identb'])
        CP(m01b_p[:], M01_P, r=['cst'], w=['m01b'])
        CP(m01b_s[:], M01_S, r=['cs2'], w=['m01b'])
        KC = {'P': dict(U=U_P, L=L_P, BLK=ONE, MN=MN_P, M01=m01b_p),
              'S': dict(U=U_S, L=L_S, BLK=BLK_S, MN=MN_S, M01=m01b_s)}
        CK = ['cst', 'cs2']

        NW = 2
        WS = [sb(f"ws{i}", [128, 16, 512], BF16) for i in range(NW)]
        A_fm = sb("A_fm", [128, 16, T], BF16)
        U_fm = sb("u_fm", [128, 16, T], BF16)
        MRG = sb("mrg", [128, 16, T], BF16)
        ST = sb("st", [128, D]); GS = sb("gs", [128, 8, 512])
        KFM = sb("kfm", [128, 16, NMEM], BF16); VTM = sb("vtm", [128, 2, D], BF16)
        HALO = sb("halo", [128, 24, 3]); FHALO = sb("fhalo", [128, 88, 2])
        SS = sb("ss", [128, 8]); TMP32 = sb("tmp32", [128, 8, 32])
        RH = NT * D * 4
        R1 = 8192 * 2 + 4096 + 2048
        R2 = max(NT * (4096 + 6144 + 128), NT * 16 * 1024 + NT * 256, NT * 11 * 1024 + NT * 2048 + 32 + 4096) + 64
        R3 = 21504
        ARENA = sb("arena", [128, (RH + R1 + R2 + R3) // 4])
        class Carver:
            def __init__(self, base, size):
                self.base = base; self.size = size; self.off = 0
            def reset(self):
                self.off = 0
            def get(self, shape, dt=F32, key=None):
                esz = 4 if dt == F32 else 2
                n = 1
                for x in shape[1:]:
                    n *= x
                nb = (n * esz + 31) // 32 * 32
                assert self.off + nb <= self.size, (self.off, nb, self.size, shape)
                o = (self.base + self.off) // 4
                if key is not None:
                    P.alias[key] = (self.base + self.off, self.base + self.off + nb)
                ap = ARENA[0:shape[0], o:o + nb // 4]
                self.off += nb
                if dt != F32:
                    ap = ap.bitcast(dt)
                ap = ap[:, 0:n]
                if len(shape) == 3:
                    ap = ap.rearrange("p (a b) -> p a b", b=shape[2])
                elif len(shape) == 4:
                    ap = ap.rearrange("p (a b c) -> p a b c", b=shape[2], c=shape[3])
                return ap
        cH = Carver(0, RH); c1 = Carver(RH, R1); c2 = Carver(RH + R1, R2); c3 = Carver(RH + R1 + R2, R3)
        H = cH.get([128, NT, D])
        for t_ in range(NT):
            P.alias[('H', t_)] = (t_ * D * 4, (t_ + 1) * D * 4)
        cH.reset()
        SQ = cH.get([128, 16, 128], key='sq'); SQB = cH.get([128, D], BF16, key='sqb'); XWM = cH.get([128, D], BF16, key='xwm')
        T1 = c1.get([128, D], key='t1'); T2 = c1.get([128, D], key='t2'); XNT = c1.get([128, D], BF16, key='xnt')
        CMS = c1.get([128, 16, 64], BF16, key='cms')
        SZ = c2.get([128, NT, D], BF16, key='sz'); XBC = c2.get([128, 24, T], BF16, key='xbc'); DT = c2.get([128, NT, 32], key='dt')
        c2.reset()
        SG = c2.get([128, 16, T], BF16, key='sg'); QF = c2.get([128, 8, T], BF16, key='qf'); KF = c2.get([128, 8, T], BF16, key='kf')
        VT = c2.get([128, NT, D], BF16, key='vt'); SR = c2.get([128, NT, D], BF16, key='sr')
        GLR = c2.get([32, T], BF16, key='glr')
        c2.reset()
        ACTF = c2.get([128, 44, T], BF16, key='actf'); RAW2 = c2.get([128, 2, 2 + T], key='raw2'); ACC2 = c2.get([128, 2, T], key='acc2'); FST = c2.get([32, 1024], key='fst')
        OST = c3.get([128, 512], key='ost'); RAW = c3.get([128, 3 + T], key='raw'); ACC = c3.get([128, T], key='acc')
        RAWS = c3.get([128, 16, 8], key='raws'); RAWS2 = c3.get([128, 16, 8], key='raws2')
        c3.reset()
        XS = c3.get([128, D], BF16, key='xs'); BT = c3.get([128, 512], BF16, key='bt'); UA = c3.get([128, 4, 128], key='ua'); DEC = c3.get([128, 2, 128], key='dec')
        CBT = c3.get([128, 4, 128], key='cbt'); MT = c3.get([128, 2, 4, 128], BF16, key='mt'); STB = c3.get([128, D], BF16, key='stb'); XW = c3.get([128, D], BF16, key='xw')
        ELS = c3.get([128, 16, 16], key='els')
        c3.reset()
        G = c3.get([128, 1024], key='g'); GT1 = T1[:, 0:1024]; GT2 = T2[:, 0:1024]; QE = c3.get([128, 8, 128], BF16, key='qe')
        KE = c3.get([128, 8, 128], BF16, key='ke'); EBT = c3.get([128, 2, 128], key='ebt'); ATT = c3.get([128, 4, 128], BF16, key='att'); KW = c3.get([128, 1024], BF16, key='kw')
        GSB = c3.get([128, 8, 512], BF16, key='gsb'); ELG = c3.get([128, 8, 16], key='elg')
        c3.reset()
        PSM = T1[:, 0:4 * NMEM].rearrange("p (h m) -> p h m", m=NMEM); PB = c3.get([128, 4, NMEM], BF16, key='pb'); PT = c3.get([128, 8, 128], BF16, key='ptt')
        KSB = c3.get([128, 2, D], BF16, key='ksb'); KTS = c3.get([128, 16, NMEM], BF16, key='kts')
        TMPG = c3.get([128, T], key='tmpg')

        wslot = [0]
        passes = []

        NBLK_MAX = 112
        wcache = nc.dram_tensor("wcache", [NBLK_MAX, 128, 16 * 512], BF16, kind="Internal").ap()
        cache_ids = {}

        def load_w(wd, r0, nkc, c0, ncols, cache=True):
            s = wslot[0] % NW
            wslot[0] += 1
            ck_ = (wd.name, r0, nkc, c0, ncols)
            dst = WS[s][:, 0:nkc, 0:ncols]
            if cache and ck_ in cache_ids:
                cid = cache_ids[ck_]
                dma('pool', dst, wcache[cid][:, 0:nkc * ncols].rearrange("p (k n) -> p k n", n=ncols), r=[('wc', cid)], w=[('W', s)])
                return s
            src = wd[r0:r0 + nkc * 128, c0:c0 + ncols].rearrange("(kc p) n -> p kc n", p=128)
            dma('pool', dst, src, w=[('W', s)])
            if cache and len(cache_ids) < NBLK_MAX and len(passes) > 1:
                cid = len(cache_ids)
                cache_ids[ck_] = cid
                dma('sp', wcache[cid][:, 0:nkc * ncols].rearrange("p (k n) -> p k n", n=ncols), dst, r=[('W', s)], w=[('wc', cid)], semkey=('W', s))
            return s

        def proj_tm(lhs, lkey, tlist, wd, c0, ncols, epi, nkc=16, r0=0, slot=None, cache=True):
            s = load_w(wd, r0, nkc, c0, ncols, cache) if slot is None else slot
            for t in tlist:
                b = PS()
                for kc in range(nkc):
                    mm(psum[:, b, 0:ncols], lhs[:, kc, t * 128:(t + 1) * 128], WS[s][:, kc, 0:ncols], kc == 0, kc == nkc - 1,
                       r=[lkey, ('W', s)], w=pk(b))
                epi(t, b)
            return s

        def proj_fm(rhs, rkey, Tn, wd, c0, nch, epi, nkc=16, slot=None, cache=True):
            s = load_w(wd, 0, nkc, c0, nch * 128, cache) if slot is None else slot
            for ci in range(nch):
                b = PS()
                for kc in range(nkc):
                    mm(psum[:, b, 0:Tn], WS[s][:, kc, ci * 128:(ci + 1) * 128], rhs[:, kc, 0:Tn], kc == 0, kc == nkc - 1,
                       r=[rkey, ('W', s)], w=pk(b))
                epi(ci, b)
            return s

        def rstd_from_ss(col, n=1):
            TS(SS[:, col:col + n], SS[:, col:col + n], EPS, None, ALU.add, None, r=['ss'], w=['ss'])
            ACT(SS[:, col:col + n], SS[:, col:col + n], AF.Ln, r=['ss'], w=['ss'])
            ACT(SS[:, col:col + n], SS[:, col:col + n], AF.Exp, r=['ss'], w=['ss'], scale=-0.5)

        def norm_rstd(src, skey, n, col):
            ACT(T2[:, 0:n], src, AF.Square, r=skey, w=['t2', 'ss'], scale=float(n) ** -0.5, accum_out=SS[:, col:col + 1])
            rstd_from_ss(col)

        def to_fm(src_bf, skey, dst, dkey, tcol, gain, nk=16):
            for g4 in range(0, nk, 4):
                b = PS()
                pv = psum[:, b, :].bitcast(BF16)
                for j in range(4):
                    tr(pv[:, j * 128:(j + 1) * 128], src_bf[:, (g4 + j) * 128:(g4 + j + 1) * 128], identb[:], r=[skey, 'identb'], w=pk(b))
                pin = pv[:, 0:512].rearrange("p (a b) -> p a b", b=128)
                if gain is None:
                    ACT(dst[:, g4:g4 + 4, tcol:tcol + 128], pin, AF.Copy, r=pk(b), w=[dkey])
                else:
                    TT(dst[:, g4:g4 + 4, tcol:tcol + 128], pin, gain(g4).unsqueeze(2).to_broadcast([128, 4, 128]), ALU.mult,
                       r=pk(b) + ['par'], w=[dkey])

        def rms_to_fm(src, skey, gi, dst, dkey, tcol):
            norm_rstd(src, [skey], D, 0)
            ACT(XNT[:], src, AF.Copy, r=[skey, 'ss'], w=['xnt'], scale=SS[:, 0:1])
            to_fm(XNT, 'xnt', dst, dkey, tcol, lambda g4: GF[:, gi, g4:g4 + 4])

        dbg_out = {}

        def dbg(name, ap, keys):
            if name not in dbg_names:
                return
            shp = list(ap.shape)
            o = nc.dram_tensor("dbg_" + name, shp, F32, kind="ExternalOutput").ap()
            dbg_out[name] = o
            dma('pool', o, ap, r=keys, key='dbg')

        pass_idx = [0]

        def chk(name):
            if stop == name or stop == f"{name}@{pass_idx[0]}":
                raise _Stop()

        try:
            MNF = U_fm
            for mt_ in range(2):
                dma('sp', H[:, 0, :], mem_p[mt_ * 128:(mt_ + 1) * 128, :], w=[('H', 0)], key='xin')
                rms_to_fm(H[:, 0, :], ('H', 0), 2, MNF, 'u_fm', mt_ * 128)
            for cb_ in range(4):
                def epi_k(ci, b, cb_=cb_):
                    ACT(KFM[:, cb_ * 4 + ci, :], psum[:, b, 0:NMEM], AF.Copy, r=pk(b), w=['kfm'])
                s = proj_fm(MNF, 'u_fm', NMEM, w_ck, cb_ * 512, 4, epi_k, cache=False)

                def epi_kt(t, b, cb_=cb_):
                    ACT(OST[:], psum[:, b, :], AF.Copy, r=pk(b), w=['ost'])
                    dma('sp', o_pmk[t * 128:(t + 1) * 128, cb_ * 512:(cb_ + 1) * 512], OST[:], r=['ost'], key='o1')
                proj_tm(MNF, 'u_fm', range(2), w_ck, cb_ * 512, 512, epi_kt, slot=s)
            for cb_ in range(4):
                def epi_vt(t, b, cb_=cb_):
                    ACT(OST[:], psum[:, b, :], AF.Copy, r=pk(b), w=['ost'])
                    CP(VTM[:, t, cb_ * 512:(cb_ + 1) * 512], OST[:], r=['ost'], w=['vtm'])
                    dma('sp', o_pmv[t * 128:(t + 1) * 128, cb_ * 512:(cb_ + 1) * 512], OST[:], r=['ost'], key='o1')
                proj_tm(MNF, 'u_fm', range(2), w_cv, cb_ * 512, 512, epi_vt, cache=False)

            MS(ST[:], 0.0, w=['st'])
            MS(GS[:], 0.0, w=['gs'])
            MS(HALO[:], 0.0, w=['halo']); MS(FHALO[:], 0.0, w=['fhalo'])
            chk('memkv')

            tiles_all = ([('C', i) for i in range(NCTX - 1)] + [('H', NCTX - 1)] if NCTX > 0 else []) + \
                [('P', i) for i in range(NCTX, NPT)] + ([('S', 0)] if NS > 0 else [])
            passes = [tiles_all[i:i + NT] for i in range(0, len(tiles_all), NT)]

            for pi_, tiles in enumerate(passes):
                pass_idx[0] = pi_
                nt = len(tiles)
                Tn = nt * 128
                has_s = tiles[-1][0] == 'S'
                np_t = nt - (1 if has_s else 0)
                Tp = np_t * 128
                flagged = [ti for ti, (k, i) in enumerate(tiles) if k == 'S' or i == NPT - 1]
                full = [ti for ti, (k, i) in enumerate(tiles) if k != 'C']
                has_h = any(k == 'H' for k, _ in tiles)
                for ti, (k, i) in enumerate(tiles):
                    src = x_p[i * 128:(i + 1) * 128, :] if k != 'S' else x_s[:, :]
                    dma('sp', H[:, ti, :], src, w=[('H', ti)], key='xin')
                    rms_to_fm(H[:, ti, :], ('H', ti), 0, A_fm, 'a_fm', ti * 128)
                dbg('a_fm', A_fm[:, :, 0:Tn], ['a_fm'])
                chk('a_fm')

                for cb_ in range(4):
                    def epi_z(t, b, cb_=cb_):
                        ACT(SZ[:, t, cb_ * 512:(cb_ + 1) * 512], psum[:, b, :], AF.Silu, r=pk(b), w=[('sz', t)])
                    if full:
                        proj_tm(A_fm, 'a_fm', full, w_in, C_Z + cb_ * 512, 512, epi_z)
                if has_s:
                    dma('sp', T1[0:NS * 3, :], st_sc[:, 0:D], w=['t1'], key='xin')
                    dma('sp', T2[0:NS * 3, 0:1024], st_sc[:, D:CONV_DIM], w=['t2'], key='xin')

                def conv_chunk(b, c, RAWb, rk, ACCb, ak, HAL, W, ntap, dst_fn):
                    hl = ntap - 1
                    if np_t > 0:
                        ACT(RAWb[:, hl:hl + Tp], psum[:, b, 0:Tp], AF.Copy, r=pk(b), w=[rk])
                        ACT(RAWb[:, 0:hl], HAL[:, c, :], AF.Copy, r=['halo'], w=[rk])
                        TS(ACCb[:, 0:Tp], RAWb[:, hl:hl + Tp], W[:, c, hl:hl + 1], None, ALU.mult, None, r=[rk, 'par', 'par'], w=[ak])
                        for k_ in range(hl):
                            STT(ACCb[:, 0:Tp], RAWb[:, k_:k_ + Tp], W[:, c, k_:k_ + 1], ACCb[:, 0:Tp], ALU.mult, ALU.add, r=[rk, 'par', 'par', ak], w=[ak])
                        ACT(HAL[:, c, :], RAWb[:, Tp:Tp + hl], AF.Copy, r=[rk], w=['halo'])

                def conv_chunk_s(b, c, RS, rsk, stT, stkeys, ccol, ACCb, ak, W, ntap):
                    hl = ntap - 1
                    b2 = PS()
                    tr(psum[:, b2, 0:NS * hl], stT[0:NS * hl, ccol:ccol + 128], identf[0:NS * hl, 0:NS * hl], r=stkeys + ['cst'], w=pk(b2))
                    ACT(RS[:, 0:NS, 0:hl], psum[:, b2, 0:NS * hl].rearrange("p (s r) -> p s r", r=hl), AF.Copy, r=pk(b2), w=[rsk])
                    ACT(RS[:, :, hl:hl + 4], psum[:, b, Tp:Tp + 64].rearrange("p (s r) -> p s r", r=4), AF.Copy, r=pk(b), w=[rsk])
                    accv = ACCb[:, Tp:Tp + 64].rearrange("p (s r) -> p s r", r=4)
                    TS(accv, RS[:, :, hl:hl + 4], W[:, c, hl:hl + 1], None, ALU.mult, None, r=[rsk, 'par', 'par'], w=[ak])
                    for k_ in range(hl):
                        STT(accv, RS[:, :, k_:k_ + 4], W[:, c, k_:k_ + 1], accv, ALU.mult, ALU.add, r=[rsk, 'par', 'par', ak], w=[ak])

                if has_s:
                    MS(RAWS[:], 0.0, w=['raws']); MS(RAWS2[:], 0.0, w=['raws2'])
                for cb_ in range(6):
                    def epi_x(ci, b, cb_=cb_):
                        c = cb_ * 4 + ci
                        conv_chunk(b, c, RAW, 'raw', ACC, 'acc', HALO, CW, 4, None)
                        if has_s:
                            stT = T1 if c < 16 else T2
                            conv_chunk_s(b, c, RAWS, 'raws', stT, ['t1', 't2'], (c % 16) * 128, ACC, 'acc', CW, 4)
                            MS(XBC[:, c, Tp + 64:Tp + 128], 0.0, w=['xbc'])
                        nact = Tp + (64 if has_s else 0)
                        ACT(XBC[:, c, 0:nact], ACC[:, 0:nact], AF.Silu, r=['acc', 'par'], w=['xbc'], bias=CB[:, c:c + 1])
                    s = proj_fm(A_fm, 'a_fm', Tn, w_in, C_XBC + cb_ * 512, 4, epi_x)
                    for ti in flagged:
                        b = PS()
                        for kc in range(16):
                            mm(psum[:, b, :], A_fm[:, kc, ti * 128:(ti + 1) * 128], WS[s][:, kc, :], kc == 0, kc == 15, r=['a_fm', ('W', s)], w=pk(b))
                        ACT(OST[:], psum[:, b, :], AF.Copy, r=pk(b), w=['ost'])
                        if tiles[ti][0] == 'P':
                            dma('sp', o_psc[:, cb_ * 512:(cb_ + 1) * 512], OST[125:128, :], r=['ost'], key='o1')
                        else:
                            for r_ in range(1, 4):
                                dma('sp', o_ssc[:, r_ - 1, cb_ * 512:(cb_ + 1) * 512], OST[r_:4 * NS:4, :], r=['ost'], key='o1')
                dbg('xbc', XBC[:, :, 0:Tn], ['xbc'])
                chk('xbc')

                def epi_dt(t, b):
                    x_ = TMP32[:, 0, :]; a_ = TMP32[:, 1, :]
                    TT(x_, psum[:, b, 0:32], DTB[:], ALU.add, r=pk(b) + ['par'], w=['tmpA'])
                    STT(a_, x_, -1.0, x_, ALU.mult, ALU.max, r=['tmpA'], w=['tmpB'])
                    ACT(a_, a_, AF.Exp, r=['tmpB'], w=['tmpB'], scale=-1.0)
                    ACT(a_, a_, AF.Ln, r=['tmpB'], w=['tmpB'], bias=1.0)
                    STT(DT[:, t, :], x_, 0.0, a_, ALU.max, ALU.add, r=['tmpA', 'tmpB'], w=['dt'])
                proj_tm(A_fm, 'a_fm', range(nt), w_in, C_DT, 32, epi_dt)
                dbg('dt', DT[:, 0:nt, :], ['dt'])
                chk('dt')

                ACT(STB[:], ST[:], AF.Copy, r=['st'], w=['stb'])
                for ti, (kind, i) in enumerate(tiles):
                    kc_ = KC['S' if kind == 'S' else 'P']
                    tc = ti * 128
                    for g4 in range(0, 16, 4):
                        b = PS(); pv = psum[:, b, :].bitcast(BF16)
                        for j in range(4):
                            tr(pv[:, j * 128:(j + 1) * 128], XBC[:, g4 + j, tc:tc + 128], identb[:], r=['xbc', 'identb'], w=pk(b))
                        ACT(XS[:, g4 * 128:(g4 + 4) * 128], pv[:, 0:512], AF.Copy, r=pk(b), w=['xs'])
                    b = PS(); pv = psum[:, b, :].bitcast(BF16)
                    for j in range(4):
                        tr(pv[:, j * 128:(j + 1) * 128], XBC[:, 16 + j, tc:tc + 128], identb[:], r=['xbc', 'identb'], w=pk(b))
                    ACT(BT[:], pv[:, 0:512], AF.Copy, r=pk(b), w=['bt'])
                    a_ = TMP32[:, 2, :]; acum = TMP32[:, 3, :]; nacum = TMP32[:, 4, :]; wgt = TMP32[:, 5, :]; elast = TMP32[:, 6, :]; eacum = TMP32[:, 7, :]
                    TT(a_, DT[:, ti, :], AB[:], ALU.mult, r=['dt', 'ab'], w=['tmp_a'])
                    b = PS()
                    mm(psum[:, b, 0:32], kc_['U'], a_, True, True, r=['tmp_a'] + CK, w=pk(b))
                    mm(psum[:, b, 32:64], kc_['BLK'], a_, True, True, r=['tmp_a'] + CK, w=pk(b))
                    CP(acum, psum[:, b, 0:32], r=pk(b), w=['tmp_ac'])
                    TS(nacum, psum[:, b, 0:32], -1.0, None, ALU.mult, None, r=pk(b), w=['tmp_nac'])
                    ACT(eacum, psum[:, b, 0:32], AF.Exp, r=pk(b), w=['tmp_eac'])
                    ACT(elast, psum[:, b, 32:64], AF.Exp, r=pk(b), w=['tmp_el'])
                    TT(wgt, psum[:, b, 32:64], acum, ALU.subtract, r=pk(b) + ['tmp_ac'], w=['tmp_w'])
                    ACT(wgt, wgt, AF.Exp, r=['tmp_w'], w=['tmp_w'])
                    TT(wgt, wgt, DT[:, ti, :], ALU.mult, r=['tmp_w', 'dt'], w=['tmp_w'])
                    if kind in ('C', 'H'):
                        TS(wgt, wgt, FLAG[:, 0:1], None, ALU.mult, None, r=['tmp_w', 'flag'], w=['tmp_w'])
                    TT(XW[:].rearrange("p (h d) -> p h d", d=64), XS[:].rearrange("p (h d) -> p h d", d=64),
                       wgt.unsqueeze(2).to_broadcast([128, 32, 64]), ALU.mult, r=['xs', 'tmp_w'], w=['xw'])

                    def ssd_state_update():
                        for g_ in range(4):
                            b = PS()
                            mm(psum[:, b, :], BT[:, g_ * 128:(g_ + 1) * 128], XW[:, g_ * 512:(g_ + 1) * 512], True, True, r=['bt', 'xw'], w=pk(b))
                            sv = ST[:, g_ * 512:(g_ + 1) * 512]
                            TT(sv.rearrange("p (h d) -> p h d", d=64), sv.rearrange("p (h d) -> p h d", d=64),
                               elast[:, g_ * 8:(g_ + 1) * 8].unsqueeze(2).to_broadcast([128, 8, 64]), ALU.mult, r=['st', 'tmp_el'], w=['st'])
                            TT(sv, sv, psum[:, b, :], ALU.add, r=['st'] + pk(b), w=['st'])
                        ACT(STB[:], ST[:], AF.Copy, r=['st'], w=['stb'])
                    if kind == 'C':
                        ssd_state_update()
                        continue
                    b = PS()
                    for g_ in range(4):
                        mm(psum[:, b, g_ * 128:(g_ + 1) * 128], XBC[:, 16 + g_, tc:tc + 128], XBC[:, 20 + g_, tc:tc + 128], True, True, r=['xbc'], w=pk(b))
                    ACT(CBT[:], psum[:, b, :].rearrange("p (a b) -> p a b", b=128), AF.Copy, r=pk(b), w=['cbt'])
                    if kind != 'S':
                        for g_ in range(4):
                            b = PS()
                            mm(psum[:, b, :], XBC[:, 20 + g_, tc:tc + 128], STB[:, g_ * 512:(g_ + 1) * 512], True, True, r=['xbc', 'stb'], w=pk(b))
                            TT(T1[:, g_ * 512:(g_ + 1) * 512].rearrange("p (h d) -> p h d", d=64), psum[:, b, :].rearrange("p (h d) -> p h d", d=64),
                               eacum[:, g_ * 8:(g_ + 1) * 8].unsqueeze(2).to_broadcast([128, 8, 64]), ALU.mult, r=pk(b) + ['tmp_eac'], w=['t1'])
                    else:
                        AEX = T2
                        CP(AEX[:].rearrange("p (h d) -> p h d", d=64), a_.unsqueeze(2).to_broadcast([128, 32, 64]), r=['tmp_a'], w=['t2'])
                        b = PS()
                        for t_ in range(16):
                            mm(psum[:, b, t_ * 16:(t_ + 1) * 16], AEX[:, t_ * 128:(t_ + 1) * 128], BLKC, True, True, r=['t2', 'cs2'], w=pk(b))
                        ACT(ELS[:], psum[:, b, 0:256].rearrange("p (t s) -> p t s", s=16), AF.Exp, r=pk(b), w=['els'])
                        for s_ in range(NS):
                            SQx, sqk = (SQ, 'sq') if s_ % 2 == 0 else (T2[:].rearrange("p (a b) -> p a b", b=128), 't2')
                            dma('sp', SQx[:], st_ssd[s_].rearrange("(t p) n -> p t n", p=128), w=[sqk])
                            for q in range(4):
                                b = q % 2
                                for j in range(4):
                                    tr(psum[:, b, j * 128:(j + 1) * 128], SQx[:, q * 4 + j, :], identf, r=[sqk, 'cst'], w=pk(b))
                                ACT(SQB[:, q * 512:(q + 1) * 512], psum[:, b, :], AF.Copy, r=pk(b), w=['sqb'])
                            for g_ in range(4):
                                TT(CMS[:, g_, :], XBC[:, 20 + g_, tc:tc + 64], BLKS[:, s_, :], ALU.mult, r=['xbc', 'cs2'], w=['cms'])
                            for g_ in range(4):
                                mm(psum[0:64, 4 + g_, :], CMS[:, g_, :], SQB[:, g_ * 512:(g_ + 1) * 512], s_ == 0, s_ == NS - 1, r=['cms', 'sqb'], w=pk(4 + g_))
                            TS(XWM[:], XW[:], BLKC[:, s_:s_ + 1], None, ALU.mult, None, r=['xw', 'cs2'], w=['xwm'])
                            for q in range(4):
                                b = 2 + q % 2
                                for j in range(4):
                                    t_ = q * 4 + j
                                    mm(psum[:, b, j * 128:(j + 1) * 128], XWM[:, t_ * 128:(t_ + 1) * 128], BT[:, (t_ // 4) * 128:(t_ // 4 + 1) * 128], True, True,
                                       r=['xwm', 'bt'], w=pk(b))
                                for j in range(4):
                                    t_ = q * 4 + j
                                    STT(SQx[:, t_, :], SQx[:, t_, :], ELS[:, t_, s_:s_ + 1], psum[:, b, j * 128:(j + 1) * 128], ALU.mult, ALU.add,
                                        r=[sqk, 'els'] + pk(b), w=[sqk])
                            dma('sp', o_sssd[s_].rearrange("(t p) n -> p t n", p=128), SQx[:], r=[sqk])
                        for g_ in range(4):
                            TT(T1[0:64, g_ * 512:(g_ + 1) * 512].rearrange("p (h d) -> p h d", d=64), psum[0:64, 4 + g_, :].rearrange("p (h d) -> p h d", d=64),
                               eacum[0:64, g_ * 8:(g_ + 1) * 8].unsqueeze(2).to_broadcast([64, 8, 64]), ALU.mult, r=pk(4 + g_) + ['tmp_eac'], w=['t1'])
                        MS(T1[64:128, :], 0.0, w=['t1'])
                    for h4 in range(8):
                        TT(UA[:], kc_['U'].unsqueeze(1).to_broadcast([128, 4, 128]), a_[:, h4 * 4:(h4 + 1) * 4].unsqueeze(2).to_broadcast([128, 4, 128]), ALU.mult,
                           r=['tmp_a'] + CK, w=['ua'])
                        b = PS()
                        mm(psum[:, b, :], ONE, UA[:].rearrange("p a b -> p (a b)"), True, False, r=['ua'] + CK, w=pk(b))
                        for hh in range(4):
                            mm(psum[:, b, hh * 128:(hh + 1) * 128], identf, kc_['MN'], False, hh == 3, r=CK, w=pk(b))
                        mb = h4 % 2
                        for hh in range(4):
                            h = h4 * 4 + hh
                            ACT(DEC[:, hh % 2, :], psum[:, b, hh * 128:(hh + 1) * 128], AF.Exp, r=pk(b) + ['tmp_nac'], w=[('dec', hh % 2)],
                                bias=nacum[:, h:h + 1])
                            STT(MT[:, mb, hh, :], DEC[:, hh % 2, :], DT[:, ti, h:h + 1], CBT[:, h // 8, :], ALU.mult, ALU.mult,
                                r=[('dec', hh % 2), 'dt', 'cbt'], w=[('mt', mb)])
                        for hh in range(4):
                            h = h4 * 4 + hh
                            bb = 4 + h // 8
                            mm(psum[:, bb, (h % 8) * 64:(h % 8 + 1) * 64], MT[:, mb, hh, :], XS[:, h * 64:(h + 1) * 64], True, True, r=[('mt', mb), 'xs'], w=pk(bb))
                    for g_ in range(4):
                        TT(T1[:, g_ * 512:(g_ + 1) * 512], T1[:, g_ * 512:(g_ + 1) * 512], psum[:, 4 + g_, :], ALU.add, r=['t1'] + pk(4 + g_), w=['t1'])
                    TT(T2[:].rearrange("p (h d) -> p h d", d=64), XS[:].rearrange("p (h d) -> p h d", d=64),
                       DB[:].unsqueeze(2).to_broadcast([128, 32, 64]), ALU.mult, r=['xs', 'par'], w=['t2'])
                    TT(T1[:], T1[:], T2[:], ALU.add, r=['t1', 't2'], w=['t1'])
                    TT(T1[:], T1[:], SZ[:, ti, :], ALU.mult, r=['t1', ('sz', ti)], w=['t1'])
                    if ti == 0:
                        dbg('yssd', T1[:], ['t1'])
                    for g_ in range(4):
                        ACT(T2[:, g_ * 512:(g_ + 1) * 512], T1[:, g_ * 512:(g_ + 1) * 512], AF.Square, r=['t1'], w=['t2', 'ss'], scale=512.0 ** -0.5,
                            accum_out=SS[:, 1 + g_:2 + g_])
                    rstd_from_ss(1, 4)
                    for g_ in range(4):
                        ACT(XNT[:, g_ * 512:(g_ + 1) * 512], T1[:, g_ * 512:(g_ + 1) * 512], AF.Copy, r=['t1', 'ss'], w=['xnt'], scale=SS[:, 1 + g_:2 + g_])
                    to_fm(XNT, 'xnt', U_fm, 'u_fm', tc, lambda g4: GF[:, 4, g4:g4 + 4])
                    if kind != 'S':
                        ssd_state_update()
                dbg('u_fm', U_fm[:, :, 0:Tn], ['u_fm'])
                chk('ssd')

                for cb_ in (range(4) if full else ()):
                    def epi_ga(ci, b, cb_=cb_):
                        ACT(SG[:, cb_ * 4 + ci, 0:Tn], psum[:, b, 0:Tn], AF.Sigmoid, r=pk(b), w=['sg'])
                    proj_fm(A_fm, 'a_fm', Tn, w_in, C_GA + cb_ * 512, 4, epi_ga)
                for cb_ in (range(4) if full else ()):
                    def epi_a(ci, b, cb_=cb_):
                        TT(MRG[:, cb_ * 4 + ci, 0:Tn], psum[:, b, 0:Tn], SG[:, cb_ * 4 + ci, 0:Tn], ALU.mult, r=pk(b) + ['sg'], w=['mrg'])
                    proj_fm(U_fm, 'u_fm', Tn, w_ssd_out, cb_ * 512, 4, epi_a)

                if not SKIPGLA:
                    for cb_ in (range(2) if full else ()):
                        def epi_q(ci, b, cb_=cb_):
                            ACT(QF[:, cb_ * 4 + ci, 0:Tn], psum[:, b, 0:Tn], AF.Copy, r=pk(b), w=['qf'], scale=1.0 / 16.0)
                        proj_fm(A_fm, 'a_fm', Tn, w_in, C_Q + cb_ * 512, 4, epi_q)
                    for cb_ in range(2):
                        def epi_kf(ci, b, cb_=cb_):
                            ACT(KF[:, cb_ * 4 + ci, 0:Tn], psum[:, b, 0:Tn], AF.Copy, r=pk(b), w=['kf'])
                        proj_fm(A_fm, 'a_fm', Tn, w_in, C_K + cb_ * 512, 4, epi_kf)
                    for cb_ in range(4):
                        def epi_v(t, b, cb_=cb_):
                            ACT(VT[:, t, cb_ * 512:(cb_ + 1) * 512], psum[:, b, :], AF.Copy, r=pk(b), w=['vt'])
                        proj_tm(A_fm, 'a_fm', range(nt), w_in, C_V + cb_ * 512, 512, epi_v)
                    for cb_ in range(4):
                        def epi_r(t, b, cb_=cb_):
                            ACT(SR[:, t, cb_ * 512:(cb_ + 1) * 512], psum[:, b, :], AF.Silu, r=pk(b), w=['sr'])
                        if full:
                            proj_tm(A_fm, 'a_fm', full, w_in, C_R + cb_ * 512, 512, epi_r)
                    s = load_w(w_in, 0, 16, C_G, 16)
                    b = PS()
                    for kc in range(16):
                        mm(psum[0:16, b, 0:Tn], WS[s][:, kc, 0:16], A_fm[:, kc, 0:Tn], kc == 0, kc == 15, r=['a_fm', ('W', s)], w=pk(b))
                    MS(GLR[:, 0:Tn], 1.0, w=['glr'])
                    ACT(GLR[0:16, 0:Tn], psum[0:16, b, 0:Tn], AF.Copy, r=pk(b), w=['glr'])

                    ACT(GSB[:].rearrange("p a b -> p (a b)"), GS[:].rearrange("p a b -> p (a b)"), AF.Copy, r=['gs'], w=['gsb'])
                    for ti, (kind, i) in enumerate(tiles):
                        kc_ = KC['S' if kind == 'S' else 'P']
                        tc = ti * 128
                        for hb in range(2):
                            b = PS()
                            mm(psum[:, b, :], GLR[0:17, tc:tc + 128], WG[0:17, hb * 512:(hb + 1) * 512], True, True, r=['glr', 'wg'], w=pk(b))
                            gs_ = slice(hb * 512, (hb + 1) * 512)
                            CP(GT2[:, gs_], psum[:, b, :], r=pk(b), w=['t2'])
                            STT(GT1[:, gs_], GT2[:, gs_], -1.0, GT2[:, gs_], ALU.mult, ALU.max, r=['t2'], w=['t1'])
                            ACT(GT1[:, gs_], GT1[:, gs_], AF.Exp, r=['t1'], w=['t1'], scale=-1.0)
                            ACT(GT1[:, gs_], GT1[:, gs_], AF.Ln, r=['t1'], w=['t1'], bias=1.0)
                            TS(GT2[:, gs_], GT2[:, gs_], 0.0, None, ALU.min, None, r=['t2'], w=['t2'])
                            TT(G[:, gs_], GT2[:, gs_], GT1[:, gs_], ALU.subtract, r=['t1', 't2'], w=['g'])
                        TS(G[:], G[:], 1.0 / 16.0, None, ALU.mult, None, r=['g'], w=['g'])
                        if ti == 0:
                            dbg('glog', G[:], ['g'])
                        for c in (range(8) if kind != 'C' else ()):
                            b = PS()
                            mm(psum[:, b, 0:128], G[:, c * 128:(c + 1) * 128], kc_['U'], True, True, r=['g'] + CK, w=pk(b))
                            ACT(EBT[:, 0, :], psum[:, b, 0:128], AF.Exp, r=pk(b), w=[('ebt', 0)])
                            ACT(EBT[:, 1, :], psum[:, b, 0:128], AF.Exp, r=pk(b), w=[('ebt', 1)], scale=-1.0)
                            TT(QE[:, c, :], QF[:, c, tc:tc + 128], EBT[:, 0, :], ALU.mult, r=['qf', ('ebt', 0)], w=['qe'])
                            TT(KE[:, c, :], KF[:, c, tc:tc + 128], EBT[:, 1, :], ALU.mult, r=['kf', ('ebt', 1)], w=['ke'])
                        if kind != 'C':
                            b = PS()
                            for h in range(4):
                                for kc in range(2):
                                    mm(psum[:, b, h * 128:(h + 1) * 128], KE[:, h * 2 + kc, :], QE[:, h * 2 + kc, :], kc == 0, kc == 1, r=['ke', 'qe'], w=pk(b))
                            TT(ATT[:], psum[:, b, :].rearrange("p (h i) -> p h i", i=128), kc_['M01'][:].unsqueeze(1).to_broadcast([128, 4, 128]), ALU.mult,
                               r=pk(b) + ['m01b'], w=['att'])
                        for hb in range(2):
                            b = PS()
                            mm(psum[:, b, :], kc_['L'], G[:, hb * 512:(hb + 1) * 512], True, True, r=['g'] + CK, w=pk(b))
                            ACT(GT1[:, hb * 512:(hb + 1) * 512], psum[:, b, :], AF.Exp, r=pk(b), w=['t1'])
                        b = PS(); pv = psum[:, b, :].bitcast(BF16)
                        for c in range(8):
                            tr(pv[:, c * 128:(c + 1) * 128], KF[:, c, tc:tc + 128], identb[:], r=['kf', 'identb'], w=pk(b))
                        ACT(KW[:], pv[:, 0:1024], AF.Copy, r=pk(b), w=['kw'])
                        TT(KW[:], KW[:], GT1[:], ALU.mult, r=['kw', 't1'], w=['kw'])
                        b = PS()
                        ncol = 1 if kind != 'S' else 16
                        for c in range(8):
                            mm(psum[:, b, c * 16:c * 16 + ncol], G[:, c * 128:(c + 1) * 128], (ONE[:, 0:1] if kind != 'S' else BLKC), True, True,
                               r=['g'] + CK, w=pk(b))
                        ACT(ELG[:, :, 0:ncol], psum[:, b, 0:128].rearrange("p (c s) -> p c s", s=16)[:, :, 0:ncol], AF.Exp, r=pk(b), w=['elg'])
                        if kind != 'S':
                            for h in (range(4) if kind != 'C' else ()):
                                mm(psum[:, 4 + h, :], ATT[:, h, :], VT[:, ti, h * 512:(h + 1) * 512], True, False, r=['att', 'vt'], w=pk(4 + h))
                                for kc in range(2):
                                    mm(psum[:, 4 + h, :], QE[:, h * 2 + kc, :], GSB[:, h * 2 + kc, :], False, kc == 1, r=['qe', 'gsb'], w=pk(4 + h))
                            for c in range(8):
                                b = PS()
                                mm(psum[:, b, :], KW[:, c * 128:(c + 1) * 128], VT[:, ti, (c // 2) * 512:(c // 2 + 1) * 512], True, True, r=['kw', 'vt'], w=pk(b))
                                STT(GS[:, c, :], GS[:, c, :], ELG[:, c, 0:1], psum[:, b, :], ALU.mult, ALU.add, r=['gs', 'elg'] + pk(b), w=['gs'])
                        else:
                            for h in range(4):
                                mm(psum[:, 4 + h, :], ATT[:, h, :], VT[:, ti, h * 512:(h + 1) * 512], True, False, r=['att', 'vt'], w=pk(4 + h))
                            for s_ in range(NS):
                                for hh in range(2):
                                    SQx, sqk = (SQ, 'sq') if hh == 0 else (T2[:].rearrange("p (a b) -> p a b", b=128), 't2')
                                    sgv = SQx[:].rearrange("p a b -> p (a b)")
                                    dma('sp', sgv.rearrange("p (c v) -> p c v", v=512),
                                        st_gla[s_, hh * 2:hh * 2 + 2].rearrange("h (kc p) v -> p (h kc) v", p=128), w=[sqk], key=sqk)
                                    ACT(SQB[:], sgv, AF.Copy, r=[sqk], w=['sqb'])
                                    TT(CMS[:, 0:4, :], QE[:, hh * 4:hh * 4 + 4, 0:64], BLKS[:, s_:s_ + 1, :].to_broadcast([128, 4, 64]), ALU.mult,
                                       r=['qe', 'cs2'], w=['cms'])
                                    for c4 in range(4):
                                        h = hh * 2 + c4 // 2
                                        last = (s_ == NS - 1) and (c4 % 2 == 1)
                                        mm(psum[0:64, 4 + h, :], CMS[:, c4, :], SQB[:, c4 * 512:(c4 + 1) * 512], False, last, r=['cms', 'sqb'], w=pk(4 + h))
                                    TS(XWM[:, 0:512], KW[:, hh * 512:(hh + 1) * 512], BLKC[:, s_:s_ + 1], None, ALU.mult, None, r=['kw', 'cs2'], w=['xwm'])
                                    for c4 in range(4):
                                        c = hh * 4 + c4
                                        b = PS()
                                        mm(psum[:, b, :], XWM[:, c4 * 128:(c4 + 1) * 128], VT[:, ti, (c // 2) * 512:(c // 2 + 1) * 512], True, True, r=['xwm', 'vt'], w=pk(b))
                                        STT(SQx[:, c4 * 4:(c4 + 1) * 4, :].rearrange("p a b -> p (a b)"), SQx[:, c4 * 4:(c4 + 1) * 4, :].rearrange("p a b -> p (a b)"),
                                            ELG[:, c, s_:s_ + 1], psum[:, b, :], ALU.mult, ALU.add, r=[sqk, 'elg'] + pk(b), w=[sqk])
                                    dma('sp', o_sgla[s_, hh * 2:hh * 2 + 2].rearrange("h (kc p) v -> p (h kc) v", p=128),
                                        sgv.rearrange("p (c v) -> p c v", v=512), r=[sqk], key=sqk)
                        if kind == 'C':
                            ACT(GSB[:].rearrange("p a b -> p (a b)"), GS[:].rearrange("p a b -> p (a b)"), AF.Copy, r=['gs'], w=['gsb'])
                            continue
                        for h in range(4):
                            ACT(T2[:, h * 512:(h + 1) * 512], psum[:, 4 + h, :], AF.Square, r=pk(4 + h), w=['t2', 'ss'], scale=512.0 ** -0.5, accum_out=SS[:, 1 + h:2 + h])
                        rstd_from_ss(1, 4)
                        for h in range(4):
                            ACT(T1[:, h * 512:(h + 1) * 512], psum[:, 4 + h, :], AF.Copy, r=pk(4 + h) + ['ss'], w=['t1'], scale=SS[:, 1 + h:2 + h])
                        if ti == 0:
                            dbg('ogla', T1[:], ['t1'])
                        TT(XNT[:], T1[:], SR[:, ti, :], ALU.mult, r=['t1', 'sr'], w=['xnt'])
                        to_fm(XNT, 'xnt', U_fm, 'u_fm', tc, lambda g4: GF[:, 5, 0:4])
                        if kind != 'S':
                            ACT(GSB[:].rearrange("p a b -> p (a b)"), GS[:].rearrange("p a b -> p (a b)"), AF.Copy, r=['gs'], w=['gsb'])
                    dbg('o_fm', U_fm[:, :, 0:Tn], ['u_fm'])
                    chk('gla')

                if not full:
                    continue
                for cb_ in range(int(os.environ.get("GBN", 4))):
                    def epi_gb(ci, b, cb_=cb_):
                        ACT(SG[:, cb_ * 4 + ci, 0:Tn], psum[:, b, 0:Tn], AF.Sigmoid, r=pk(b), w=['sg'])
                    proj_fm(A_fm, 'a_fm', Tn, w_in, C_GB + cb_ * 512, 4, epi_gb)
                chk('gb')
                for cb_ in range(4):
                    def epi_b(ci, b, cb_=cb_):
                        c = cb_ * 4 + ci
                        TT(TMPG[:, 0:Tn], psum[:, b, 0:Tn], SG[:, c, 0:Tn], ALU.mult, r=pk(b) + ['sg'], w=['tmpg'])
                        TT(MRG[:, c, 0:Tn], MRG[:, c, 0:Tn], TMPG[:, 0:Tn], ALU.add, r=['mrg', 'tmpg'], w=['mrg'])
                    proj_fm(U_fm, 'u_fm', Tn, w_gla_out, cb_ * 512, 4, epi_b)
                dbg('mrg', MRG[:, :, 0:Tn], ['mrg'])
                chk('mrg')
                for ti, (k, i) in enumerate(tiles):
                    src = x_p[i * 128:(i + 1) * 128, :] if k != 'S' else x_s[:, :]
                    dma('sp', H[:, ti, :], src, w=[('H', ti)])
                for cb_ in range(4):
                    def epi_m(t, b, cb_=cb_):
                        hv = H[:, t, cb_ * 512:(cb_ + 1) * 512]
                        TT(hv, hv, psum[:, b, :], ALU.add, r=[('H', t)] + pk(b), w=[('H', t)])
                    proj_tm(MRG, 'mrg', full, w_mix, cb_ * 512, 512, epi_m)
                dbg('h1', H[:, 0:nt, :], [('H', t) for t in range(nt)])
                chk('h1')

                for ti in range(nt):
                    rms_to_fm(H[:, ti, :], ('H', ti), 1, A_fm, 'a_fm', ti * 128)
                QC = SG
                for cb_ in range(4):
                    def epi_cq(ci, b, cb_=cb_):
                        ACT(QC[:, cb_ * 4 + ci, 0:Tn], psum[:, b, 0:Tn], AF.Copy, r=pk(b), w=['sg'], scale=512.0 ** -0.5)
                    proj_fm(A_fm, 'a_fm', Tn, w_cq, cb_ * 512, 4, epi_cq)
                OC = U_fm

                def softmax_rows(np_):
                    for h in range(4):
                        P.op('dve', lambda e, h=h: e.reduce_max(out=SS[0:np_, 1 + h:2 + h], in_=psum[0:np_, 4 + h, 0:NMEM], axis=mybir.AxisListType.X),
                             r=pk(4 + h), w=['ss'])
                    TS(SS[0:np_, 1:5], SS[0:np_, 1:5], -1.0, None, ALU.mult, None, r=['ss'], w=['ss'])
                    for h in range(4):
                        ACT(PSM[0:np_, h, :], psum[0:np_, 4 + h, 0:NMEM], AF.Exp, r=pk(4 + h) + ['ss'], w=['t1', 'tmpA'], bias=SS[0:np_, 1 + h:2 + h],
                            accum_out=TMP32[0:np_, 0, h:h + 1])
                    P.op('dve', lambda e: e.reciprocal(out=TMP32[0:np_, 1, 0:4], in_=TMP32[0:np_, 0, 0:4]), r=['tmpA'], w=['tmpB'])
                    TT(PB[0:np_], PSM[0:np_], TMP32[0:np_, 1, 0:4].unsqueeze(2).to_broadcast([np_, 4, NMEM]), ALU.mult, r=['t1', 'tmpB'], w=['pb'])

                def probs_T(np_):
                    b = PS(); pv = psum[:, b, :].bitcast(BF16)
                    for h in range(4):
                        for m_ in range(2):
                            j = h * 2 + m_
                            tr(pv[:, j * 128:j * 128 + np_], PB[0:np_, h, m_ * 128:(m_ + 1) * 128], identb[0:np_, 0:np_], r=['pb', 'identb'], w=pk(b))
                    ACT(PT[:, :, 0:np_], pv[:, 0:1024].rearrange("p (j t) -> p j t", t=128)[:, :, 0:np_], AF.Copy, r=pk(b), w=['ptt'])

                for ti, (kind, i) in enumerate(tiles):
                    tc = ti * 128
                    if kind == 'C':
                        continue
                    if kind != 'S':
                        for h in range(4):
                            for dc in range(4):
                                mm(psum[:, 4 + h, 0:NMEM], QC[:, h * 4 + dc, tc:tc + 128], KFM[:, h * 4 + dc, :], dc == 0, dc == 3, r=['sg', 'kfm'], w=pk(4 + h))
                        softmax_rows(128)
                        probs_T(128)
                        for q in range(4):
                            b = PS()
                            for j in range(4):
                                dc = q * 4 + j
                                for m_ in range(2):
                                    mm(psum[:, b, j * 128:(j + 1) * 128], VTM[:, m_, dc * 128:(dc + 1) * 128], PT[:, (dc // 4) * 2 + m_, :], m_ == 0, m_ == 1,
                                       r=['vtm', 'ptt'], w=pk(b))
                            ACT(OC[:, q * 4:(q + 1) * 4, tc:tc + 128], psum[:, b, :].rearrange("p (a t) -> p a t", t=128), AF.Copy, r=pk(b), w=['u_fm'])
                    else:
                        for s_ in range(NS):
                            KSBx, ksk = (KSB, 'ksb') if s_ % 2 == 0 else (T2[:].bitcast(BF16).rearrange("p (m d) -> p m d", d=D), 't2')
                            dma('pool', KSBx[:], ck[s_].rearrange("(m p) d -> p m d", p=128), w=[ksk])
                            for m_ in range(2):
                                for q in range(2):
                                    b = PS(); pv = psum[:, b, :].bitcast(BF16)
                                    for j in range(8):
                                        dc = q * 8 + j
                                        tr(pv[:, j * 128:(j + 1) * 128], KSBx[:, m_, dc * 128:(dc + 1) * 128], identb[:], r=[ksk, 'identb'], w=pk(b))
                                    ACT(KTS[:, q * 8:(q + 1) * 8, m_ * 128:(m_ + 1) * 128], pv[:, 0:1024].rearrange("p (j t) -> p j t", t=128), AF.Copy,
                                        r=pk(b), w=['kts'])
                            TT(CMS[:], QC[:, :, tc:tc + 64], BLKS[:, s_:s_ + 1, :].to_broadcast([128, 16, 64]), ALU.mult, r=['sg', 'cs2'], w=['cms'])
                            for h in range(4):
                                for dc in range(4):
                                    mm(psum[0:64, 4 + h, 0:NMEM], CMS[:, h * 4 + dc, :], KTS[:, h * 4 + dc, :], s_ == 0 and dc == 0, s_ == NS - 1 and dc == 3,
                                       r=['cms', 'kts'], w=pk(4 + h))
                        softmax_rows(64)
                        probs_T(64)
                        for s_ in range(NS):
                            KSBx, ksk = (KSB, 'ksb') if s_ % 2 == 0 else (T2[:].bitcast(BF16).rearrange("p (m d) -> p m d", d=D), 't2')
                            dma('pool', KSBx[:], cv[s_].rearrange("(m p) d -> p m d", p=128), w=[ksk])
                            for dc in range(16):
                                bb = 4 + dc // 8
                                for m_ in range(2):
                                    first = (s_ == 0 and dc % 8 == 0 and m_ == 0)
                                    mm(psum[:, bb, (dc % 8) * 64 + s_ * 4:(dc % 8) * 64 + s_ * 4 + 4], KSBx[:, m_, dc * 128:(dc + 1) * 128],
                                       PT[:, (dc // 4) * 2 + m_, s_ * 4:s_ * 4 + 4], first, False, r=[ksk, 'ptt'], w=pk(bb))
                        for q in range(2):
                            ACT(OC[:, q * 8:(q + 1) * 8, tc:tc + 4 * NS], psum[:, 4 + q, :].rearrange("p (a t) -> p a t", t=64)[:, :, 0:4 * NS], AF.Copy,
                                r=pk(4 + q), w=['u_fm'])
                        if 4 * NS < 128:
                            MS(OC[:, :, tc + 4 * NS:tc + 128], 0.0, w=['u_fm'])
                dbg('oc', OC[:, :, 0:Tn], ['u_fm'])
                chk('oc')
                for cb_ in range(4):
                    def epi_co(t, b, cb_=cb_):
                        hv = H[:, t, cb_ * 512:(cb_ + 1) * 512]
                        TT(hv, hv, psum[:, b, :], ALU.add, r=[('H', t)] + pk(b), w=[('H', t)])
                    proj_tm(OC, 'u_fm', full, w_co, cb_ * 512, 512, epi_co)
                dbg('h2', H[:, 0:nt, :], [('H', t) for t in range(nt)])
                chk('h2')

                for ti in range(nt):
                    rms_to_fm(H[:, ti, :], ('H', ti), 3, A_fm, 'a_fm', ti * 128)
                for cb_ in range(11):
                    sa = load_w(w_up, 0, 16, cb_ * 512, 512)
                    sg_ = load_w(w_up, 0, 16, FFN + cb_ * 512, 512)
                    if has_s:
                        dma('sp', FST[0:NS * 2, 0:512], st_fc[:, cb_ * 512:(cb_ + 1) * 512], w=['fst'], key='xin')
                        dma('sp', FST[0:NS * 2, 512:1024], st_fc[:, FFN + cb_ * 512:FFN + (cb_ + 1) * 512], w=['fst'], key='xin')
                    for ci in range(4):
                        c = cb_ * 4 + ci
                        ba = PS(); bg_ = PS()
                        for kc in range(16):
                            mm(psum[:, ba, 0:Tn], WS[sa][:, kc, ci * 128:(ci + 1) * 128], A_fm[:, kc, 0:Tn], kc == 0, kc == 15, r=['a_fm', ('W', sa)], w=pk(ba))
                        for kc in range(16):
                            mm(psum[:, bg_, 0:Tn], WS[sg_][:, kc, ci * 128:(ci + 1) * 128], A_fm[:, kc, 0:Tn], kc == 0, kc == 15, r=['a_fm', ('W', sg_)], w=pk(bg_))
                        for half, (b, cc) in enumerate([(ba, c), (bg_, 44 + c)]):
                            conv_chunk(b, cc, RAW2[:, half, :], ('raw2', half), ACC2[:, half, :], ('acc2', half), FHALO, FW, 3, None)
                            if has_s:
                                conv_chunk_s(b, cc, (RAWS if half == 0 else RAWS2), ('raws' if half == 0 else 'raws2'), FST, ['fst'], half * 512 + ci * 128,
                                             ACC2[:, half, :], ('acc2', half), FW, 3)
                        nact = Tp + (64 if has_s else 0)
                        ACT(ACC2[:, 1, 0:nact], ACC2[:, 1, 0:nact], AF.Silu, r=[('acc2', 1), 'par'], w=[('acc2', 1)], bias=FB[:, 44 + c:45 + c])
                        STT(ACTF[:, c, 0:nact], ACC2[:, 0, 0:nact], FB[:, c:c + 1], ACC2[:, 1, 0:nact], ALU.add, ALU.mult,
                            r=[('acc2', 0), ('acc2', 1), 'par'], w=['actf'])
                        if has_s:
                            MS(ACTF[:, c, Tp + 64:Tp + 128], 0.0, w=['actf'])
                    for ti in flagged:
                        for half, s in enumerate([sa, sg_]):
                            b = PS()
                            for kc in range(16):
                                mm(psum[:, b, :], A_fm[:, kc, ti * 128:(ti + 1) * 128], WS[s][:, kc, :], kc == 0, kc == 15, r=['a_fm', ('W', s)], w=pk(b))
                            ACT(OST[:], psum[:, b, :], AF.Copy, r=pk(b), w=['ost'])
                            col0 = half * FFN + cb_ * 512
                            if tiles[ti][0] == 'P':
                                dma('sp', o_pfc[:, col0:col0 + 512], OST[126:128, :], r=['ost'], key='o1')
                            else:
                                for r_ in range(2, 4):
                                    dma('sp', o_sfc[:, r_ - 2, col0:col0 + 512], OST[r_:4 * NS:4, :], r=['ost'], key='o1')
                dbg('actf', ACTF[:, :, 0:Tn], ['actf'])
                chk('actf')
                assert nt <= 4
                for cb_ in range(4):
                    for kg, (k0, nk) in enumerate([(0, 16), (16, 16), (32, 12)]):
                        s = load_w(w_down, k0 * 128, nk, cb_ * 512, 512)
                        for t in full:
                            for kc in range(nk):
                                mm(psum[:, 4 + t, :], ACTF[:, k0 + kc, t * 128:(t + 1) * 128], WS[s][:, kc, :], kg == 0 and kc == 0, kg == 2 and kc == nk - 1,
                                   r=['actf', ('W', s)], w=pk(4 + t))
                    for t in full:
                        hv = H[:, t, cb_ * 512:(cb_ + 1) * 512]
                        TT(hv, hv, psum[:, 4 + t, :], ALU.add, r=[('H', t)] + pk(4 + t), w=[('H', t)])
                if has_h:
                    TS(FHALO[:].rearrange("p a b -> p (a b)"), FHALO[:].rearrange("p a b -> p (a b)"), FLAG[:, 0:1], None, ALU.mult, None,
                       r=['halo', 'flag'], w=['halo'])
                for ti, (kind, i) in enumerate(tiles):
                    if kind in ('C', 'H'):
                        continue
                    norm_rstd(H[:, ti, :], [('H', ti)], D, 0)
                    dma('sp', T2[:], vec["norm_final"].partition_broadcast(128), w=['t2'])
                    STT(T1[:], H[:, ti, :], SS[:, 0:1], T2[:], ALU.mult, ALU.mult, r=[('H', ti), 'ss', 't2'], w=['t1'])
                    dst = y_p[(i - NCTX) * 128:(i - NCTX + 1) * 128, :] if kind == 'P' else y_s[:, :]
                    dma('sp', dst, T1[:], r=['t1'], key='o2')

            for q in range(4):
                b = PS()
                for j in range(4):
                    tr(psum[:, b, j * 128:(j + 1) * 128], ST[:, (q * 4 + j) * 128:(q * 4 + j + 1) * 128], identf, r=['st', 'cst'], w=pk(b))
                ACT(SQ[:, q * 4:(q + 1) * 4, :], psum[:, b, :].rearrange("p (a b) -> p a b", b=128), AF.Copy, r=pk(b), w=['sq'])
            dma('sp', o_pssd.rearrange("(t p) n -> p t n", p=128), SQ[:], r=['sq'], key='sq')
            dma('sp', o_pgla.rearrange("h (kc p) v -> p (h kc) v", p=128), GS[:], r=['gs'], key='o2')


        except _Stop:
            pass

        sems = {}
        for e_ in ENGS:
            sems[e_] = es.enter_context(nc.semaphore("sem_" + e_))
        dma_sems = {}
        for i_, k in enumerate(dsem_keys):
            dma_sems[k] = es.enter_context(nc.semaphore(f"dsem{i_}"))
        P.finalize(sems, dma_sems)
        if dbg_names or stop:
            print('STATS', P.stats)
        block = es.enter_context(nc.Block())

        @block.tensor
        def _(e):
            P.run('pe', e)

        @block.vector
        def _(e):
            P.run('dve', e)

        @block.scalar
        def _(e):
            P.run('act', e)

        @block.gpsimd
        def _(e):
            P.run('pool', e)

        @block.sync
        def _(e):
            P.run('sp', e)
    return nc, din_, dout_, dbg_out


def make_consts():
    t = np.arange(128)
    c = np.zeros((128, 1024), np.float32)
    c[:, 0:128] = np.eye(128)
    c[:, 128:256] = (t[:, None] <= t[None, :])
    c[:, 256:384] = (t[:, None] > t[None, :])
    c[:, 384:512] = 1.0
    valid = (t[None, :] >= t[:, None])
    c[:, 512:640] = np.where(valid, 0.0, -30000.0)
    c[:, 640:768] = valid
    c2 = np.zeros((128, 2048), np.float32)
    same = (t[:, None] // 4) == (t[None, :] // 4)
    c2[:, 0:128] = (t[:, None] <= t[None, :]) & same
    c2[:, 128:256] = (t[:, None] > t[None, :]) & same
    c2[:, 256:384] = same
    c2[:, 384:512] = np.where(valid & same, 0.0, -30000.0)
    c2[:, 512:640] = valid & same
    c2[:, 640:656] = (t[:, None] // 4) == np.arange(16)[None, :]
    blks = ((np.arange(64)[None, :] // 4) == np.arange(16)[:, None]).astype(np.float32)
    c2[:, 1024:2048] = blks.reshape(1, 1024)
    return c, c2


WEIGHT_NAMES = ["w_in", "w_ssd_out", "w_gla_out", "w_mix_out", "w_cq", "w_ck", "w_cv", "w_co", "w_up", "w_down", "w_gla_gate",
                "norm_final", "b_gla_gate"]


def make_params(inp):
    f = lambda a: np.asarray(a, dtype=np.float32)
    p = np.zeros((128, 672), np.float32)
    for gi, n in enumerate(["norm_mix", "norm_cross", "norm_mem", "norm_ffn", "ssd_norm"]):
        p[:, gi * 16:(gi + 1) * 16] = f(inp[n]).reshape(16, 128).T
    p[:, 80:84] = f(inp["gla_norm"]).reshape(4, 128).T
    p[:, 96:192] = f(inp["ssd_conv_w"]).reshape(4, 24, 128).transpose(2, 1, 0).reshape(128, 96)
    p[:, 192:216] = f(inp["ssd_conv_b"]).reshape(24, 128).T
    p[:, 216:480] = f(inp["ffn_conv_w"]).reshape(3, 88, 128).transpose(2, 1, 0).reshape(128, 264)
    p[:, 480:568] = f(inp["ffn_conv_b"]).reshape(88, 128).T
    p[:, 568:600] = f(inp["ssd_dt_bias"])[None, :]
    p[:, 600:632] = f(inp["ssd_A_log"])[None, :]
    p[:, 632:664] = f(inp["ssd_D"])[None, :]
    return p


TPP_FULL = 2


def kernel(**inp):
    f = lambda a: np.ascontiguousarray(np.asarray(a, dtype=np.float32))
    NB, L = inp["x_prompt"].shape[:2]
    NPT = L // 128
    NCTX = NPT // 2
    NSB = inp["x_sample"].shape[0]
    ncores = 8
    NS = NSB // ncores
    nc, _, _, _ = build(NPT, NS, TPP_FULL, NCTX=NCTX)
    c1, c2 = make_consts()
    shared = {n: f(inp[n]) for n in WEIGHT_NAMES}
    shared["consts"] = c1; shared["consts2"] = c2; shared["params"] = make_params(inp)
    in_maps = []
    Lh = L // 2
    for c in range(ncores):
        b, half = c // 2, c % 2
        m = dict(shared)
        xp = np.zeros((L, D), np.float32)
        if half == 1:
            xp[:] = f(inp["x_prompt"][b])
        else:
            xp[Lh:] = f(inp["x_prompt"][b, :Lh])
        m["x_p"] = xp
        m["flag"] = np.full((128, 1), float(half), np.float32)
        xs = np.zeros((128, D), np.float32)
        xs[:NS * 4] = f(inp["x_sample"][c * NS:(c + 1) * NS]).reshape(NS * 4, D)
        m["x_s"] = xs
        m["mem_p"] = f(inp["mem_prompt"][b])
        sl = slice(c * NS, (c + 1) * NS)
        m["cache_k"] = f(inp["cache_mem_k"][sl]).reshape(NS, NMEM, D)
        m["cache_v"] = f(inp["cache_mem_v"][sl]).reshape(NS, NMEM, D)
        m["st_ssd_conv"] = f(inp["state_ssd_conv"][sl]).reshape(NS * 3, CONV_DIM)
        m["st_ssd"] = f(inp["state_ssd"][sl]).reshape(NS, D, 128)
        m["st_gla"] = f(inp["state_gla"][sl])
        m["st_ffn_conv"] = f(inp["state_ffn_conv"][sl]).reshape(NS * 2, 2 * FFN)
        in_maps.append(m)
    res = run_bass_kernel_spmd(nc, in_maps, core_ids=list(range(ncores)))
    R = res.results
    pc = [R[2 * b + 1] for b in range(NB)]
    y_prompt = np.stack([np.concatenate([R[2 * b]["y_p"], R[2 * b + 1]["y_p"]], axis=0) for b in range(NB)]).reshape(NB, L, D)
    y_sample = np.concatenate([r["y_s"][:NS * 4].reshape(NS, 4, D) for r in R])
    p_ssd_conv = np.stack([r["p_ssd_conv"] for r in pc])
    p_ssd = np.stack([r["p_ssd"].reshape(32, 64, 128) for r in pc])
    p_gla = np.stack([r["p_gla"] for r in pc])
    p_ffn_conv = np.stack([r["p_ffn_conv"] for r in pc])
    p_mem_k = np.stack([r["p_mem_k"].reshape(NMEM, 4, 512) for r in pc])
    p_mem_v = np.stack([r["p_mem_v"].reshape(NMEM, 4, 512) for r in pc])
    s_ssd_conv = np.concatenate([r["s_ssd_conv"] for r in R])
    s_ssd = np.concatenate([r["s_ssd"].reshape(NS, 32, 64, 128) for r in R])
    s_gla = np.concatenate([r["s_gla"] for r in R])
    s_ffn_conv = np.concatenate([r["s_ffn_conv"] for r in R])
    outs = (y_prompt, y_sample, p_ssd_conv, p_ssd, p_gla, p_ffn_conv, p_mem_k, p_mem_v, s_ssd_conv, s_ssd, s_gla, s_ffn_conv)
    return tuple(np.ascontiguousarray(o, dtype=np.float32) for o in outs)
```
